# Optimizing a Trainium2 kernel written in Bass

```python
import jax, jax.numpy as jnp
from jax import lax
import numpy as np

D_MODEL = 1024
BATCH = 32
SEQ = 256
DEPTH = 2
DEC_BATCH = 2
DEC_SEQ = 4096
PAST_LEN = 512

GRID_W = 64
HEAD_DIM = 64
D_A = D_MODEL
N_HEADS_A = D_A // HEAD_DIM
R_W = 64
R_A = 64
R_G = 128
D_B = D_MODEL
D_FF = ((8 * D_MODEL // 3 + 127) // 128) * 128
N_DIR = 2
IN_SIZES = (D_A, D_A, D_A, R_W, R_W, R_A, R_A, R_G, D_B, D_B, D_B, 2 * D_MODEL)
IN_TOTAL = 3 * D_A + 2 * R_W + 2 * R_A + R_G + 3 * D_B + 2 * D_MODEL
NORM_EPS = 1e-6
GN_EPS = 64e-5
L2_EPS = 1e-12

kernel_name = 'bidir_rwkv7_shortconv_convffn_diffusion_step'


def rmsnorm(x, g):
    xf = x.astype(jnp.float32)
    y = xf * lax.rsqrt(jnp.mean(xf * xf, axis=-1, keepdims=True) + NORM_EPS)
    return (y * g.astype(jnp.float32)).astype(x.dtype)


def dwconv3(x, w, b, axis):
    n = x.shape[axis]
    pad = [(0, 0)] * x.ndim
    pad[axis] = (1, 1)
    xp = jnp.pad(x, pad)
    prev = lax.slice_in_dim(xp, 0, n, axis=axis)
    nxt = lax.slice_in_dim(xp, 2, n + 2, axis=axis)
    return prev * w[0] + x * w[1] + nxt * w[2] + b


def seq_conv(x, w, b, grid, along):
    if not grid:
        return dwconv3(x, w, b, 1)
    bsz, t, ch = x.shape
    rows = t // GRID_W
    xg = x.reshape(bsz, rows, GRID_W, ch)
    axis = 2 if along == 'row' else 1
    return dwconv3(xg, w, b, axis).reshape(bsz, t, ch)


def wkv_scan(s0, r, w, k, v, kk, a, reverse):
    xs = tuple(jnp.moveaxis(t, 1, 0) for t in (r, w, k, v, -kk, kk * a))

    def step(s, inp):
        r_t, w_t, k_t, v_t, a_t, b_t = inp
        sa = jnp.einsum('bhij,bhj->bhi', s, a_t)
        s = s * w_t[:, :, None, :] + sa[..., None] * b_t[:, :, None, :] + v_t[..., None] * k_t[:, :, None, :]
        return s, jnp.einsum('bhij,bhj->bhi', s, r_t)

    s_fin, ys = lax.scan(step, s0, xs, reverse=reverse)
    return jnp.moveaxis(ys, 0, 1), s_fin


def rwkv_mixer(r, k, v, wdf, wdb, adf, adb, gd, s0_f, s0_b, lp):
    bsz, t, _ = r.shape
    f32 = jnp.float32
    dt = r.dtype
    heads = lambda z: z.reshape(bsz, t, N_HEADS_A, HEAD_DIM)
    r, k, v = r.astype(f32), k.astype(f32), v.astype(f32)
    rh, vh = heads(r), heads(v)
    kk = heads(k * lp['k_k'].astype(f32))
    kk = kk / jnp.maximum(jnp.linalg.norm(kk, axis=-1, keepdims=True), L2_EPS)
    r_k = lp['r_k'].astype(f32)
    k_a = lp['k_a'].astype(f32)

    def direction(d, wdown, adown, s0, reverse):
        w_log = -jax.nn.softplus(-(lp['decay_w0'][d].astype(f32)
                                   + jnp.tanh(wdown.astype(f32)) @ lp['decay_w2'][d].astype(f32))) - 0.5
        decay = jnp.exp(-jnp.exp(w_log))
        a = jax.nn.sigmoid(lp['iclr_a0'][d].astype(f32) + adown.astype(f32) @ lp['iclr_a2'][d].astype(f32))
        k_d = heads(k * (1.0 + (a - 1.0) * k_a))
        y, s_fin = wkv_scan(s0.astype(f32), rh, heads(decay), k_d, vh, kk, heads(a), reverse)
        bonus = jnp.sum(rh * k_d * r_k, axis=-1, keepdims=True) * vh
        return y, bonus, s_fin

    y_f, bonus_f, s_f = direction(0, wdf, adf, s0_f, False)
    y_b, bonus_b, s_b = direction(1, wdb, adb, s0_b, True)
    y = y_f + y_b
    mu = jnp.mean(y, axis=-1, keepdims=True)
    var = jnp.mean(jnp.square(y - mu), axis=-1, keepdims=True)
    yn = ((y - mu) * lax.rsqrt(var + GN_EPS)).reshape(bsz, t, D_A)
    o = yn * lp['gn_w'].astype(f32) + lp['gn_b'].astype(f32) + (bonus_f + bonus_b).reshape(bsz, t, D_A)
    g = jax.nn.sigmoid(gd.astype(f32)) @ lp['gate_g2'].astype(f32)
    return (o * g).astype(dt), s_f, s_b


def trunk_layer(x, cond, s0_f, s0_b, grid, lp):
    dt = x.dtype
    mod = (jax.nn.silu(cond) @ lp['w_mod'] + lp['b_mod']).astype(dt)[:, None, :]
    sh1, sc1, gt1, sh2, sc2, gt2 = jnp.split(mod, 6, axis=-1)
    h = rmsnorm(x, lp['norm1_g']) * (1 + sc1) + sh1
    z = h @ lp['w_in']
    splits = [int(s) for s in np.cumsum(IN_SIZES)[:-1]]
    r, k, v, wdf, wdb, adf, adb, gd, cb, cc, cx, gates = jnp.split(z, splits, axis=-1)
    o_a, s_f, s_b = rwkv_mixer(r, k, v, wdf, wdb, adf, adb, gd, s0_f, s0_b, lp)
    o_b = cb * seq_conv(cc * cx, lp['conv_mix_w'], lp['conv_mix_b'], grid, 'row')
    g_a, g_b = jnp.split(jax.nn.sigmoid(gates), 2, axis=-1)
    merged = g_a * (o_a @ lp['w_pa']) + g_b * (o_b @ lp['w_pb'])
    x = x + gt1 * (merged @ lp['w_o'])
    h2 = rmsnorm(x, lp['norm2_g']) * (1 + sc2) + sh2
    u = seq_conv(h2 @ lp['w_up'], lp['conv_ffn_w'], lp['conv_ffn_b'], grid, 'col')
    u_act, u_lin = jnp.split(u, 2, axis=-1)
    x = x + gt2 * ((jax.nn.silu(u_act) * u_lin) @ lp['w_down'])
    return x, s_f, s_b


def setup_inputs(seed: int = 0) -> dict:
    key = jax.random.key(seed)
    ks = iter(jax.random.split(key, 32))
    nrm = lambda shape, s: jax.random.normal(next(ks), shape, jnp.float32) * s
    L, D = DEPTH, D_MODEL
    return {
        'x_prompt': nrm((BATCH, SEQ, D), 1.0),
        'x_sample': nrm((DEC_BATCH, DEC_SEQ, D), 1.0),
        'state_wkv': nrm((DEC_BATCH, DEPTH, N_DIR, N_HEADS_A, HEAD_DIM, HEAD_DIM), 0.3),
        'c': nrm((DEC_BATCH, D), 1.0),
        'c_ctx': nrm((D,), 1.0),
        'w_mod': nrm((L, D, 6 * D), 0.5 * D ** -0.5),
        'b_mod': nrm((L, 6 * D), 0.01),
        'norm1_g': 1.0 + nrm((L, D), 0.01),
        'w_in': nrm((L, D, IN_TOTAL), D ** -0.5),
        'decay_w0': nrm((L, N_DIR, D_A), 0.5) - 1.0,
        'decay_w2': nrm((L, N_DIR, R_W, D_A), 0.5 * R_W ** -0.5),
        'iclr_a0': nrm((L, N_DIR, D_A), 0.1),
        'iclr_a2': nrm((L, N_DIR, R_A, D_A), 0.5 * R_A ** -0.5),
        'gate_g2': nrm((L, R_G, D_A), R_G ** -0.5),
        'k_k': 0.85 + nrm((L, D_A), 0.1),
        'k_a': 1.0 + nrm((L, D_A), 0.1),
        'r_k': nrm((L, N_HEADS_A, HEAD_DIM), 0.1),
        'gn_w': 1.0 + nrm((L, D_A), 0.01),
        'gn_b': nrm((L, D_A), 0.01),
        'conv_mix_w': nrm((L, 3, D_B), 0.5),
        'conv_mix_b': nrm((L, D_B), 0.01),
        'w_pa': nrm((L, D_A, D), D_A ** -0.5),
        'w_pb': nrm((L, D_B, D), D_B ** -0.5),
        'w_o': nrm((L, D, D), D ** -0.5),
        'norm2_g': 1.0 + nrm((L, D), 0.01),
        'w_up': nrm((L, D, 2 * D_FF), D ** -0.5),
        'conv_ffn_w': nrm((L, 3, 2 * D_FF), 0.5),
        'conv_ffn_b': nrm((L, 2 * D_FF), 0.01),
        'w_down': nrm((L, D_FF, D), D_FF ** -0.5),
        'norm_f_g': 1.0 + nrm((D,), 0.01),
    }


def reference(x_prompt, x_sample, state_wkv, c, c_ctx, w_mod, b_mod, norm1_g, w_in, decay_w0, decay_w2,
              iclr_a0, iclr_a2, gate_g2, k_k, k_a, r_k, gn_w, gn_b, conv_mix_w, conv_mix_b, w_pa, w_pb, w_o,
              norm2_g, w_up, conv_ffn_w, conv_ffn_b, w_down, norm_f_g):
    lps = [dict(w_mod=w_mod[l], b_mod=b_mod[l], norm1_g=norm1_g[l], w_in=w_in[l], decay_w0=decay_w0[l],
                decay_w2=decay_w2[l], iclr_a0=iclr_a0[l], iclr_a2=iclr_a2[l], gate_g2=gate_g2[l], k_k=k_k[l],
                k_a=k_a[l], r_k=r_k[l], gn_w=gn_w[l], gn_b=gn_b[l], conv_mix_w=conv_mix_w[l],
                conv_mix_b=conv_mix_b[l], w_pa=w_pa[l], w_pb=w_pb[l], w_o=w_o[l], norm2_g=norm2_g[l],
                w_up=w_up[l], conv_ffn_w=conv_ffn_w[l], conv_ffn_b=conv_ffn_b[l], w_down=w_down[l])
           for l in range(DEPTH)]

    xp = x_prompt
    zeros = jnp.zeros((x_prompt.shape[0], N_HEADS_A, HEAD_DIM, HEAD_DIM), jnp.float32)
    ctx_states = []
    for l in range(DEPTH):
        xp, s_f, s_b = trunk_layer(xp, c_ctx[None, :], zeros, zeros, False, lps[l])
        ctx_states.append(jnp.stack([s_f, s_b], axis=1))
    new_state_wkv = jnp.stack(ctx_states, axis=1).astype(x_prompt.dtype)
    y_prompt = rmsnorm(xp, norm_f_g)

    xs = x_sample
    for l in range(DEPTH):
        xs, _, _ = trunk_layer(xs, c, state_wkv[:, l, 0], state_wkv[:, l, 1], True, lps[l])
    y_sample = rmsnorm(xs, norm_f_g)
    return (y_prompt, y_sample, new_state_wkv)
```

```python
import os
import numpy as np
from contextlib import ExitStack
import concourse.bass as bass
import concourse.mybir as mybir
from concourse.bass_utils import run_bass_kernel_spmd

F32 = mybir.dt.float32
BF16 = mybir.dt.bfloat16
AF = mybir.ActivationFunctionType
ALU = mybir.AluOpType

D = 1024
L = 2
NH = 16
HD = 64
DFF = 2816
NIN = 8576
NPT = 1024
NST = 4096
NTA = NPT + NST
TT = 512
C = 128
EPOCH = 8000
NSLOT = 20
SAME_SYNC = True
WD = BF16

PV = {}
_o = 0
for _n, _c in [("n1g", 8), ("n2g", 8), ("w0", 16), ("a0", 16), ("kk", 8), ("ka", 8), ("rk", 8), ("gnw", 8),
               ("gnb", 8), ("cmw", 24), ("cmb", 8), ("cfw", 132), ("cfb", 44), ("bmod", 48)]:
    PV[_n] = _o
    _o += _c
NPV = _o


class Buf:
    def __init__(self, t):
        self.t = t
        self.w = None
        self.r = {}

    def __getitem__(self, i):
        return self.t[i]


class PReg(Buf):
    def __init__(self, t, c0, n, bank=None):
        self.t = t
        self.bank = bank if bank is not None else Buf(t)
        self.c0 = c0
        self.n = n

    w = property(lambda self: self.bank.w, lambda self, v: setattr(self.bank, "w", v))
    r = property(lambda self: self.bank.r, lambda self, v: setattr(self.bank, "r", v))

    def v(self, n=None, p0=0, p1=128, o=0):
        n = self.n - o if n is None else n
        return self.t[p0:p1, self.c0 + o:self.c0 + o + n]


class Ring:
    def __init__(self, bufs):
        self.bufs = bufs
        self.i = 0

    def new(self):
        b = self.bufs[self.i % len(self.bufs)]
        self.i += 1
        return b


class KB:
    def __init__(self, nc, es):
        self.nc = nc
        self.es = es
        self.E = {"pe": nc.tensor, "act": nc.scalar, "dve": nc.vector, "pool": nc.gpsimd, "sp": nc.sync}
        self.comp = ["pe", "act", "dve", "pool"]
        self.cnt = {e: 0 for e in self.comp}
        self.esem = {e: [] for e in self.comp}
        self.waited = {e: {} for e in self.E}
        self.dsem = {}
        self.duse = {}
        self.dnext = {"sp": 0, "pool": 0}
        for q in ("sp", "pool"):
            for i in range(NSLOT):
                self.dsem[(q, i)] = es.enter_context(nc.semaphore("d%s%d" % (q, i)))
                self.duse[(q, i)] = 0
        self.barsem = es.enter_context(nc.semaphore("bar"))
        self.bar = 0
        self.bufs = []
        self.nins = 0

    def sb(self, name, shape, dt, es=None):
        self.uid = getattr(self, "uid", 0) + 1
        t = (es or self.es).enter_context(self.nc.sbuf_tensor("s%d_%s" % (self.uid, name), list(shape), dt))
        b = Buf(t)
        self.bufs.append(b)
        return b

    def ring(self, name, shape, dt, n, es=None):
        return Ring([self.sb("%s%d" % (name, i), shape, dt, es) for i in range(n)])

    def reg(self, b):
        self.bufs.append(b)
        return b

    def _sem(self, e, ep):
        while len(self.esem[e]) <= ep:
            self.esem[e].append(self.es.enter_context(self.nc.semaphore("s%s%d" % (e, len(self.esem[e])))))
        return self.esem[e][ep]

    def _wait(self, e, tok):
        kind, key, val = tok
        if kind == "eng":
            if key == e and (e == "pe" or not SAME_SYNC):
                return
            if self.waited[e].get(key, 0) >= val:
                return
            self.waited[e][key] = val
            ep = (val - 1) // EPOCH
            self.E[e].wait_ge(self._sem(key, ep), (val - 1) % EPOCH + 1)
        else:
            if self.waited[e].get(key, 0) >= val:
                return
            self.waited[e][key] = val
            self.E[e].wait_ge(self.dsem[key], val)

    def _deps(self, e, reads, writes):
        for b in reads:
            if b.w is not None:
                self._wait(e, b.w)
        for b in writes:
            if b.w is not None:
                self._wait(e, b.w)
            for tk in b.r.values():
                self._wait(e, tk)

    def _mark(self, tok, reads, writes):
        k = tok[1]
        for b in reads:
            b.r[k] = tok
        for b in writes:
            b.w = tok
            b.r = {}

    def I(self, e, reads, writes, fn):
        self._deps(e, reads, writes)
        n = self.cnt[e]
        sem = self._sem(e, n // EPOCH)
        fn(self.E[e]).then_inc(sem, 1)
        self.cnt[e] = n + 1
        self._mark(("eng", e, n + 1), reads, writes)
        self.nins += 1

    def dma(self, q, out, in_, reads, writes):
        self._deps(q, reads, writes)
        key = (q, self.dnext[q] % NSLOT)
        self.dnext[q] += 1
        u = self.duse[key]
        if u > 0:
            self._wait(q, ("dma", key, 16 * u))
        self.E[q].dma_start(out=out, in_=in_).then_inc(self.dsem[key], 16)
        self.duse[key] = u + 1
        self._mark(("dma", key, 16 * (u + 1)), reads, writes)
        self.nins += 1

    def barrier(self):
        for e in self.comp:
            if self.cnt[e] > 0:
                self._wait("sp", ("eng", e, self.cnt[e]))
        for key, u in self.duse.items():
            if u > 0:
                self._wait("sp", ("dma", key, 16 * u))
        self.bar += 1
        self.nc.sync.sem_inc(self.barsem, 1)
        for e in self.comp:
            self.E[e].wait_ge(self.barsem, self.bar)
        for e in self.E:
            for f in self.comp:
                self.waited[e][f] = self.cnt[f]
            for key, u in self.duse.items():
                self.waited[e][key] = 16 * u
        for b in self.bufs:
            b.w = None
            b.r = {}

    def mm(self, pr, out, lhsT, rhs, reads, start=True, stop=True):
        self.I("pe", reads, [pr], lambda e: e.matmul(out, lhsT=lhsT, rhs=rhs, start=start, stop=stop))

    def act(self, out, in_, func, reads, writes, bias=None, scale=None):
        kw = {}
        if bias is not None:
            kw["bias"] = bias
        if scale is not None:
            kw["scale"] = scale
        self.I("act", reads, writes, lambda e: e.activation(out=out, in_=in_, func=func, **kw))

    def tt(self, eng, out, in0, in1, op, reads, writes):
        self.I(eng, reads, writes, lambda e: e.tensor_tensor(out=out, in0=in0, in1=in1, op=op))

    def ts(self, eng, out, in0, s1, op0, reads, writes, s2=None, op1=None):
        if op1 is None:
            self.I(eng, reads, writes, lambda e: e.tensor_scalar(out=out, in0=in0, scalar1=s1, scalar2=None, op0=op0))
        else:
            self.I(eng, reads, writes,
                   lambda e: e.tensor_scalar(out=out, in0=in0, scalar1=s1, scalar2=s2, op0=op0, op1=op1))

    def stt(self, out, in0, scalar, in1, op0, op1, reads, writes):
        self.I("dve", reads, writes,
               lambda e: e.scalar_tensor_tensor(out=out, in0=in0, scalar=scalar, in1=in1, op0=op0, op1=op1))

    def cp(self, eng, out, in_, reads, writes):
        if eng == "act":
            self.act(out, in_, AF.Copy, reads, writes)
        else:
            self.I(eng, reads, writes, lambda e: e.tensor_copy(out=out, in_=in_))


GROUPS = [
    dict(t0=0, nt=NPT, seqs=[(256 * s, 256) for s in range(4)], grid=False, cond=0, rl=256, H=0, sh=1),
    dict(t0=NPT, nt=NST, seqs=[(NPT, NST)], grid=True, cond=1, rl=64, H=64, sh=64),
]


def build_nc(debug=False, nlayers=L, groups=(0, 1), phases=None):
    nc = bass.Bass("TRN2", target_bir_lowering=False)
    dbg = {}

    def din(name, shape, dt=F32):
        return nc.dram_tensor(name, list(shape), dt, kind="ExternalInput").ap()

    def dscr(name, shape, dt=F32):
        kind = "ExternalOutput" if (debug and name in DEBUG_OUT) else "Internal"
        return nc.dram_tensor(name, list(shape), dt, kind=kind).ap()

    xin = din("xin", [NTA, D])
    st_in = din("st_in", [L, 2, NH, HD, HD])
    cond_in = din("cond", [128, 8, 2])
    pv_in = din("pv", [L, 128, NPV])
    nfg_in = din("nfg", [128, 8])
    tinyw = phases is not None and "w" not in phases
    if tinyw:
        w_mod = w_in = w_pa = w_pb = w_o = w_up = w_dn = None
    else:
        w_mod = din("w_mod", [L, D, 6 * D])
        w_in = din("w_in", [L, D, NIN])
    dw2 = din("decay_w2", [L, 128, D])
    ia2 = din("iclr_a2", [L, 128, D])
    gg2 = din("gate_g2", [L, 128, D])
    if not tinyw:
        w_pa = din("w_pa", [L, D, D])
        w_pb = din("w_pb", [L, D, D])
        w_o = din("w_o", [L, D, D])
        w_up = din("w_up", [L, D, 2 * DFF])
        w_dn = din("w_down", [L, DFF, D])

    yout = nc.dram_tensor("yout", [NTA, D], F32, kind="ExternalOutput").ap()
    nsout = nc.dram_tensor("nsout", [4, L, 2, NH, HD, HD], F32, kind="ExternalOutput").ap()

    WSPEC = [("win", w_in, D, NIN), ("wpa", w_pa, D, D), ("wpb", w_pb, D, D), ("wo", w_o, D, D),
             ("wup", w_up, D, 2 * DFF), ("wdn", w_dn, DFF, D)]
    WB = {}
    for nm, _, Kd, Nd in WSPEC:
        WB[nm] = dscr("wb_" + nm, [L, Nd // 128, 128, Kd // 128, 128], BF16)
    XTA = dscr("XTA", [D, NTA])
    XTB = dscr("XTB", [D, NTA])
    ZR = dscr("ZR", [D, NTA])
    ZKK = dscr("ZKK", [D, NTA])
    ZKD = dscr("ZKD", [2, D, NTA])
    ZBA = dscr("ZBA", [2, D, NTA])
    ZLW = dscr("ZLW", [2, D, NTA])
    ZV = dscr("ZV", [D, NTA], BF16)
    ZBON = dscr("ZBON", [D, NTA], BF16)
    ZG = dscr("ZG", [D, NTA], BF16)
    ZOB = dscr("ZOB", [D, NTA], BF16)
    ZGAB = dscr("ZGAB", [2 * D, NTA], BF16)
    YTD = dscr("YTD", [2, D, NTA])

    with ExitStack() as es:
        K = KB(nc, es)
        ident = K.sb("ident", [128, 128], F32)
        identb = K.sb("identb", [128, 128], BF16)
        onesb = K.sb("onesb", [128, 128], BF16)
        oblk = K.sb("oblk", [128, 128], BF16)
        oblk64 = K.sb("oblk64", [128, 128], BF16)
        maskf = K.sb("maskf", [128, 256], BF16)
        maskb = K.sb("maskb", [128, 256], BF16)
        maskf2 = K.sb("maskf2", [128, 512], BF16)
        maskb2 = K.sb("maskb2", [128, 512], BF16)
        mlow4 = K.sb("mlow4", [128, 512], BF16)
        mupp4 = K.sb("mupp4", [128, 512], BF16)
        identw4 = K.sb("identw4", [128, 512], WD)
        rstf = K.sb("rstf", [128, 1024], F32)
        rstb = K.sb("rstb", [128, 1024], F32)
        epsn = K.sb("epsn", [128, 1], F32)
        epsg = K.sb("epsg", [128, 1], F32)
        pvt = K.sb("pvt", [128, L, NPV], F32)
        omka = K.sb("omka", [128, L, 8], F32)
        nfg = K.sb("nfg", [128, 8], F32)
        condt = K.sb("condt", [128, 8, 2], F32)
        sct = K.sb("sct", [128, 8, 2], F32)
        modt = K.sb("modt", [128, L, 48, 2], F32)
        a1t = K.sb("a1t", [128, L, 8, 2], F32)
        a2t = K.sb("a2t", [128, L, 8, 2], F32)
        smallw = {}
        for nm in ("dw2", "ia2", "gg2"):
            smallw[nm] = K.sb("sw_" + nm, [128, L, D], BF16)
        PS = [K.es.enter_context(nc.psum_tensor("ps%d" % i, [128, 512], F32)) for i in range(8)]

        def memset(b, ap, v):
            K.I("pool", [], [b], lambda e: e.memset(ap, v))

        def asel(b, ap, step, cm, op, fill, base=0, n=128):
            K.I("pool", [b], [b], lambda e: e.affine_select(out=ap, in_=ap, pattern=[[step, n]], compare_op=op,
                                                            fill=fill, base=base, channel_multiplier=cm))

        memset(ident, ident[:, :], 0.0)
        asel(ident, ident[:, :], -1, 1, ALU.not_equal, 1.0)
        K.cp("pool", identb[:, :], ident[:, :], [ident], [identb])
        memset(onesb, onesb[:, :], 1.0)
        memset(oblk, oblk[:, :], 0.0)
        memset(oblk, oblk[0:64, 0:64], 1.0)
        memset(oblk, oblk[64:128, 64:128], 1.0)
        memset(oblk64, oblk64[:, :], 0.0)
        memset(oblk64, oblk64[0:64, 0:64], 1.0 / 64)
        memset(oblk64, oblk64[64:128, 64:128], 1.0 / 64)
        memset(maskf, maskf[:, :], 1.0)
        memset(maskb, maskb[:, :], 1.0)
        asel(maskf, maskf[:, 0:128], 1, -1, ALU.is_gt, 0.0)
        asel(maskf, maskf[:, 128:256], 1, -1, ALU.is_ge, 0.0)
        asel(maskb, maskb[:, 0:128], -1, 1, ALU.is_gt, 0.0)
        asel(maskb, maskb[:, 128:256], -1, 1, ALU.is_ge, 0.0)
        for r2 in range(2):
            K.cp("pool", maskf2[:, 256 * r2:256 * r2 + 256], maskf[:, :], [maskf], [maskf2])
            K.cp("pool", maskb2[:, 256 * r2:256 * r2 + 256], maskb[:, :], [maskb], [maskb2])
        for r4 in range(4):
            K.cp("pool", mlow4[:, 128 * r4:128 * r4 + 128], maskb[:, 0:128], [maskb], [mlow4])
            K.cp("pool", mupp4[:, 128 * r4:128 * r4 + 128], maskf[:, 0:128], [maskf], [mupp4])
            K.cp("pool", identw4[:, 128 * r4:128 * r4 + 128], ident[:, :], [ident], [identw4])
        memset(rstf, rstf[:, :], 1.0)
        memset(rstb, rstb[:, :], 1.0)
        memset(rstf, rstf[:, :].rearrange("p (m t) -> p m t", t=128)[:, :, 0:1], 0.0)
        memset(rstb, rstb[:, :].rearrange("p (m t) -> p m t", t=128)[:, :, 127:128], 0.0)
        memset(epsn, epsn[:, :], 1e-6)
        memset(epsg, epsg[:, :], 64e-5)
        K.dma("sp", pvt[:, :, :], pv_in.rearrange("l p n -> p l n"), [], [pvt])
        K.dma("sp", nfg[:, :], nfg_in, [], [nfg])
        K.dma("sp", condt[:, :, :], cond_in, [], [condt])
        for l in range(L):
            K.ts("dve", omka[:, l, :], pvt[:, l, PV["ka"]:PV["ka"] + 8], -1.0, ALU.mult, [pvt], [omka], s2=1.0,
                 op1=ALU.add)
        K.act(sct[:, :, :], condt[:, :, :], AF.Silu, [condt], [sct])

        def pvc(l, name, j):
            o = PV[name] + j
            return pvt[:, l, o:o + 1]

        def phase_w():
            with ExitStack() as pes:
                stg = K.ring("wstg", [128, 2048], F32, 3, pes)
                stb = K.ring("wstb", [128, 2048], BF16, 3, pes)
                ci = 0
                for nm, wap, Kd, Nd in WSPEC:
                    for l in range(nlayers):
                        for kc in range(Kd // 128):
                            for c0 in range(0, Nd, 2048):
                                cw = min(2048, Nd - c0)
                                s = stg.new()
                                b = stb.new()
                                K.dma("sp", s[:, 0:cw], wap[l, kc * 128:(kc + 1) * 128, c0:c0 + cw], [], [s])
                                K.cp(("dve", "pool", "act")[ci % 3], b[:, 0:cw], s[:, 0:cw], [s], [b])
                                ci += 1
                                K.dma("pool", WB[nm][l, c0 // 128:(c0 + cw) // 128, :, kc, :].rearrange("j p c -> p j c"),
                                      b[:, 0:cw].rearrange("p (j c) -> p j c", c=128), [b], [])
                for nm, wap in (("dw2", dw2), ("ia2", ia2), ("gg2", gg2)):
                    for l in range(L):
                        s = stg.new()
                        K.dma("sp", s[:, 0:D], wap[l], [], [s])
                        K.cp("dve", smallw[nm][:, l, :], s[:, 0:D], [s], [smallw[nm]])
                wm = K.ring("wmod", [128, 8, 128], F32, 3, pes)
                pi = 0
                for l in range(nlayers):
                    for j in range(48):
                        w = wm.new()
                        K.dma("sp", w[:, :, :], w_mod[l, :, j * 128:(j + 1) * 128].rearrange("(kc p) c -> p kc c", p=128),
                              [], [w])
                        pr = preg_big[pi % 8]
                        pi += 1
                        for kc in range(8):
                            K.mm(pr, pr.v(2), w[:, kc, :], sct[:, kc, :], [w, sct], start=(kc == 0), stop=(kc == 7))
                        K.ts("dve", modt[:, l, j, :], pr.v(2), pvc(l, "bmod", j), ALU.add, [pr, pvt], [modt])
                    for kc in range(8):
                        K.ts("dve", a1t[:, l, kc, :], modt[:, l, 8 + kc, :], 1.0, ALU.add, [modt, pvt], [a1t],
                             s2=pvc(l, "n1g", kc), op1=ALU.mult)
                        K.ts("dve", a2t[:, l, kc, :], modt[:, l, 32 + kc, :], 1.0, ALU.add, [modt, pvt], [a2t],
                             s2=pvc(l, "n2g", kc), op1=ALU.mult)

        preg_big = [K.reg(PReg(PS[i], 0, 512)) for i in range(8)]
        preg_half = [K.reg(PReg(PS[i % 8], 256 * (i // 8), 256, bank=preg_big[i % 8].bank)) for i in range(16)]
        pbig = Ring(preg_big)
        phalf = Ring(preg_half)

        def modc(l, which, kc, ci):
            return modt[:, l, which * 8 + kc, ci:ci + 1]

        def phase0():
            with ExitStack() as pes:
                xin_r = K.ring("xin", [128, D], F32, 2, pes)
                xo_r = K.ring("xo", [128, 8, 128], F32, 2, pes)
                for blk in range(NTA // 128):
                    t0 = blk * 128
                    xi = xin_r.new()
                    K.dma("sp", xi[:, :], xin[t0:t0 + 128, :], [], [xi])
                    xo = xo_r.new()
                    for half in range(2):
                        pr = pbig.new()
                        for q in range(4):
                            kc = half * 4 + q
                            K.I("pe", [xi, ident], [pr],
                                lambda e, kc=kc, q=q, pr=pr, xi=xi: e.transpose(out=pr.v(128, o=q * 128),
                                                                               in_=xi[:, kc * 128:(kc + 1) * 128],
                                                                               identity=ident[:, :]))
                        K.cp("act" if half else "dve", xo[:, half * 4:half * 4 + 4, :],
                             pr.v(512).rearrange("p (k t) -> p k t", t=128), [pr], [xo])
                    K.dma("pool", XTA[:, t0:t0 + 128].rearrange("(kc p) t -> p kc t", p=128), xo[:, :, :], [xo], [])

        def rms_gen(xT, hT, nt, a_t, l, shwhich, ci, rb, sq, pring=None):
            pring = pring or pbig
            for c0 in range(0, nt, 512):
                cw = min(512, nt - c0)
                for kc in range(8):
                    K.act(sq[:, kc, 0:cw], xT[:, kc, c0:c0 + cw], AF.Square, [xT], [sq])
                    yield
                pr = pring.new()
                for kc in range(8):
                    K.mm(pr, pr.v(cw), onesb[:, :], sq[:, kc, 0:cw], [onesb, sq], start=(kc == 0), stop=(kc == 7))
                yield
                sd = rb["sd"]
                K.act(sd[:, 0:cw], pr.v(cw), AF.Sqrt, [pr, epsn], [sd], bias=epsn[:, 0:1], scale=1.0 / D)
                yield
                rs = rb["rs"]
                K.I("dve", [sd], [rs], lambda e, rs=rs, sd=sd, cw=cw: e.reciprocal(out=rs[:, 0:cw], in_=sd[:, 0:cw]))
                yield
                for kc in range(8):
                    t = rb["t"].new()
                    K.tt("dve" if kc % 2 else "pool", t[:, 0:cw], xT[:, kc, c0:c0 + cw], rs[:, 0:cw], ALU.mult,
                         [xT, rs], [t])
                    yield
                    if a_t is None:
                        K.ts("dve", hT[:, kc, c0:c0 + cw], t[:, 0:cw], nfg[:, kc:kc + 1], ALU.mult, [t, nfg], [hT])
                    else:
                        K.ts("dve", hT[:, kc, c0:c0 + cw], t[:, 0:cw], a_t[:, l, kc, ci:ci + 1], ALU.mult,
                             [t, a_t, modt], [hT], s2=modc(l, shwhich, kc, ci), op1=ALU.add)
                    yield

        def rms_bufs(name, pes):
            return dict(sd=K.sb(name + "sd", [128, 512], F32, pes), rs=K.sb(name + "rs", [128, 512], F32, pes),
                        t=K.ring(name + "t", [128, 512], F32, 2, pes))

        def gdrip(gen, n):
            if gen is None:
                return
            for _ in range(n):
                try:
                    next(gen)
                except StopIteration:
                    return

        def phase1(l, g):
            ci = g["cond"]
            rl = g["rl"]
            with ExitStack() as pes:
                xT_r = K.ring("p1x", [128, 8, TT], F32, 2, pes)
                hT2 = [K.sb("p1h%d" % i, [128, 8, TT], BF16, pes) for i in range(2)]
                rb = rms_bufs("p1r_", pes)
                sq = K.sb("p1sq", [128, 8, TT], BF16, pes)
                wr = K.ring("p1w", [128, 8, 128], BF16, 6, pes)
                tmp = K.ring("p1t", [128, TT], F32, 14, pes)
                tmb = K.ring("p1tb", [128, TT], BF16, 6, pes)
                r32r = K.ring("p1r", [128, TT], F32, 2, pes)
                k32r = K.ring("p1k", [128, TT], F32, 2, pes)
                kqr = K.ring("p1kq", [128, TT], F32, 2, pes)
                v16r = K.ring("p1v", [128, TT], BF16, 2, pes)
                sqkr = K.ring("p1sqk", [128, TT], BF16, 2, pes)
                avr = K.ring("p1av", [128, TT], F32, 4, pes)
                rkdr = K.ring("p1rkd", [128, TT], BF16, 4, pes)
                tw = K.sb("p1tw", [128, TT], BF16, pes)
                ta = K.sb("p1ta", [128, TT], BF16, pes)
                tg = K.sb("p1tg", [128, TT], BF16, pes)
                order = [24, 25, 26]
                for m in range(8):
                    order += [m, 8 + m, 16 + m]
                for m in range(8):
                    order += [27 + m, 35 + m, 43 + m]
                order += list(range(51, 67))
                ntile = g["nt"] // TT
                xTs = {}

                def xload(tix):
                    if tix < ntile and tix not in xTs:
                        tk = g["t0"] + tix * TT
                        xT = xT_r.new()
                        K.dma("sp", xT[:, :, :], XTA[:, tk:tk + TT].rearrange("(kc p) t -> p kc t", p=128), [], [xT])
                        xTs[tix] = xT

                xload(0)
                gdrip(rms_gen(xTs.pop(0), hT2[0], TT, a1t, l, 0, ci, rb, sq), 10 ** 6)
                for tix in range(ntile):
                    tk0 = g["t0"] + tix * TT
                    hT = hT2[tix % 2]
                    xload(tix + 1)
                    rgen = None
                    if tix + 1 < ntile:
                        rgen = rms_gen(xTs.pop(tix + 1), hT2[(tix + 1) % 2], TT, a1t, l, 0, ci, rb, sq)
                    wl = {}
                    st = {"n": 0}

                    def wload(upto):
                        while st["n"] <= min(upto, len(order) - 1):
                            w = wr.new()
                            K.dma("sp", w[:, :, :], WB["win"][l, order[st["n"]]], [], [w])
                            wl[st["n"]] = w
                            st["n"] += 1

                    pos = {"i": 0}

                    def zmm():
                        i = pos["i"]
                        pos["i"] += 1
                        wload(i + 4)
                        w = wl.pop(i)
                        pr = pbig.new()
                        for kc in range(8):
                            K.mm(pr, pr.v(TT), w[:, kc, :], hT[:, kc, :], [w, hT], start=(kc == 0), stop=(kc == 7))
                        return pr

                    def store(dst, src_b, src_ap):
                        K.dma("pool", dst, src_ap, [src_b], [])

                    z = zmm()
                    K.act(tw[:, :], z.v(TT), AF.Tanh, [z], [tw])
                    z = zmm()
                    K.cp("dve", ta[:, :], z.v(TT), [z], [ta])
                    z = zmm()
                    K.act(tg[:, :], z.v(TT), AF.Sigmoid, [z], [tg])
                    cols = slice(tk0, tk0 + TT)

                    def part1(m):
                        rows = slice(m * 128, (m + 1) * 128)
                        zr = zmm()
                        zk = zmm()
                        zv = zmm()
                        r32 = r32r.new()
                        K.cp("act", r32[:, :], zr.v(TT), [zr], [r32])
                        store(ZR[rows, cols], r32, r32[:, :])
                        k32 = k32r.new()
                        K.cp("dve", k32[:, :], zk.v(TT), [zk], [k32])
                        kq = kqr.new()
                        K.ts("dve", kq[:, :], zk.v(TT), pvc(l, "kk", m), ALU.mult, [zk, pvt], [kq])
                        v16 = v16r.new()
                        K.cp("act", v16[:, :], zv.v(TT), [zv], [v16])
                        store(ZV[rows, cols], v16, v16[:, :])
                        sqk = sqkr.new()
                        K.act(sqk[:, :], kq[:, :], AF.Square, [kq], [sqk])
                        avs, rkds = [], []
                        pls = []
                        for d in range(2):
                            if os.environ.get('P1_Y') == '1':
                                pls.append((None, None))
                                continue
                            hp = slice(64 * d, 64 * d + 64)
                            pl = pbig.new()
                            K.mm(pl, pl.v(TT), smallw["dw2"][hp, l, m * 128:(m + 1) * 128], tw[hp, :],
                                 [smallw["dw2"], tw])
                            pa = pbig.new()
                            K.mm(pa, pa.v(TT), smallw["ia2"][hp, l, m * 128:(m + 1) * 128], ta[hp, :],
                                 [smallw["ia2"], ta])
                            pls.append((pl, pa))
                        pg = pbig.new()
                        K.mm(pg, pg.v(TT), smallw["gg2"][:, l, m * 128:(m + 1) * 128], tg[:, :], [smallw["gg2"], tg])
                        for d in range(2):
                            pl, pa = pls[d]
                            if os.environ.get('P1_Y') == '1':
                                hp = slice(64 * d, 64 * d + 64)
                                pl = pbig.new()
                                K.mm(pl, pl.v(TT), smallw["dw2"][hp, l, m * 128:(m + 1) * 128], tw[hp, :],
                                     [smallw["dw2"], tw])
                            sg = tmp.new()
                            K.act(sg[:, :], pl.v(TT), AF.Sigmoid, [pl, pvt], [sg], bias=pvc(l, "w0", d * 8 + m))
                            lw = tmp.new()
                            K.act(lw[:, :], sg[:, :], AF.Identity, [sg], [lw], scale=-0.6065306597126334)
                            store(ZLW[d, rows, cols], lw, lw[:, :])
                            if os.environ.get('P1_Y') == '1':
                                pa = pbig.new()
                                K.mm(pa, pa.v(TT), smallw["ia2"][hp, l, m * 128:(m + 1) * 128], ta[hp, :],
                                     [smallw["ia2"], ta])
                            av = avr.new()
                            K.act(av[:, :], pa.v(TT), AF.Sigmoid, [pa, pvt], [av], bias=pvc(l, "a0", d * 8 + m))
                            f = tmp.new()
                            K.ts("dve", f[:, :], av[:, :], pvc(l, "ka", m), ALU.mult, [av, pvt, omka], [f],
                                 s2=omka[:, l, m:m + 1], op1=ALU.add)
                            kd = tmp.new()
                            K.tt("dve", kd[:, :], k32[:, :], f[:, :], ALU.mult, [k32, f], [kd])
                            store(ZKD[d, rows, cols], kd, kd[:, :])
                            rkd = rkdr.new()
                            K.stt(rkd[:, :], r32[:, :], pvc(l, "rk", m), kd[:, :], ALU.mult, ALU.mult,
                                  [r32, kd, pvt], [rkd])
                            avs.append(av)
                            rkds.append(rkd)
                        gt = tmb.new()
                        K.cp("act", gt[:, :], pg.v(TT), [pg], [gt])
                        store(ZG[rows, cols], gt, gt[:, :])
                        return dict(m=m, kq=kq, sqk=sqk, v16=v16, avs=avs, rkds=rkds)

                    def part2(c):
                        m = c["m"]
                        rows = slice(m * 128, (m + 1) * 128)
                        pss = pbig.new()
                        K.mm(pss, pss.v(TT), oblk[:, :], c["sqk"][:, :], [oblk, c["sqk"]])
                        pbs = pbig.new()
                        for d in range(2):
                            K.mm(pbs, pbs.v(TT), oblk[:, :], c["rkds"][d][:, :], [oblk, c["rkds"][d]], start=(d == 0),
                                 stop=(d == 1))
                        nrm = tmp.new()
                        K.act(nrm[:, :], pss.v(TT), AF.Sqrt, [pss], [nrm])
                        K.ts("dve", nrm[:, :], nrm[:, :], 1e-12, ALU.max, [nrm], [nrm])
                        rinv = tmp.new()
                        K.I("dve", [nrm], [rinv], lambda e, a=rinv, b=nrm: e.reciprocal(out=a[:, :], in_=b[:, :]))
                        kk = tmp.new()
                        K.tt("dve", kk[:, :], c["kq"][:, :], rinv[:, :], ALU.mult, [c["kq"], rinv], [kk])
                        store(ZKK[rows, cols], kk, kk[:, :])
                        for d in range(2):
                            ba = tmp.new()
                            K.tt("dve" if d else "pool", ba[:, :], kk[:, :], c["avs"][d][:, :], ALU.mult,
                                 [kk, c["avs"][d]], [ba])
                            store(ZBA[d, rows, cols], ba, ba[:, :])
                        bon = tmb.new()
                        K.tt("dve", bon[:, :], pbs.v(TT), c["v16"][:, :], ALU.mult, [pbs, c["v16"]], [bon])
                        store(ZBON[rows, cols], bon, bon[:, :])

                    prev = None
                    for m in range(8):
                        cur = part1(m)
                        if os.environ.get('P1_X') == '1':
                            part2(cur)
                            continue
                        if prev is not None:
                            part2(prev)
                        prev = cur
                    first_conv = True
                    for m in range(8):
                        rows = slice(m * 128, (m + 1) * 128)
                        zcb = zmm()
                        zcc = zmm()
                        zcx = zmm()
                        if first_conv and prev is not None:
                            part2(prev)
                            first_conv = False
                        cxs = tmp.new()
                        K.cp("act", cxs[:, :], zcx.v(TT), [zcx], [cxs])
                        pp = tmp.new()
                        K.tt("dve", pp[:, :], zcc.v(TT), cxs[:, :], ALU.mult, [zcc, cxs], [pp])
                        acc = tmp.new()
                        K.ts("pool", acc[:, :], pp[:, :], pvc(l, "cmw", 8 + m), ALU.mult, [pp, pvt], [acc],
                             s2=pvc(l, "cmb", m), op1=ALU.add)
                        p3 = pp[:, :].rearrange("p (r c) -> p r c", c=rl)
                        a3 = acc[:, :].rearrange("p (r c) -> p r c", c=rl)
                        K.stt(a3[:, :, 1:rl], p3[:, :, 0:rl - 1], pvc(l, "cmw", m), a3[:, :, 1:rl], ALU.mult, ALU.add,
                              [pp, acc, pvt], [acc])
                        K.stt(a3[:, :, 0:rl - 1], p3[:, :, 1:rl], pvc(l, "cmw", 16 + m), a3[:, :, 0:rl - 1], ALU.mult,
                              ALU.add, [pp, acc, pvt], [acc])
                        ob = tmb.new()
                        K.tt("dve", ob[:, :], zcb.v(TT), acc[:, :], ALU.mult, [zcb, acc], [ob])
                        store(ZOB[rows, cols], ob, ob[:, :])
                    for j in range(16):
                        zg = zmm()
                        sgt = tmb.new()
                        K.act(sgt[:, :], zg.v(TT), AF.Sigmoid, [zg], [sgt])
                        store(ZGAB[j * 128:(j + 1) * 128, tk0:tk0 + TT], sgt, sgt[:, :])
                        gdrip(rgen, 3)
                    gdrip(rgen, 10 ** 6)

        def phase2(l, g):
            with ExitStack() as pes:
                ld = {}
                for nm, dt in (("R", F32), ("KK", F32), ("KD", F32), ("BA", F32), ("LW", F32), ("V", BF16)):
                    ld[nm] = [K.sb("p2%s%d" % (nm, d), [128, 1024], dt, pes) for d in range(2)]
                cum = [K.sb("p2cum%d" % d, [128, 1024], F32, pes) for d in range(2)]
                tmp = K.ring("p2t", [128, 1024], F32, 3, pes)
                AR = [K.sb("p2ar%d" % d, [128, 8, 2, 128], BF16, pes) for d in range(2)]
                BH = [K.sb("p2bh%d" % d, [128, 1024], BF16, pes) for d in range(2)]
                KH = [K.sb("p2kh%d" % d, [128, 1024], BF16, pes) for d in range(2)]
                GC = [K.sb("p2gc%d" % d, [128, 8], F32, pes) for d in range(2)]
                DG = [K.sb("p2dg%d" % d, [128, 8, 128], BF16, pes) for d in range(2)]
                VT = [K.sb("p2vt%d" % d, [128, 8, 128], BF16, pes) for d in range(2)]
                BT = [K.sb("p2bt%d" % d, [128, 8, 128], BF16, pes) for d in range(2)]
                KT = [K.sb("p2kt%d" % d, [128, 8, 128], BF16, pes) for d in range(2)]
                YT = [K.sb("p2yt%d" % d, [128, 8, 128], F32, pes) for d in range(2)]
                S32 = [K.sb("p2s32_%d" % d, [128, 8, 64], F32, pes) for d in range(2)]
                S16 = [K.sb("p2s16_%d" % d, [128, 8, 64], BF16, pes) for d in range(2)]
                LMK = [K.sb("p2lmk%d" % gq, [128, 4, 256], WD, pes) for gq in range(4)]
                LMB = [K.sb("p2lmb%d" % gq, [128, 4, 256], WD, pes) for gq in range(4)]
                LAB = [K.sb("p2lab%d" % gq, [128, 4, 128], WD, pes) for gq in range(4)]
                PP = [[K.sb("p2p%d_%d" % (gq, i), [128, 4, 128], WD, pes) for i in range(2)] for gq in range(4)]
                PT = [[K.sb("p2pt%d_%d" % (gq, i), [128, 4, 128], WD, pes) for i in range(2)] for gq in range(4)]
                TTt = [[K.sb("p2tt%d_%d" % (gq, i), [128, 4, 128], WD, pes) for i in range(2)] for gq in range(4)]
                XX = [K.sb("p2x%d" % u, [128, 8, 64], BF16, pes) for u in range(2)]
                UU = [K.sb("p2u%d" % u, [128, 8, 64], BF16, pes) for u in range(2)]
                sld = K.sb("p2sld", [64, NH, HD], F32, pes)
                sout = K.ring("p2so", [64, 128], F32, 2, pes)
                identw = identb if WD == BF16 else ident

                def v3(b):
                    return b[:, :].rearrange("p (m t) -> p m t", t=128)

                for si, (s0, slen) in enumerate(g["seqs"]):
                    nch = slen // C
                    for d in range(2):
                        if g["grid"]:
                            K.dma("sp", sld[:, :, :], st_in[l, d].rearrange("h i j -> i h j"), [], [sld])
                            for m in range(8):
                                pr = pbig.new()
                                K.I("pe", [sld, ident], [pr],
                                    lambda e, pr=pr, m=m: e.transpose(
                                        out=pr.v(64), in_=sld[:, 2 * m:2 * m + 2, :].rearrange("i h j -> i (h j)"),
                                        identity=ident[0:64, 0:64]))
                                K.cp("dve", S32[d][:, m, :], pr.v(64), [pr], [S32[d]])
                                K.cp("act", S16[d][:, m, :], pr.v(64), [pr], [S16[d]])
                        else:
                            memset(S32[d], S32[d][:, :, :], 0.0)
                            memset(S16[d], S16[d][:, :, :], 0.0)

                    def stageA(step, d):
                        cidx = step if d == 0 else nch - 1 - step
                        tk0 = s0 + cidx * C
                        for nm, src in (("R", ZR), ("KK", ZKK), ("KD", ZKD[d]), ("BA", ZBA[d]), ("LW", ZLW[d]),
                                        ("V", ZV)):
                            b = ld[nm][d]
                            K.dma("sp", v3(b), src[:, tk0:tk0 + C].rearrange("(m p) t -> p m t", p=128), [], [b])
                            yield
                        lw = ld["LW"][d]
                        cm = cum[d]
                        if d == 0:
                            K.I("dve", [rstf, lw], [cm],
                                lambda e, cm=cm, lw=lw: e.tensor_tensor_scan(out=cm[:, :], data0=rstf[:, :],
                                                                            data1=lw[:, :], initial=0.0,
                                                                            op0=ALU.mult, op1=ALU.add))
                            cend = v3(cm)[:, :, 127:128]
                        else:
                            K.I("dve", [rstb, lw], [cm],
                                lambda e, cm=cm, lw=lw: e.tensor_tensor_scan(out=cm[:, ::-1], data0=rstb[:, ::-1],
                                                                            data1=lw[:, ::-1], initial=0.0,
                                                                            op0=ALU.mult, op1=ALU.add))
                            cend = v3(cm)[:, :, 0:1]
                        yield
                        K.act(GC[d][:, :].rearrange("p (m o) -> p m o", o=1), cend, AF.Exp, [cm], [GC[d]])
                        yield
                        for m in range(8):
                            K.ts("dve", DG[d][:, m, :], identb[:, :], GC[d][:, m:m + 1], ALU.mult,
                                 [identb, GC[d]], [DG[d]])
                        yield
                        er = tmp.new()
                        K.act(er[:, :], cm[:, :], AF.Exp, [cm], [er])
                        yield
                        K.tt("pool", AR[d][:, :, 1, :], v3(ld["R"][d]), v3(er), ALU.mult, [ld["R"][d], er], [AR[d]])
                        yield
                        cml = tmp.new()
                        K.tt("pool", cml[:, :], cm[:, :], lw[:, :], ALU.subtract, [cm, lw], [cml])
                        yield
                        ea = tmp.new()
                        K.act(ea[:, :], cml[:, :], AF.Exp, [cml], [ea])
                        yield
                        K.stt(AR[d][:, :, 0, :], v3(ld["KK"][d]), -1.0, v3(ea), ALU.mult, ALU.mult,
                              [ld["KK"][d], ea], [AR[d]])
                        yield
                        en = tmp.new()
                        K.act(en[:, :], cm[:, :], AF.Exp, [cm], [en], scale=-1.0)
                        yield
                        K.tt("pool", BH[d][:, :], ld["BA"][d][:, :], en[:, :], ALU.mult, [ld["BA"][d], en], [BH[d]])
                        yield
                        K.tt("dve", KH[d][:, :], ld["KD"][d][:, :], en[:, :], ALU.mult, [ld["KD"][d], en], [KH[d]])
                        yield
                        for (srcb, dst, useid) in ((ld["V"][d], VT[d], True), (BH[d], BT[d], False),
                                                   (KH[d], KT[d], False)):
                            for m4 in range(2):
                                pr = pbig.new()
                                for q in range(4):
                                    m = m4 * 4 + q
                                    rhs = identb[:, :] if useid else DG[d][:, m, :]
                                    K.mm(pr, pr.v(128, o=q * 128), v3(srcb)[:, m, :], rhs,
                                         [srcb, identb if useid else DG[d]])
                                K.cp("act" if m4 % 2 else "dve", dst[:, 4 * m4:4 * m4 + 4, :],
                                     pr.v(512).rearrange("p (m c) -> p m c", c=128), [pr], [dst])
                                yield

                    def drip(gen, n):
                        if gen is None:
                            return
                        for _ in range(n):
                            try:
                                next(gen)
                            except StopIteration:
                                return

                    GH = [[8 * (gq // 2) + (gq % 2) + 2 * e4 for e4 in range(4)] for gq in range(4)]

                    def gof(h):
                        return 2 * (h // 8) + (h % 2), (h % 8) // 2

                    def stageBCD(step, d, gen):
                        cidx = step if d == 0 else nch - 1 - step
                        tk0 = s0 + cidx * C
                        if os.environ.get('WKV_STOP') == 'A':
                            return
                        mk = maskf2 if d == 0 else maskb2
                        mk2 = mlow4 if d == 0 else mupp4
                        wkvb = os.environ.get('WKV_B', 'klt')
                        for gq in range(4):
                            for hh in range(2 if 'k' in wkvb else 0):
                                for (srcK, dstT) in ((KH[d], LMK[gq]), (BH[d], LMB[gq])):
                                    pr = pbig.new()
                                    for e2 in range(2):
                                        h = GH[gq][2 * hh + e2]
                                        m = h // 2
                                        hp = slice(64 * (h % 2), 64 * (h % 2) + 64)
                                        arh = AR[d][hp, m, :, :].rearrange("p a t -> p (a t)")
                                        K.mm(pr, pr.v(256, o=256 * e2), v3(srcK)[hp, m, :], arh, [srcK, AR[d]])
                                    K.tt("dve", dstT[:, 2 * hh:2 * hh + 2, :],
                                         pr.v(512).rearrange("p (a c) -> p a c", c=256),
                                         mk[:, :].rearrange("p (a c) -> p a c", c=256), ALU.mult, [pr, mk], [dstT])
                            if 'l' in wkvb:
                                pr = pbig.new()
                                for e4 in range(4):
                                    h = GH[gq][e4]
                                    m = h // 2
                                    hp = slice(64 * (h % 2), 64 * (h % 2) + 64)
                                    K.mm(pr, pr.v(128, o=128 * e4), AR[d][hp, m, 0, :], v3(BH[d])[hp, m, :],
                                         [AR[d], BH[d]])
                                K.tt("dve", LAB[gq][:, :, :], pr.v(512).rearrange("p (a c) -> p a c", c=128),
                                     mk2[:, :].rearrange("p (a c) -> p a c", c=128), ALU.mult, [pr, mk2], [LAB[gq]])
                            if 't' in wkvb:
                                K.tt("pool", TTt[gq][0][:, :, :], identw4[:, :].rearrange("p (a c) -> p a c", c=128),
                                     LMB[gq][:, :, 0:128], ALU.add, [identw4, LMB[gq]], [TTt[gq][0]])
                        drip(gen, 3)
                        if os.environ.get('WKV_STOP') == 'B':
                            return
                        for k in range(6):
                            for gq in range(4):
                                pkb = LAB[gq] if k == 0 else PP[gq][(k - 1) % 2]
                                ptb = LMB[gq] if k == 0 else PT[gq][(k - 1) % 2]
                                pn = PP[gq][k % 2]
                                pr = pbig.new()
                                for e4 in range(4):
                                    ptk = ptb[:, e4, 0:128]
                                    K.mm(pr, pr.v(128, o=128 * e4), ptk, pkb[:, e4, :], [ptb, pkb])
                                K.cp("act", pn[:, :, :], pr.v(512).rearrange("p (a c) -> p a c", c=128), [pr], [pn])
                                if k < 5:
                                    ptn = PT[gq][k % 2]
                                    pr2 = pbig.new()
                                    for e4 in range(4):
                                        ptk = ptb[:, e4, 0:128]
                                        K.mm(pr2, pr2.v(128, o=128 * e4), pkb[:, e4, :], ptk, [ptb, pkb])
                                    K.cp("dve" if gq == 3 else "act", ptn[:, :, :],
                                         pr2.v(512).rearrange("p (a c) -> p a c", c=128), [pr2], [ptn])
                            drip(gen, 3)
                            for gq in range(4):
                                pn = PP[gq][k % 2]
                                tcur = TTt[gq][k % 2]
                                tnx = TTt[gq][(k + 1) % 2]
                                pr3 = pbig.new()
                                for e4 in range(4):
                                    K.mm(pr3, pr3.v(128, o=128 * e4), pn[:, e4, :], tcur[:, e4, :], [pn, tcur])
                                K.tt("dve", tnx[:, :, :], pr3.v(512).rearrange("p (a c) -> p a c", c=128),
                                     tcur[:, :, :], ALU.add, [pr3, tcur], [tnx])
                            drip(gen, 3)
                        if os.environ.get('WKV_STOP') == 'C':
                            return
                        for h8 in range(2):
                            px = pbig.new()
                            for e8 in range(8):
                                h = 8 * h8 + e8
                                m, q = h // 2, h % 2
                                hp = slice(64 * q, 64 * q + 64)
                                vh = VT[d][:, m, 64 * q:64 * q + 64]
                                K.mm(px, px.v(64, o=64 * e8), AR[d][hp, m, 0, :], S16[d][hp, m, :], [AR[d], S16[d]],
                                     start=True, stop=False)
                                K.mm(px, px.v(64, o=64 * e8), LMK[gof(h)[0]][:, gof(h)[1], 0:128], vh, [LMK[gof(h)[0]], VT[d]],
                                     start=False, stop=True)
                            K.cp("act" if h8 else "dve", XX[h8][:, :, :], px.v(512).rearrange("p (a c) -> p a c", c=64),
                                 [px], [XX[h8]])
                        drip(gen, 4)
                        for h8 in range(2):
                            pu = pbig.new()
                            for e8 in range(8):
                                h = 8 * h8 + e8
                                tT = TTt[gof(h)[0]][0]
                                K.mm(pu, pu.v(64, o=64 * e8), tT[:, gof(h)[1], :], XX[h8][:, e8, :], [tT, XX[h8]])
                            K.cp("dve" if h8 else "act", UU[h8][:, :, :], pu.v(512).rearrange("p (a c) -> p a c", c=64),
                                 [pu], [UU[h8]])
                        drip(gen, 4)
                        if os.environ.get('WKV_STOP') == 'D2':
                            return
                        psn = pbig.new()
                        for m4 in range(2):
                            py = pbig.new()
                            for e4 in range(4):
                                m = 4 * m4 + e4
                                for q in range(2):
                                    h = 2 * m + q
                                    hp = slice(64 * q, 64 * q + 64)
                                    vh = VT[d][:, m, 64 * q:64 * q + 64]
                                    uh = UU[h // 8][:, h % 8, :]
                                    lmk = LMK[gof(h)[0]]
                                    lmb = LMB[gof(h)[0]]
                                    yo = py.v(128, hp.start, hp.stop, o=128 * e4)
                                    K.mm(py, yo, S16[d][hp, m, :], AR[d][hp, m, 1, :], [S16[d], AR[d]],
                                         start=True, stop=False)
                                    K.mm(py, yo, uh, lmb[:, gof(h)[1], 128:256], [UU[h // 8], lmb], start=False, stop=False)
                                    K.mm(py, yo, vh, lmk[:, gof(h)[1], 128:256], [VT[d], lmk], start=False, stop=True)
                                    so_ = psn.v(64, hp.start, hp.stop, o=64 * m)
                                    K.mm(psn, so_, BT[d][:, m, 64 * q:64 * q + 64], uh, [BT[d], UU[h // 8]],
                                         start=True, stop=False)
                                    K.mm(psn, so_, KT[d][:, m, 64 * q:64 * q + 64], vh, [KT[d], VT[d]],
                                         start=False, stop=True)
                            K.cp("act", YT[d][:, 4 * m4:4 * m4 + 4, :], py.v(512).rearrange("p (a c) -> p a c", c=128),
                                 [py], [YT[d]])
                        for m in range(8):
                            K.stt(S32[d][:, m, :], S32[d][:, m, :], GC[d][:, m:m + 1], psn.v(64, o=64 * m), ALU.mult,
                                  ALU.add, [S32[d], GC[d], psn], [S32[d]])
                        K.cp("act", S16[d][:, :, :], S32[d][:, :, :], [S32[d]], [S16[d]])
                        K.dma("pool", YTD[d, :, tk0:tk0 + C].rearrange("(m p) t -> p m t", p=128), YT[d][:, :, :],
                              [YT[d]], [])

                    g0 = stageA(0, 0)
                    drip(g0, 10 ** 6)
                    for step in range(nch):
                        for d in range(2):
                            if d == 0:
                                gen = stageA(step, 1)
                            elif step + 1 < nch:
                                gen = stageA(step + 1, 0)
                            else:
                                gen = None
                            stageBCD(step, d, gen)
                            drip(gen, 10 ** 6)
                    if not g["grid"]:
                        for d in range(2):
                            for m in range(8):
                                pr = pbig.new()
                                K.I("pe", [S32[d], ident], [pr],
                                    lambda e, pr=pr, d=d, m=m: e.transpose(out=pr.v(128, 0, 64), in_=S32[d][:, m, :],
                                                                           identity=ident[:, :]))
                                so = sout.new()
                                K.cp("dve", so[:, :], pr.v(128, 0, 64), [pr], [so])
                                K.dma("pool", nsout[si, l, d, 2 * m:2 * m + 2].rearrange("h i j -> i h j"),
                                      so[:, :].rearrange("i (h j) -> i h j", j=64), [so], [])

        def phase3(l, g):
            ci = g["cond"]
            with ExitStack() as pes:
                yf = [K.sb("p3yf%d" % m, [128, TT], F32, pes) for m in range(8)]
                yb = [K.sb("p3yb%d" % m, [128, TT], F32, pes) for m in range(8)]
                h16 = [K.sb("p3h%d" % m, [128, TT], BF16, pes) for m in range(8)]
                bon = K.sb("p3bon", [128, 8, TT], BF16, pes)
                gg = K.sb("p3g", [128, 8, TT], BF16, pes)
                ob = K.sb("p3ob", [128, 8, TT], BF16, pes)
                gab = K.sb("p3gab", [128, 16, TT], BF16, pes)
                xT = K.sb("p3x", [128, 8, TT], F32, pes)
                oa = K.sb("p3oa", [128, 8, TT], BF16, pes)
                mg = K.sb("p3mg", [128, 8, TT], BF16, pes)
                tmp = K.ring("p3t", [128, TT], F32, 6, pes)
                wr = K.ring("p3w", [128, 8, 128], BF16, 4, pes)
                ntile3 = g["nt"] // TT

                def ld3(b, src):
                    K.dma("sp", b[:, :, :], src.rearrange("(m p) t -> p m t", p=128), [], [b])

                def loads3(tix, which):
                    if tix >= ntile3:
                        return
                    cs_ = slice(g["t0"] + tix * TT, g["t0"] + (tix + 1) * TT)
                    if which == 0:
                        for m in range(8):
                            K.dma("sp", yf[m][:, :], YTD[0, m * 128:(m + 1) * 128, cs_], [], [yf[m]])
                            K.dma("sp", yb[m][:, :], YTD[1, m * 128:(m + 1) * 128, cs_], [], [yb[m]])
                        ld3(bon, ZBON[:, cs_])
                        ld3(gg, ZG[:, cs_])
                    elif which == 1:
                        ld3(ob, ZOB[:, cs_])
                    elif which == 2:
                        ld3(gab, ZGAB[:, cs_])
                    else:
                        ld3(xT, XTA[:, cs_])

                for w_ in range(4):
                    loads3(0, w_)
                for tix in range(ntile3):
                    tk0 = g["t0"] + tix * TT
                    cs = slice(tk0, tk0 + TT)
                    for m in range(8):
                        K.tt("pool", yf[m][:, :], yf[m][:, :], yb[m][:, :], ALU.add, [yf[m], yb[m]], [yf[m]])
                    for m in range(8):
                        K.cp("act", h16[m][:, :], yf[m][:, :], [yf[m]], [h16[m]])
                    pms = []
                    for m in range(8):
                        pm = pbig.new()
                        K.mm(pm, pm.v(TT), oblk64[:, :], h16[m][:, :], [oblk64, h16[m]])
                        pms.append(pm)
                    for m in range(8):
                        K.tt("dve", yf[m][:, :], yf[m][:, :], pms[m].v(TT), ALU.subtract, [yf[m], pms[m]], [yf[m]])
                    for m in range(8):
                        K.act(h16[m][:, :], yf[m][:, :], AF.Square, [yf[m]], [h16[m]])
                    pvs = []
                    for m in range(8):
                        pv_ = pbig.new()
                        K.mm(pv_, pv_.v(TT), oblk64[:, :], h16[m][:, :], [oblk64, h16[m]])
                        pvs.append(pv_)
                    for m in range(8):
                        K.act(yb[m][:, :], pvs[m].v(TT), AF.Sqrt, [pvs[m], epsg], [yb[m]], bias=epsg[:, 0:1])
                    for m in range(8):
                        K.I("dve", [yb[m]], [yb[m]], lambda e, b=yb[m]: e.reciprocal(out=b[:, :], in_=b[:, :]))
                    for m in range(8):
                        K.tt("dve", yf[m][:, :], yf[m][:, :], yb[m][:, :], ALU.mult, [yf[m], yb[m]], [yf[m]])
                    for m in range(8):
                        K.ts("pool", yf[m][:, :], yf[m][:, :], pvc(l, "gnw", m), ALU.mult, [yf[m], pvt], [yf[m]],
                             s2=pvc(l, "gnb", m), op1=ALU.add)
                    for m in range(8):
                        K.tt("dve" if m % 2 else "pool", yf[m][:, :], yf[m][:, :], bon[:, m, :], ALU.add,
                             [yf[m], bon], [yf[m]])
                    for m in range(8):
                        K.tt("dve", oa[:, m, :], yf[m][:, :], gg[:, m, :], ALU.mult, [yf[m], gg], [oa])
                    loads3(tix + 1, 0)
                    for mo in range(8):
                        wa = wr.new()
                        K.dma("sp", wa[:, :, :], WB["wpa"][l, mo], [], [wa])
                        wb_ = wr.new()
                        K.dma("sp", wb_[:, :, :], WB["wpb"][l, mo], [], [wb_])
                        pA = pbig.new()
                        pB = pbig.new()
                        for m in range(8):
                            K.mm(pA, pA.v(TT), wa[:, m, :], oa[:, m, :], [wa, oa], start=(m == 0), stop=(m == 7))
                        for m in range(8):
                            K.mm(pB, pB.v(TT), wb_[:, m, :], ob[:, m, :], [wb_, ob], start=(m == 0), stop=(m == 7))
                        t1 = tmp.new()
                        K.tt("dve", t1[:, :], pA.v(TT), gab[:, mo, :], ALU.mult, [pA, gab], [t1])
                        t2 = tmp.new()
                        K.tt("dve", t2[:, :], pB.v(TT), gab[:, 8 + mo, :], ALU.mult, [pB, gab], [t2])
                        K.tt("pool", mg[:, mo, :], t1[:, :], t2[:, :], ALU.add, [t1, t2], [mg])
                    loads3(tix + 1, 1)
                    loads3(tix + 1, 2)
                    for mo in range(8):
                        ww = wr.new()
                        K.dma("sp", ww[:, :, :], WB["wo"][l, mo], [], [ww])
                        pO = pbig.new()
                        for m in range(8):
                            K.mm(pO, pO.v(TT), ww[:, m, :], mg[:, m, :], [ww, mg], start=(m == 0), stop=(m == 7))
                        xm = tmp.new()
                        K.stt(xm[:, :], pO.v(TT), modc(l, 2, mo, ci), xT[:, mo, :], ALU.mult, ALU.add,
                              [pO, modt, xT], [xm])
                        K.dma("pool", XTB[mo * 128:(mo + 1) * 128, cs], xm[:, :], [xm], [])
                    loads3(tix + 1, 3)

        def phase4(l, g):
            ci = g["cond"]
            H = g["H"]
            sh = g["sh"]
            W = TT + 2 * H
            with ExitStack() as pes:
                xm_r = K.ring("p4x", [128, 8, W], F32, 2, pes)
                h2r = [K.sb("p4h%d" % i, [128, 8, W], BF16, pes) for i in range(2)]
                rb = rms_bufs("p4r_", pes)
                sq = K.sb("p4sq", [128, 8, 512], BF16, pes)
                qq = K.sb("p4q", [128, 22, TT], BF16, pes)
                tmp = K.ring("p4t", [128, TT], F32, 12, pes)
                wr = K.ring("p4w", [128, 8, 128], BF16, 4, pes)
                wdr = K.ring("p4wd", [128, 22, 128], BF16, 2, pes)
                pairs = [K.reg(PReg(PS[2 * i], 0, 1024)) for i in range(3)]
                p4big = Ring([preg_big[6], preg_big[7]])
                for (s0, slen) in ([(g["t0"], g["nt"])] if not g["grid"] else g["seqs"]):
                    xms = {}

                    def xmload(tix):
                        if tix >= slen // TT or tix in xms:
                            return
                        tk = s0 + tix * TT
                        lo_ = max(s0, tk - H)
                        hi_ = min(s0 + slen, tk + TT + H)
                        b = xm_r.new()
                        K.dma("sp", b[:, :, 0:hi_ - lo_], XTB[:, lo_:hi_].rearrange("(kc p) t -> p kc t", p=128), [], [b])
                        xms[tix] = b

                    def nof(tix):
                        tk = s0 + tix * TT
                        return min(s0 + slen, tk + TT + H) - max(s0, tk - H)

                    xmload(0)
                    gdrip(rms_gen(xms[0], h2r[0], nof(0), a2t, l, 3, ci, rb, sq, p4big), 10 ** 6)
                    for tix in range(slen // TT):
                        tk0 = s0 + tix * TT
                        lo = max(s0, tk0 - H)
                        hi = min(s0 + slen, tk0 + TT + H)
                        n = hi - lo
                        co = tk0 - lo
                        xm = xms.pop(tix)
                        h2 = h2r[tix % 2]
                        xmload(tix + 1)
                        rgen = None
                        if tix + 1 < slen // TT:
                            rgen = rms_gen(xms[tix + 1], h2r[(tix + 1) % 2], nof(tix + 1), a2t, l, 3, ci, rb, sq, p4big)

                        def umm(j, pi):
                            w = wr.new()
                            K.dma("sp", w[:, :, :], WB["wup"][l, j], [], [w])
                            pr = pairs[pi]
                            for c0 in range(0, n, 512):
                                cw = min(512, n - c0)
                                for kc in range(8):
                                    K.mm(pr, PS[2 * pi + c0 // 512][:, 0:cw], w[:, kc, :],
                                         h2[:, kc, c0:c0 + cw], [w, h2], start=(kc == 0), stop=(kc == 7))
                            return pr

                        def uap(pi, a, b):
                            bk = a // 512
                            assert (b - 1) // 512 == bk
                            return PS[2 * pi + bk][:, a - 512 * bk:b - 512 * bk]

                        def conv(pr, pi, j):
                            acc = tmp.new()
                            w0 = pvc(l, "cfw", j)
                            w1 = pvc(l, "cfw", 44 + j)
                            w2 = pvc(l, "cfw", 88 + j)
                            bb = pvc(l, "cfb", j)
                            for (a, b) in splits(co, co + TT):
                                K.ts("dve", acc[:, a - co:b - co], uap(pi, a, b), w1, ALU.mult, [pr, pvt], [acc],
                                     s2=bb, op1=ALU.add)
                            if g["grid"]:
                                a0 = max(tk0, s0 + sh)
                                b0 = min(tk0 + TT, s0 + slen - sh)
                                for (a, b) in splits(a0 - sh - lo, tk0 + TT - sh - lo):
                                    K.stt(acc[:, a + sh + lo - tk0:b + sh + lo - tk0], uap(pi, a, b), w0,
                                          acc[:, a + sh + lo - tk0:b + sh + lo - tk0], ALU.mult, ALU.add,
                                          [pr, acc, pvt], [acc])
                                for (a, b) in splits(tk0 + sh - lo, b0 + sh - lo):
                                    K.stt(acc[:, a - sh + lo - tk0:b - sh + lo - tk0], uap(pi, a, b), w2,
                                          acc[:, a - sh + lo - tk0:b - sh + lo - tk0], ALU.mult, ALU.add,
                                          [pr, acc, pvt], [acc])
                            else:
                                rl = g["rl"]
                                u3 = PS[2 * pi][:, 0:TT].rearrange("p (r c) -> p r c", c=rl)
                                a3 = acc[:, :].rearrange("p (r c) -> p r c", c=rl)
                                K.stt(a3[:, :, 1:rl], u3[:, :, 0:rl - 1], w0, a3[:, :, 1:rl], ALU.mult, ALU.add,
                                      [pr, acc, pvt], [acc])
                                K.stt(a3[:, :, 0:rl - 1], u3[:, :, 1:rl], w2, a3[:, :, 0:rl - 1], ALU.mult, ALU.add,
                                      [pr, acc, pvt], [acc])
                            return acc

                        def splits(a, b):
                            out = []
                            while a < b:
                                e = min(b, (a // 512 + 1) * 512)
                                out.append((a, e))
                                a = e
                            return out

                        for f in range(22):
                            pa_i = (2 * f) % 3
                            pl_i = (2 * f + 1) % 3
                            pra = umm(f, pa_i)
                            prl = umm(22 + f, pl_i)
                            ua = conv(pra, pa_i, f)
                            ul = conv(prl, pl_i, 22 + f)
                            sa = tmp.new()
                            K.act(sa[:, :], ua[:, :], AF.Silu, [ua], [sa])
                            K.tt("pool", qq[:, f, :], sa[:, :], ul[:, :], ALU.mult, [sa, ul], [qq])
                        for mo in range(8):
                            wd = wdr.new()
                            K.dma("sp", wd[:, :, :], WB["wdn"][l, mo], [], [wd])
                            pO = p4big.new()
                            for f in range(22):
                                K.mm(pO, pO.v(TT), wd[:, f, :], qq[:, f, :], [wd, qq], start=(f == 0), stop=(f == 21))
                            xo = tmp.new()
                            K.stt(xo[:, :], pO.v(TT), modc(l, 5, mo, ci), xm[:, mo, co:co + TT], ALU.mult, ALU.add,
                                  [pO, modt, xm], [xo])
                            K.dma("pool", XTA[mo * 128:(mo + 1) * 128, tk0:tk0 + TT], xo[:, :], [xo], [])
                            gdrip(rgen, 9)
                        gdrip(rgen, 10 ** 6)

        def phase5():
            with ExitStack() as pes:
                xT_r = K.ring("p5x", [128, 8, TT], F32, 1, pes)
                yT = K.sb("p5y", [128, 8, TT], F32, pes)
                sq = K.sb("p5sq", [128, 8, TT], BF16, pes)
                rb5 = rms_bufs("p5r_", pes)
                yo_r = K.ring("p5o", [128, D], F32, 2, pes)
                for tix in range(NTA // TT):
                    tk0 = tix * TT
                    xT = xT_r.new()
                    K.dma("sp", xT[:, :, :], XTA[:, tk0:tk0 + TT].rearrange("(kc p) t -> p kc t", p=128), [], [xT])
                    gdrip(rms_gen(xT, yT, TT, None, 0, 0, 0, rb5, sq), 10 ** 6)
                    for blk in range(TT // 128):
                        yo = yo_r.new()
                        for half in range(2):
                            pr = pbig.new()
                            for q in range(4):
                                kc = half * 4 + q
                                K.I("pe", [yT, ident], [pr],
                                    lambda e, kc=kc, q=q, pr=pr, blk=blk: e.transpose(
                                        out=pr.v(128, o=q * 128), in_=yT[:, kc, blk * 128:(blk + 1) * 128],
                                        identity=ident[:, :]))
                            K.cp("act" if half else "dve", yo[:, half * 512:(half + 1) * 512], pr.v(512), [pr], [yo])
                        K.dma("pool", yout[tk0 + blk * 128:tk0 + (blk + 1) * 128, :], yo[:, :], [yo], [])

        ph = phases or ("w", "0", "1", "2", "3", "4", "5")
        if "w" in ph:
            phase_w()
            K.barrier()
        else:
            with ExitStack() as pes:
                ff = K.sb("dbgfill", [128, 1024], F32, pes)
                fb = K.sb("dbgfillb", [128, 2816], BF16, pes)
                memset(ff, ff[:, :], -0.05)
                memset(fb, fb[:, :], 0.05)
                for t5 in range(NTA // 1024):
                    cs = slice(t5 * 1024, (t5 + 1) * 1024)
                    for m in range(8):
                        rs_ = slice(m * 128, (m + 1) * 128)
                        for dst in (ZR[rs_, cs], ZKK[rs_, cs], ZKD[0, rs_, cs], ZKD[1, rs_, cs], ZBA[0, rs_, cs],
                                    ZBA[1, rs_, cs], ZLW[0, rs_, cs], ZLW[1, rs_, cs]):
                            K.dma("sp", dst, ff[:, :], [ff], [])
                        K.dma("sp", ZV[rs_, cs], fb[:, 0:1024], [fb], [])
                        K.dma("sp", XTA[rs_, cs], ff[:, :], [ff], [])
                        K.dma("sp", XTB[rs_, cs], ff[:, :], [ff], [])
                        K.dma("sp", YTD[0, rs_, cs], ff[:, :], [ff], [])
                        K.dma("sp", YTD[1, rs_, cs], ff[:, :], [ff], [])
                        for dst in (ZBON[rs_, cs], ZG[rs_, cs], ZOB[rs_, cs], ZGAB[rs_, cs], ZGAB[1024 + m * 128:1152 + m * 128, cs]):
                            K.dma("sp", dst, fb[:, 0:1024], [fb], [])
                for nm_, _, Kd_, Nd_ in WSPEC:
                    for j in range(Nd_ // 128):
                        K.dma("sp", WB[nm_][0, j], fb[:, 0:Kd_].rearrange("p (k c) -> p k c", c=128), [fb], [])
                K.barrier()
            for b in (modt, a1t, a2t):
                memset(b, b[:, :, :, :], 0.5)
            for b in smallw.values():
                memset(b, b[:, :, :], 0.01)
            K.barrier()
        if "0" in ph:
            phase0()
            K.barrier()
        for l in range(nlayers):
            for gi in groups:
                g = GROUPS[gi]
                for nm, fn in (("1", phase1), ("2", phase2), ("3", phase3), ("4", phase4)):
                    if nm in ph:
                        fn(l, g)
                        K.barrier()
        if "5" in ph:
            phase5()
        K.barrier()
        build_nc.nins = K.nins
    return nc


DEBUG_OUT = set()


def _pvec(v):
    v = np.asarray(v, np.float32).reshape(-1)
    return v.reshape(-1, 128).T


def make_inputs(core, inp):
    b = core % 2
    xin = np.concatenate([np.asarray(inp["x_prompt"][4 * core:4 * core + 4]).reshape(NPT, D),
                          np.asarray(inp["x_sample"][b]).reshape(NST, D)], 0)
    cond = np.stack([_pvec(inp["c_ctx"]), _pvec(inp["c"][b])], -1)
    pv = np.zeros((L, 128, NPV), np.float32)
    for l in range(L):
        def put(name, v):
            a = _pvec(v)
            pv[l, :, PV[name]:PV[name] + a.shape[1]] = a
        put("n1g", inp["norm1_g"][l])
        put("n2g", inp["norm2_g"][l])
        put("w0", inp["decay_w0"][l])
        put("a0", inp["iclr_a0"][l])
        put("kk", inp["k_k"][l])
        put("ka", inp["k_a"][l])
        put("rk", inp["r_k"][l])
        put("gnw", inp["gn_w"][l])
        put("gnb", inp["gn_b"][l])
        put("cmw", inp["conv_mix_w"][l])
        put("cmb", inp["conv_mix_b"][l])
        put("cfw", inp["conv_ffn_w"][l])
        put("cfb", inp["conv_ffn_b"][l])
        put("bmod", inp["b_mod"][l])
    m = {
        "xin": np.ascontiguousarray(xin, np.float32),
        "st_in": np.ascontiguousarray(inp["state_wkv"][b], np.float32),
        "cond": np.ascontiguousarray(cond, np.float32),
        "pv": pv,
        "nfg": np.ascontiguousarray(_pvec(inp["norm_f_g"]), np.float32),
        "w_mod": inp["w_mod"], "w_in": inp["w_in"],
        "decay_w2": np.asarray(inp["decay_w2"]).reshape(L, 128, D),
        "iclr_a2": np.asarray(inp["iclr_a2"]).reshape(L, 128, D),
        "gate_g2": inp["gate_g2"], "w_pa": inp["w_pa"], "w_pb": inp["w_pb"], "w_o": inp["w_o"],
        "w_up": inp["w_up"], "w_down": inp["w_down"],
    }
    return {k: np.ascontiguousarray(np.asarray(v, np.float32)) for k, v in m.items()}


def kernel(**inputs):
    inp = {k: np.asarray(v) for k, v in inputs.items()}
    nc = build_nc()
    in_maps = [make_inputs(c, inp) for c in range(8)]
    res = run_bass_kernel_spmd(nc, in_maps, core_ids=list(range(8)))
    rs = res.results
    y_prompt = np.concatenate([np.asarray(rs[c]["yout"])[0:NPT].reshape(4, 256, D) for c in range(8)], 0)
    y_sample = np.stack([np.asarray(rs[b]["yout"])[NPT:NTA] for b in range(2)], 0)
    new_state = np.concatenate([np.asarray(rs[c]["nsout"]) for c in range(8)], 0)
    return (y_prompt.astype(np.float32), y_sample.astype(np.float32), new_state.astype(np.float32))
```

```python
import os
import numpy as np
from contextlib import ExitStack
import concourse.bass as bass
import concourse.mybir as mybir
from concourse.bass_utils import run_bass_kernel_spmd

F32 = mybir.dt.float32
BF16 = mybir.dt.bfloat16
AF = mybir.ActivationFunctionType
ALU = mybir.AluOpType

D = 1024
L = 2
NH = 16
HD = 64
DFF = 2816
NIN = 8576
NPT = 1024
NST = 4096
NTA = NPT + NST
TT = 512
C = 128
EPOCH = 8000
NSLOT = 20
SAME_SYNC = True
WD = BF16

PV = {}
_o = 0
for _n, _c in [("n1g", 8), ("n2g", 8), ("w0", 16), ("a0", 16), ("kk", 8), ("ka", 8), ("rk", 8), ("gnw", 8),
               ("gnb", 8), ("cmw", 24), ("cmb", 8), ("cfw", 132), ("cfb", 44), ("bmod", 48)]:
    PV[_n] = _o
    _o += _c
NPV = _o


class Buf:
    def __init__(self, t):
        self.t = t
        self.w = None
        self.r = {}

    def __getitem__(self, i):
        return self.t[i]


class PReg(Buf):
    def __init__(self, t, c0, n, bank=None):
        self.t = t
        self.bank = bank if bank is not None else Buf(t)
        self.c0 = c0
        self.n = n

    w = property(lambda self: self.bank.w, lambda self, v: setattr(self.bank, "w", v))
    r = property(lambda self: self.bank.r, lambda self, v: setattr(self.bank, "r", v))

    def v(self, n=None, p0=0, p1=128, o=0):
        n = self.n - o if n is None else n
        return self.t[p0:p1, self.c0 + o:self.c0 + o + n]


class Ring:
    def __init__(self, bufs):
        self.bufs = bufs
        self.i = 0

    def new(self):
        b = self.bufs[self.i % len(self.bufs)]
        self.i += 1
        return b


class KB:
    def __init__(self, nc, es):
        self.nc = nc
        self.es = es
        self.E = {"pe": nc.tensor, "act": nc.scalar, "dve": nc.vector, "pool": nc.gpsimd, "sp": nc.sync}
        self.comp = ["pe", "act", "dve", "pool"]
        self.cnt = {e: 0 for e in self.comp}
        self.esem = {e: [] for e in self.comp}
        self.waited = {e: {} for e in self.E}
        self.dsem = {}
        self.duse = {}
        self.dnext = {"sp": 0, "pool": 0}
        for q in ("sp", "pool"):
            for i in range(NSLOT):
                self.dsem[(q, i)] = es.enter_context(nc.semaphore("d%s%d" % (q, i)))
                self.duse[(q, i)] = 0
        self.barsem = es.enter_context(nc.semaphore("bar"))
        self.bar = 0
        self.bufs = []
        self.nins = 0

    def sb(self, name, shape, dt, es=None):
        self.uid = getattr(self, "uid", 0) + 1
        t = (es or self.es).enter_context(self.nc.sbuf_tensor("s%d_%s" % (self.uid, name), list(shape), dt))
        b = Buf(t)
        self.bufs.append(b)
        return b

    def ring(self, name, shape, dt, n, es=None):
        return Ring([self.sb("%s%d" % (name, i), shape, dt, es) for i in range(n)])

    def reg(self, b):
        self.bufs.append(b)
        return b

    def _sem(self, e, ep):
        while len(self.esem[e]) <= ep:
            self.esem[e].append(self.es.enter_context(self.nc.semaphore("s%s%d" % (e, len(self.esem[e])))))
        return self.esem[e][ep]

    def _wait(self, e, tok):
        kind, key, val = tok
        if kind == "eng":
            if key == e and (e == "pe" or not SAME_SYNC):
                return
            if self.waited[e].get(key, 0) >= val:
                return
            self.waited[e][key] = val
            ep = (val - 1) // EPOCH
            self.E[e].wait_ge(self._sem(key, ep), (val - 1) % EPOCH + 1)
        else:
            if self.waited[e].get(key, 0) >= val:
                return
            self.waited[e][key] = val
            self.E[e].wait_ge(self.dsem[key], val)

    def _deps(self, e, reads, writes):
        for b in reads:
            if b.w is not None:
                self._wait(e, b.w)
        for b in writes:
            if b.w is not None:
                self._wait(e, b.w)
            for tk in b.r.values():
                self._wait(e, tk)

    def _mark(self, tok, reads, writes):
        k = tok[1]
        for b in reads:
            b.r[k] = tok
        for b in writes:
            b.w = tok
            b.r = {}

    def I(self, e, reads, writes, fn):
        self._deps(e, reads, writes)
        n = self.cnt[e]
        sem = self._sem(e, n // EPOCH)
        fn(self.E[e]).then_inc(sem, 1)
        self.cnt[e] = n + 1
        self._mark(("eng", e, n + 1), reads, writes)
        self.nins += 1

    def dma(self, q, out, in_, reads, writes):
        self._deps(q, reads, writes)
        key = (q, self.dnext[q] % NSLOT)
        self.dnext[q] += 1
        u = self.duse[key]
        if u > 0:
            self._wait(q, ("dma", key, 16 * u))
        self.E[q].dma_start(out=out, in_=in_).then_inc(self.dsem[key], 16)
        self.duse[key] = u + 1
        self._mark(("dma", key, 16 * (u + 1)), reads, writes)
        self.nins += 1

    def barrier(self):
        for e in self.comp:
            if self.cnt[e] > 0:
                self._wait("sp", ("eng", e, self.cnt[e]))
        for key, u in self.duse.items():
            if u > 0:
                self._wait("sp", ("dma", key, 16 * u))
        self.bar += 1
        self.nc.sync.sem_inc(self.barsem, 1)
        for e in self.comp:
            self.E[e].wait_ge(self.barsem, self.bar)
        for e in self.E:
            for f in self.comp:
                self.waited[e][f] = self.cnt[f]
            for key, u in self.duse.items():
                self.waited[e][key] = 16 * u
        for b in self.bufs:
            b.w = None
            b.r = {}

    def mm(self, pr, out, lhsT, rhs, reads, start=True, stop=True):
        self.I("pe", reads, [pr], lambda e: e.matmul(out, lhsT=lhsT, rhs=rhs, start=start, stop=stop))

    def act(self, out, in_, func, reads, writes, bias=None, scale=None):
        kw = {}
        if bias is not None:
            kw["bias"] = bias
        if scale is not None:
            kw["scale"] = scale
        self.I("act", reads, writes, lambda e: e.activation(out=out, in_=in_, func=func, **kw))

    def tt(self, eng, out, in0, in1, op, reads, writes):
        self.I(eng, reads, writes, lambda e: e.tensor_tensor(out=out, in0=in0, in1=in1, op=op))

    def ts(self, eng, out, in0, s1, op0, reads, writes, s2=None, op1=None):
        if op1 is None:
            self.I(eng, reads, writes, lambda e: e.tensor_scalar(out=out, in0=in0, scalar1=s1, scalar2=None, op0=op0))
        else:
            self.I(eng, reads, writes,
                   lambda e: e.tensor_scalar(out=out, in0=in0, scalar1=s1, scalar2=s2, op0=op0, op1=op1))

    def stt(self, out, in0, scalar, in1, op0, op1, reads, writes):
        self.I("dve", reads, writes,
               lambda e: e.scalar_tensor_tensor(out=out, in0=in0, scalar=scalar, in1=in1, op0=op0, op1=op1))

    def cp(self, eng, out, in_, reads, writes):
        if eng == "act":
            self.act(out, in_, AF.Copy, reads, writes)
        else:
            self.I(eng, reads, writes, lambda e: e.tensor_copy(out=out, in_=in_))


GROUPS = [
    dict(t0=0, nt=NPT, seqs=[(256 * s, 256) for s in range(4)], grid=False, cond=0, rl=256, H=0, sh=1),
    dict(t0=NPT, nt=NST, seqs=[(NPT, NST)], grid=True, cond=1, rl=64, H=64, sh=64),
]


def build_nc(debug=False, nlayers=L, groups=(0, 1), phases=None):
    nc = bass.Bass("TRN2", target_bir_lowering=False)
    dbg = {}

    def din(name, shape, dt=F32):
        return nc.dram_tensor(name, list(shape), dt, kind="ExternalInput").ap()

    def dscr(name, shape, dt=F32):
        kind = "ExternalOutput" if (debug and name in DEBUG_OUT) else "Internal"
        return nc.dram_tensor(name, list(shape), dt, kind=kind).ap()

    xin = din("xin", [NTA, D])
    st_in = din("st_in", [L, 2, NH, HD, HD])
    cond_in = din("cond", [128, 8, 2])
    pv_in = din("pv", [L, 128, NPV])
    nfg_in = din("nfg", [128, 8])
    tinyw = phases is not None and "w" not in phases
    if tinyw:
        w_mod = w_in = w_pa = w_pb = w_o = w_up = w_dn = None
    else:
        w_mod = din("w_mod", [L, D, 6 * D])
        w_in = din("w_in", [L, D, NIN])
    dw2 = din("decay_w2", [L, 128, D])
    ia2 = din("iclr_a2", [L, 128, D])
    gg2 = din("gate_g2", [L, 128, D])
    if not tinyw:
        w_pa = din("w_pa", [L, D, D])
        w_pb = din("w_pb", [L, D, D])
        w_o = din("w_o", [L, D, D])
        w_up = din("w_up", [L, D, 2 * DFF])
        w_dn = din("w_down", [L, DFF, D])

    yout = nc.dram_tensor("yout", [NTA, D], F32, kind="ExternalOutput").ap()
    nsout = nc.dram_tensor("nsout", [4, L, 2, NH, HD, HD], F32, kind="ExternalOutput").ap()

    WSPEC = [("win", w_in, D, NIN), ("wpa", w_pa, D, D), ("wpb", w_pb, D, D), ("wo", w_o, D, D),
             ("wup", w_up, D, 2 * DFF), ("wdn", w_dn, DFF, D)]
    WB = {}
    for nm, _, Kd, Nd in WSPEC:
        WB[nm] = dscr("wb_" + nm, [L, Nd // 128, 128, Kd // 128, 128], BF16)
    XTA = dscr("XTA", [D, NTA])
    XTB = dscr("XTB", [D, NTA])
    ZR = dscr("ZR", [D, NTA])
    ZKK = dscr("ZKK", [D, NTA])
    ZKD = dscr("ZKD", [2, D, NTA])
    ZBA = dscr("ZBA", [2, D, NTA])
    ZLW = dscr("ZLW", [2, D, NTA])
    ZV = dscr("ZV", [D, NTA], BF16)
    ZBON = dscr("ZBON", [D, NTA], BF16)
    ZG = dscr("ZG", [D, NTA], BF16)
    ZOB = dscr("ZOB", [D, NTA], BF16)
    ZGAB = dscr("ZGAB", [2 * D, NTA], BF16)
    YTD = dscr("YTD", [2, D, NTA])

    with ExitStack() as es:
        K = KB(nc, es)
        ident = K.sb("ident", [128, 128], F32)
        identb = K.sb("identb", [128, 128], BF16)
        onesb = K.sb("onesb", [128, 128], BF16)
        oblk = K.sb("oblk", [128, 128], BF16)
        oblk64 = K.sb("oblk64", [128, 128], BF16)
        maskf = K.sb("maskf", [128, 256], BF16)
        maskb = K.sb("maskb", [128, 256], BF16)
        maskf2 = K.sb("maskf2", [128, 512], BF16)
        maskb2 = K.sb("maskb2", [128, 512], BF16)
        mlow4 = K.sb("mlow4", [128, 512], BF16)
        mupp4 = K.sb("mupp4", [128, 512], BF16)
        identw4 = K.sb("identw4", [128, 512], WD)
        rstf = K.sb("rstf", [128, 1024], F32)
        rstb = K.sb("rstb", [128, 1024], F32)
        epsn = K.sb("epsn", [128, 1], F32)
        epsg = K.sb("epsg", [128, 1], F32)
        pvt = K.sb("pvt", [128, L, NPV], F32)
        omka = K.sb("omka", [128, L, 8], F32)
        nfg = K.sb("nfg", [128, 8], F32)
        condt = K.sb("condt", [128, 8, 2], F32)
        sct = K.sb("sct", [128, 8, 2], F32)
        modt = K.sb("modt", [128, L, 48, 2], F32)
        a1t = K.sb("a1t", [128, L, 8, 2], F32)
        a2t = K.sb("a2t", [128, L, 8, 2], F32)
        smallw = {}
        for nm in ("dw2", "ia2", "gg2"):
            smallw[nm] = K.sb("sw_" + nm, [128, L, D], BF16)
        PS = [K.es.enter_context(nc.psum_tensor("ps%d" % i, [128, 512], F32)) for i in range(8)]

        def memset(b, ap, v):
            K.I("pool", [], [b], lambda e: e.memset(ap, v))

        def asel(b, ap, step, cm, op, fill, base=0, n=128):
            K.I("pool", [b], [b], lambda e: e.affine_select(out=ap, in_=ap, pattern=[[step, n]], compare_op=op,
                                                            fill=fill, base=base, channel_multiplier=cm))

        memset(ident, ident[:, :], 0.0)
        asel(ident, ident[:, :], -1, 1, ALU.not_equal, 1.0)
        K.cp("pool", identb[:, :], ident[:, :], [ident], [identb])
        memset(onesb, onesb[:, :], 1.0)
        memset(oblk, oblk[:, :], 0.0)
        memset(oblk, oblk[0:64, 0:64], 1.0)
        memset(oblk, oblk[64:128, 64:128], 1.0)
        memset(oblk64, oblk64[:, :], 0.0)
        memset(oblk64, oblk64[0:64, 0:64], 1.0 / 64)
        memset(oblk64, oblk64[64:128, 64:128], 1.0 / 64)
        memset(maskf, maskf[:, :], 1.0)
        memset(maskb, maskb[:, :], 1.0)
        asel(maskf, maskf[:, 0:128], 1, -1, ALU.is_gt, 0.0)
        asel(maskf, maskf[:, 128:256], 1, -1, ALU.is_ge, 0.0)
        asel(maskb, maskb[:, 0:128], -1, 1, ALU.is_gt, 0.0)
        asel(maskb, maskb[:, 128:256], -1, 1, ALU.is_ge, 0.0)
        for r2 in range(2):
            K.cp("pool", maskf2[:, 256 * r2:256 * r2 + 256], maskf[:, :], [maskf], [maskf2])
            K.cp("pool", maskb2[:, 256 * r2:256 * r2 + 256], maskb[:, :], [maskb], [maskb2])
        for r4 in range(4):
            K.cp("pool", mlow4[:, 128 * r4:128 * r4 + 128], maskb[:, 0:128], [maskb], [mlow4])
            K.cp("pool", mupp4[:, 128 * r4:128 * r4 + 128], maskf[:, 0:128], [maskf], [mupp4])
            K.cp("pool", identw4[:, 128 * r4:128 * r4 + 128], ident[:, :], [ident], [identw4])
        memset(rstf, rstf[:, :], 1.0)
        memset(rstb, rstb[:, :], 1.0)
        memset(rstf, rstf[:, :].rearrange("p (m t) -> p m t", t=128)[:, :, 0:1], 0.0)
        memset(rstb, rstb[:, :].rearrange("p (m t) -> p m t", t=128)[:, :, 127:128], 0.0)
        memset(epsn, epsn[:, :], 1e-6)
        memset(epsg, epsg[:, :], 64e-5)
        K.dma("sp", pvt[:, :, :], pv_in.rearrange("l p n -> p l n"), [], [pvt])
        K.dma("sp", nfg[:, :], nfg_in, [], [nfg])
        K.dma("sp", condt[:, :, :], cond_in, [], [condt])
        for l in range(L):
            K.ts("dve", omka[:, l, :], pvt[:, l, PV["ka"]:PV["ka"] + 8], -1.0, ALU.mult, [pvt], [omka], s2=1.0,
                 op1=ALU.add)
        K.act(sct[:, :, :], condt[:, :, :], AF.Silu, [condt], [sct])

        def pvc(l, name, j):
            o = PV[name] + j
            return pvt[:, l, o:o + 1]

        def phase_w():
            with ExitStack() as pes:
                stg = K.ring("wstg", [128, 2048], F32, 3, pes)
                stb = K.ring("wstb", [128, 2048], BF16, 3, pes)
                ci = 0
                for nm, wap, Kd, Nd in WSPEC:
                    for l in range(nlayers):
                        for kc in range(Kd // 128):
                            for c0 in range(0, Nd, 2048):
                                cw = min(2048, Nd - c0)
                                s = stg.new()
                                b = stb.new()
                                K.dma("sp", s[:, 0:cw], wap[l, kc * 128:(kc + 1) * 128, c0:c0 + cw], [], [s])
                                K.cp(("dve", "pool", "act")[ci % 3], b[:, 0:cw], s[:, 0:cw], [s], [b])
                                ci += 1
                                K.dma("pool", WB[nm][l, c0 // 128:(c0 + cw) // 128, :, kc, :].rearrange("j p c -> p j c"),
                                      b[:, 0:cw].rearrange("p (j c) -> p j c", c=128), [b], [])
                for nm, wap in (("dw2", dw2), ("ia2", ia2), ("gg2", gg2)):
                    for l in range(L):
                        s = stg.new()
                        K.dma("sp", s[:, 0:D], wap[l], [], [s])
                        K.cp("dve", smallw[nm][:, l, :], s[:, 0:D], [s], [smallw[nm]])
                wm = K.ring("wmod", [128, 8, 128], F32, 3, pes)
                pi = 0
                for l in range(nlayers):
                    for j in range(48):
                        w = wm.new()
                        K.dma("sp", w[:, :, :], w_mod[l, :, j * 128:(j + 1) * 128].rearrange("(kc p) c -> p kc c", p=128),
                              [], [w])
                        pr = preg_big[pi % 8]
                        pi += 1
                        for kc in range(8):
                            K.mm(pr, pr.v(2), w[:, kc, :], sct[:, kc, :], [w, sct], start=(kc == 0), stop=(kc == 7))
                        K.ts("dve", modt[:, l, j, :], pr.v(2), pvc(l, "bmod", j), ALU.add, [pr, pvt], [modt])
                    for kc in range(8):
                        K.ts("dve", a1t[:, l, kc, :], modt[:, l, 8 + kc, :], 1.0, ALU.add, [modt, pvt], [a1t],
                             s2=pvc(l, "n1g", kc), op1=ALU.mult)
                        K.ts("dve", a2t[:, l, kc, :], modt[:, l, 32 + kc, :], 1.0, ALU.add, [modt, pvt], [a2t],
                             s2=pvc(l, "n2g", kc), op1=ALU.mult)

        preg_big = [K.reg(PReg(PS[i], 0, 512)) for i in range(8)]
        preg_half = [K.reg(PReg(PS[i % 8], 256 * (i // 8), 256, bank=preg_big[i % 8].bank)) for i in range(16)]
        pbig = Ring(preg_big)
        phalf = Ring(preg_half)

        def modc(l, which, kc, ci):
            return modt[:, l, which * 8 + kc, ci:ci + 1]

        def phase0():
            with ExitStack() as pes:
                xin_r = K.ring("xin", [128, D], F32, 2, pes)
                xo_r = K.ring("xo", [128, 8, 128], F32, 2, pes)
                for blk in range(NTA // 128):
                    t0 = blk * 128
                    xi = xin_r.new()
                    K.dma("sp", xi[:, :], xin[t0:t0 + 128, :], [], [xi])
                    xo = xo_r.new()
                    for half in range(2):
                        pr = pbig.new()
                        for q in range(4):
                            kc = half * 4 + q
                            K.I("pe", [xi, ident], [pr],
                                lambda e, kc=kc, q=q, pr=pr, xi=xi: e.transpose(out=pr.v(128, o=q * 128),
                                                                               in_=xi[:, kc * 128:(kc + 1) * 128],
                                                                               identity=ident[:, :]))
                        K.cp("act" if half else "dve", xo[:, half * 4:half * 4 + 4, :],
                             pr.v(512).rearrange("p (k t) -> p k t", t=128), [pr], [xo])
                    K.dma("pool", XTA[:, t0:t0 + 128].rearrange("(kc p) t -> p kc t", p=128), xo[:, :, :], [xo], [])

        def rms_gen(xT, hT, nt, a_t, l, shwhich, ci, rb, sq, pring=None):
            pring = pring or pbig
            for c0 in range(0, nt, 512):
                cw = min(512, nt - c0)
                for kc in range(8):
                    K.act(sq[:, kc, 0:cw], xT[:, kc, c0:c0 + cw], AF.Square, [xT], [sq])
                    yield
                pr = pring.new()
                for kc in range(8):
                    K.mm(pr, pr.v(cw), onesb[:, :], sq[:, kc, 0:cw], [onesb, sq], start=(kc == 0), stop=(kc == 7))
                yield
                sd = rb["sd"]
                K.act(sd[:, 0:cw], pr.v(cw), AF.Sqrt, [pr, epsn], [sd], bias=epsn[:, 0:1], scale=1.0 / D)
                yield
                rs = rb["rs"]
                K.I("dve", [sd], [rs], lambda e, rs=rs, sd=sd, cw=cw: e.reciprocal(out=rs[:, 0:cw], in_=sd[:, 0:cw]))
                yield
                for kc in range(8):
                    t = rb["t"].new()
                    K.tt("dve" if kc % 2 else "pool", t[:, 0:cw], xT[:, kc, c0:c0 + cw], rs[:, 0:cw], ALU.mult,
                         [xT, rs], [t])
                    yield
                    if a_t is None:
                        K.ts("dve", hT[:, kc, c0:c0 + cw], t[:, 0:cw], nfg[:, kc:kc + 1], ALU.mult, [t, nfg], [hT])
                    else:
                        K.ts("dve", hT[:, kc, c0:c0 + cw], t[:, 0:cw], a_t[:, l, kc, ci:ci + 1], ALU.mult,
                             [t, a_t, modt], [hT], s2=modc(l, shwhich, kc, ci), op1=ALU.add)
                    yield

        def rms_bufs(name, pes):
            return dict(sd=K.sb(name + "sd", [128, 512], F32, pes), rs=K.sb(name + "rs", [128, 512], F32, pes),
                        t=K.ring(name + "t", [128, 512], F32, 2, pes))

        def gdrip(gen, n):
            if gen is None:
                return
            for _ in range(n):
                try:
                    next(gen)
                except StopIteration:
                    return

        def phase1(l, g):
            ci = g["cond"]
            rl = g["rl"]
            with ExitStack() as pes:
                xT_r = K.ring("p1x", [128, 8, TT], F32, 2, pes)
                hT2 = [K.sb("p1h%d" % i, [128, 8, TT], BF16, pes) for i in range(2)]
                rb = rms_bufs("p1r_", pes)
                sq = K.sb("p1sq", [128, 8, TT], BF16, pes)
                wr = K.ring("p1w", [128, 8, 128], BF16, 6, pes)
                tmp = K.ring("p1t", [128, TT], F32, 14, pes)
                tmb = K.ring("p1tb", [128, TT], BF16, 6, pes)
                r32r = K.ring("p1r", [128, TT], F32, 2, pes)
                k32r = K.ring("p1k", [128, TT], F32, 2, pes)
                kqr = K.ring("p1kq", [128, TT], F32, 2, pes)
                v16r = K.ring("p1v", [128, TT], BF16, 2, pes)
                sqkr = K.ring("p1sqk", [128, TT], BF16, 2, pes)
                avr = K.ring("p1av", [128, TT], F32, 4, pes)
                rkdr = K.ring("p1rkd", [128, TT], BF16, 4, pes)
                tw = K.sb("p1tw", [128, TT], BF16, pes)
                ta = K.sb("p1ta", [128, TT], BF16, pes)
                tg = K.sb("p1tg", [128, TT], BF16, pes)
                order = [24, 25, 26]
                for m in range(8):
                    order += [m, 8 + m, 16 + m]
                for m in range(8):
                    order += [27 + m, 35 + m, 43 + m]
                order += list(range(51, 67))
                ntile = g["nt"] // TT
                xTs = {}

                def xload(tix):
                    if tix < ntile and tix not in xTs:
                        tk = g["t0"] + tix * TT
                        xT = xT_r.new()
                        K.dma("sp", xT[:, :, :], XTA[:, tk:tk + TT].rearrange("(kc p) t -> p kc t", p=128), [], [xT])
                        xTs[tix] = xT

                xload(0)
                gdrip(rms_gen(xTs.pop(0), hT2[0], TT, a1t, l, 0, ci, rb, sq), 10 ** 6)
                for tix in range(ntile):
                    tk0 = g["t0"] + tix * TT
                    hT = hT2[tix % 2]
                    xload(tix + 1)
                    rgen = None
                    if tix + 1 < ntile:
                        rgen = rms_gen(xTs.pop(tix + 1), hT2[(tix + 1) % 2], TT, a1t, l, 0, ci, rb, sq)
                    wl = {}
                    st = {"n": 0}

                    def wload(upto):
                        while st["n"] <= min(upto, len(order) - 1):
                            w = wr.new()
                            K.dma("sp", w[:, :, :], WB["win"][l, order[st["n"]]], [], [w])
                            wl[st["n"]] = w
                            st["n"] += 1

                    pos = {"i": 0}

                    def zmm():
                        i = pos["i"]
                        pos["i"] += 1
                        wload(i + 4)
                        w = wl.pop(i)
                        pr = pbig.new()
                        for kc in range(8):
                            K.mm(pr, pr.v(TT), w[:, kc, :], hT[:, kc, :], [w, hT], start=(kc == 0), stop=(kc == 7))
                        return pr

                    def store(dst, src_b, src_ap):
                        K.dma("pool", dst, src_ap, [src_b], [])

                    z = zmm()
                    K.act(tw[:, :], z.v(TT), AF.Tanh, [z], [tw])
                    z = zmm()
                    K.cp("dve", ta[:, :], z.v(TT), [z], [ta])
                    z = zmm()
                    K.act(tg[:, :], z.v(TT), AF.Sigmoid, [z], [tg])
                    cols = slice(tk0, tk0 + TT)

                    def part1(m):
                        rows = slice(m * 128, (m + 1) * 128)
                        zr = zmm()
                        zk = zmm()
                        zv = zmm()
                        r32 = r32r.new()
                        K.cp("act", r32[:, :], zr.v(TT), [zr], [r32])
                        store(ZR[rows, cols], r32, r32[:, :])
                        k32 = k32r.new()
                        K.cp("dve", k32[:, :], zk.v(TT), [zk], [k32])
                        kq = kqr.new()
                        K.ts("dve", kq[:, :], zk.v(TT), pvc(l, "kk", m), ALU.mult, [zk, pvt], [kq])
                        v16 = v16r.new()
                        K.cp("act", v16[:, :], zv.v(TT), [zv], [v16])
                        store(ZV[rows, cols], v16, v16[:, :])
                        sqk = sqkr.new()
                        K.act(sqk[:, :], kq[:, :], AF.Square, [kq], [sqk])
                        avs, rkds = [], []
                        pls = []
                        for d in range(2):
                            if os.environ.get('P1_Y') == '1':
                                pls.append((None, None))
                                continue
                            hp = slice(64 * d, 64 * d + 64)
                            pl = pbig.new()
                            K.mm(pl, pl.v(TT), smallw["dw2"][hp, l, m * 128:(m + 1) * 128], tw[hp, :],
                                 [smallw["dw2"], tw])
                            pa = pbig.new()
                            K.mm(pa, pa.v(TT), smallw["ia2"][hp, l, m * 128:(m + 1) * 128], ta[hp, :],
                                 [smallw["ia2"], ta])
                            pls.append((pl, pa))
                        pg = pbig.new()
                        K.mm(pg, pg.v(TT), smallw["gg2"][:, l, m * 128:(m + 1) * 128], tg[:, :], [smallw["gg2"], tg])
                        for d in range(2):
                            pl, pa = pls[d]
                            if os.environ.get('P1_Y') == '1':
                                hp = slice(64 * d, 64 * d + 64)
                                pl = pbig.new()
                                K.mm(pl, pl.v(TT), smallw["dw2"][hp, l, m * 128:(m + 1) * 128], tw[hp, :],
                                     [smallw["dw2"], tw])
                            sg = tmp.new()
                            K.act(sg[:, :], pl.v(TT), AF.Sigmoid, [pl, pvt], [sg], bias=pvc(l, "w0", d * 8 + m))
                            lw = tmp.new()
                            K.act(lw[:, :], sg[:, :], AF.Identity, [sg], [lw], scale=-0.6065306597126334)
                            store(ZLW[d, rows, cols], lw, lw[:, :])
                            if os.environ.get('P1_Y') == '1':
                                pa = pbig.new()
                                K.mm(pa, pa.v(TT), smallw["ia2"][hp, l, m * 128:(m + 1) * 128], ta[hp, :],
                                     [smallw["ia2"], ta])
                            av = avr.new()
                            K.act(av[:, :], pa.v(TT), AF.Sigmoid, [pa, pvt], [av], bias=pvc(l, "a0", d * 8 + m))
                            f = tmp.new()
                            K.ts("dve", f[:, :], av[:, :], pvc(l, "ka", m), ALU.mult, [av, pvt, omka], [f],
                                 s2=omka[:, l, m:m + 1], op1=ALU.add)
                            kd = tmp.new()
                            K.tt("dve", kd[:, :], k32[:, :], f[:, :], ALU.mult, [k32, f], [kd])
                            store(ZKD[d, rows, cols], kd, kd[:, :])
                            rkd = rkdr.new()
                            K.stt(rkd[:, :], r32[:, :], pvc(l, "rk", m), kd[:, :], ALU.mult, ALU.mult,
                                  [r32, kd, pvt], [rkd])
                            avs.append(av)
                            rkds.append(rkd)
                        gt = tmb.new()
                        K.cp("act", gt[:, :], pg.v(TT), [pg], [gt])
                        store(ZG[rows, cols], gt, gt[:, :])
                        return dict(m=m, kq=kq, sqk=sqk, v16=v16, avs=avs, rkds=rkds)

                    def part2(c):
                        m = c["m"]
                        rows = slice(m * 128, (m + 1) * 128)
                        pss = pbig.new()
                        K.mm(pss, pss.v(TT), oblk[:, :], c["sqk"][:, :], [oblk, c["sqk"]])
                        pbs = pbig.new()
                        for d in range(2):
                            K.mm(pbs, pbs.v(TT), oblk[:, :], c["rkds"][d][:, :], [oblk, c["rkds"][d]], start=(d == 0),
                                 stop=(d == 1))
                        nrm = tmp.new()
                        K.act(nrm[:, :], pss.v(TT), AF.Sqrt, [pss], [nrm])
                        K.ts("dve", nrm[:, :], nrm[:, :], 1e-12, ALU.max, [nrm], [nrm])
                        rinv = tmp.new()
                        K.I("dve", [nrm], [rinv], lambda e, a=rinv, b=nrm: e.reciprocal(out=a[:, :], in_=b[:, :]))
                        kk = tmp.new()
                        K.tt("dve", kk[:, :], c["kq"][:, :], rinv[:, :], ALU.mult, [c["kq"], rinv], [kk])
                        store(ZKK[rows, cols], kk, kk[:, :])
                        for d in range(2):
                            ba = tmp.new()
                            K.tt("dve" if d else "pool", ba[:, :], kk[:, :], c["avs"][d][:, :], ALU.mult,
                                 [kk, c["avs"][d]], [ba])
                            store(ZBA[d, rows, cols], ba, ba[:, :])
                        bon = tmb.new()
                        K.tt("dve", bon[:, :], pbs.v(TT), c["v16"][:, :], ALU.mult, [pbs, c["v16"]], [bon])
                        store(ZBON[rows, cols], bon, bon[:, :])

                    prev = None
                    for m in range(8):
                        cur = part1(m)
                        if os.environ.get('P1_X') == '1':
                            part2(cur)
                            continue
                        if prev is not None:
                            part2(prev)
                        prev = cur
                    first_conv = True
                    for m in range(8):
                        rows = slice(m * 128, (m + 1) * 128)
                        zcb = zmm()
                        zcc = zmm()
                        zcx = zmm()
                        if first_conv and prev is not None:
                            part2(prev)
                            first_conv = False
                        cxs = tmp.new()
                        K.cp("act", cxs[:, :], zcx.v(TT), [zcx], [cxs])
                        pp = tmp.new()
                        K.tt("dve", pp[:, :], zcc.v(TT), cxs[:, :], ALU.mult, [zcc, cxs], [pp])
                        acc = tmp.new()
                        K.ts("pool", acc[:, :], pp[:, :], pvc(l, "cmw", 8 + m), ALU.mult, [pp, pvt], [acc],
                             s2=pvc(l, "cmb", m), op1=ALU.add)
                        p3 = pp[:, :].rearrange("p (r c) -> p r c", c=rl)
                        a3 = acc[:, :].rearrange("p (r c) -> p r c", c=rl)
                        K.stt(a3[:, :, 1:rl], p3[:, :, 0:rl - 1], pvc(l, "cmw", m), a3[:, :, 1:rl], ALU.mult, ALU.add,
                              [pp, acc, pvt], [acc])
                        K.stt(a3[:, :, 0:rl - 1], p3[:, :, 1:rl], pvc(l, "cmw", 16 + m), a3[:, :, 0:rl - 1], ALU.mult,
                              ALU.add, [pp, acc, pvt], [acc])
                        ob = tmb.new()
                        K.tt("dve", ob[:, :], zcb.v(TT), acc[:, :], ALU.mult, [zcb, acc], [ob])
                        store(ZOB[rows, cols], ob, ob[:, :])
                    for j in range(16):
                        zg = zmm()
                        sgt = tmb.new()
                        K.act(sgt[:, :], zg.v(TT), AF.Sigmoid, [zg], [sgt])
                        store(ZGAB[j * 128:(j + 1) * 128, tk0:tk0 + TT], sgt, sgt[:, :])
                        gdrip(rgen, 3)
                    gdrip(rgen, 10 ** 6)

        def phase2(l, g):
            with ExitStack() as pes:
                ld = {}
                for nm, dt in (("R", F32), ("KK", F32), ("KD", F32), ("BA", F32), ("LW", F32), ("V", BF16)):
                    ld[nm] = [K.sb("p2%s%d" % (nm, d), [128, 1024], dt, pes) for d in range(2)]
                cum = [K.sb("p2cum%d" % d, [128, 1024], F32, pes) for d in range(2)]
                tmp = K.ring("p2t", [128, 1024], F32, 3, pes)
                AR = [K.sb("p2ar%d" % d, [128, 8, 2, 128], BF16, pes) for d in range(2)]
                BH = [K.sb("p2bh%d" % d, [128, 1024], BF16, pes) for d in range(2)]
                KH = [K.sb("p2kh%d" % d, [128, 1024], BF16, pes) for d in range(2)]
                GC = [K.sb("p2gc%d" % d, [128, 8], F32, pes) for d in range(2)]
                DG = [K.sb("p2dg%d" % d, [128, 8, 128], BF16, pes) for d in range(2)]
                VT = [K.sb("p2vt%d" % d, [128, 8, 128], BF16, pes) for d in range(2)]
                BT = [K.sb("p2bt%d" % d, [128, 8, 128], BF16, pes) for d in range(2)]
                KT = [K.sb("p2kt%d" % d, [128, 8, 128], BF16, pes) for d in range(2)]
                YT = [K.sb("p2yt%d" % d, [128, 8, 128], F32, pes) for d in range(2)]
                S32 = [K.sb("p2s32_%d" % d, [128, 8, 64], F32, pes) for d in range(2)]
                S16 = [K.sb("p2s16_%d" % d, [128, 8, 64], BF16, pes) for d in range(2)]
                LMK = [K.sb("p2lmk%d" % gq, [128, 4, 256], WD, pes) for gq in range(4)]
                LMB = [K.sb("p2lmb%d" % gq, [128, 4, 256], WD, pes) for gq in range(4)]
                LAB = [K.sb("p2lab%d" % gq, [128, 4, 128], WD, pes) for gq in range(4)]
                PP = [[K.sb("p2p%d_%d" % (gq, i), [128, 4, 128], WD, pes) for i in range(2)] for gq in range(4)]
                PT = [[K.sb("p2pt%d_%d" % (gq, i), [128, 4, 128], WD, pes) for i in range(2)] for gq in range(4)]
                TTt = [[K.sb("p2tt%d_%d" % (gq, i), [128, 4, 128], WD, pes) for i in range(2)] for gq in range(4)]
                XX = [K.sb("p2x%d" % u, [128, 8, 64], BF16, pes) for u in range(2)]
                UU = [K.sb("p2u%d" % u, [128, 8, 64], BF16, pes) for u in range(2)]
                sld = K.sb("p2sld", [64, NH, HD], F32, pes)
                sout = K.ring("p2so", [64, 128], F32, 2, pes)
                identw = identb if WD == BF16 else ident

                def v3(b):
                    return b[:, :].rearrange("p (m t) -> p m t", t=128)

                for si, (s0, slen) in enumerate(g["seqs"]):
                    nch = slen // C
                    for d in range(2):
                        if g["grid"]:
                            K.dma("sp", sld[:, :, :], st_in[l, d].rearrange("h i j -> i h j"), [], [sld])
                            for m in range(8):
                                pr = pbig.new()
                                K.I("pe", [sld, ident], [pr],
                                    lambda e, pr=pr, m=m: e.transpose(
                                        out=pr.v(64), in_=sld[:, 2 * m:2 * m + 2, :].rearrange("i h j -> i (h j)"),
                                        identity=ident[0:64, 0:64]))
                                K.cp("dve", S32[d][:, m, :], pr.v(64), [pr], [S32[d]])
                                K.cp("act", S16[d][:, m, :], pr.v(64), [pr], [S16[d]])
                        else:
                            memset(S32[d], S32[d][:, :, :], 0.0)
                            memset(S16[d], S16[d][:, :, :], 0.0)

                    def stageA(step, d):
                        cidx = step if d == 0 else nch - 1 - step
                        tk0 = s0 + cidx * C
                        for nm, src in (("R", ZR), ("KK", ZKK), ("KD", ZKD[d]), ("BA", ZBA[d]), ("LW", ZLW[d]),
                                        ("V", ZV)):
                            b = ld[nm][d]
                            K.dma("sp", v3(b), src[:, tk0:tk0 + C].rearrange("(m p) t -> p m t", p=128), [], [b])
                            yield
                        lw = ld["LW"][d]
                        cm = cum[d]
                        if d == 0:
                            K.I("dve", [rstf, lw], [cm],
                                lambda e, cm=cm, lw=lw: e.tensor_tensor_scan(out=cm[:, :], data0=rstf[:, :],
                                                                            data1=lw[:, :], initial=0.0,
                                                                            op0=ALU.mult, op1=ALU.add))
                            cend = v3(cm)[:, :, 127:128]
                        else:
                            K.I("dve", [rstb, lw], [cm],
                                lambda e, cm=cm, lw=lw: e.tensor_tensor_scan(out=cm[:, ::-1], data0=rstb[:, ::-1],
                                                                            data1=lw[:, ::-1], initial=0.0,
                                                                            op0=ALU.mult, op1=ALU.add))
                            cend = v3(cm)[:, :, 0:1]
                        yield
                        K.act(GC[d][:, :].rearrange("p (m o) -> p m o", o=1), cend, AF.Exp, [cm], [GC[d]])
                        yield
                        for m in range(8):
                            K.ts("dve", DG[d][:, m, :], identb[:, :], GC[d][:, m:m + 1], ALU.mult,
                                 [identb, GC[d]], [DG[d]])
                        yield
                        er = tmp.new()
                        K.act(er[:, :], cm[:, :], AF.Exp, [cm], [er])
                        yield
                        K.tt("dve", AR[d][:, :, 1, :], v3(ld["R"][d]), v3(er), ALU.mult, [ld["R"][d], er], [AR[d]])
                        yield
                        cml = tmp.new()
                        K.tt("pool", cml[:, :], cm[:, :], lw[:, :], ALU.subtract, [cm, lw], [cml])
                        yield
                        ea = tmp.new()
                        K.act(ea[:, :], cml[:, :], AF.Exp, [cml], [ea])
                        yield
                        K.stt(AR[d][:, :, 0, :], v3(ld["KK"][d]), -1.0, v3(ea), ALU.mult, ALU.mult,
                              [ld["KK"][d], ea], [AR[d]])
                        yield
                        en = tmp.new()
                        K.act(en[:, :], cm[:, :], AF.Exp, [cm], [en], scale=-1.0)
                        yield
                        K.tt("dve", BH[d][:, :], ld["BA"][d][:, :], en[:, :], ALU.mult, [ld["BA"][d], en], [BH[d]])
                        yield
                        K.tt("dve", KH[d][:, :], ld["KD"][d][:, :], en[:, :], ALU.mult, [ld["KD"][d], en], [KH[d]])
                        yield
                        for (srcb, dst, useid) in ((ld["V"][d], VT[d], True), (BH[d], BT[d], False),
                                                   (KH[d], KT[d], False)):
                            for m4 in range(2):
                                pr = pbig.new()
                                for q in range(4):
                                    m = m4 * 4 + q
                                    rhs = identb[:, :] if useid else DG[d][:, m, :]
                                    K.mm(pr, pr.v(128, o=q * 128), v3(srcb)[:, m, :], rhs,
                                         [srcb, identb if useid else DG[d]])
                                K.cp("act" if m4 % 2 else "dve", dst[:, 4 * m4:4 * m4 + 4, :],
                                     pr.v(512).rearrange("p (m c) -> p m c", c=128), [pr], [dst])
                                yield

                    def drip(gen, n):
                        if gen is None:
                            return
                        for _ in range(n):
                            try:
                                next(gen)
                            except StopIteration:
                                return

                    GH = [[8 * (gq // 2) + (gq % 2) + 2 * e4 for e4 in range(4)] for gq in range(4)]

                    def gof(h):
                        return 2 * (h // 8) + (h % 2), (h % 8) // 2

                    def stageBCD(step, d, gen):
                        cidx = step if d == 0 else nch - 1 - step
                        tk0 = s0 + cidx * C
                        if os.environ.get('WKV_STOP') == 'A':
                            return
                        mk = maskf2 if d == 0 else maskb2
                        mk2 = mlow4 if d == 0 else mupp4
                        wkvb = os.environ.get('WKV_B', 'klt')
                        for gq in range(4):
                            for hh in range(2 if 'k' in wkvb else 0):
                                for (srcK, dstT) in ((KH[d], LMK[gq]), (BH[d], LMB[gq])):
                                    pr = pbig.new()
                                    for e2 in range(2):
                                        h = GH[gq][2 * hh + e2]
                                        m = h // 2
                                        hp = slice(64 * (h % 2), 64 * (h % 2) + 64)
                                        arh = AR[d][hp, m, :, :].rearrange("p a t -> p (a t)")
                                        K.mm(pr, pr.v(256, o=256 * e2), v3(srcK)[hp, m, :], arh, [srcK, AR[d]])
                                    K.tt("dve", dstT[:, 2 * hh:2 * hh + 2, :],
                                         pr.v(512).rearrange("p (a c) -> p a c", c=256),
                                         mk[:, :].rearrange("p (a c) -> p a c", c=256), ALU.mult, [pr, mk], [dstT])
                            if 'l' in wkvb:
                                pr = pbig.new()
                                for e4 in range(4):
                                    h = GH[gq][e4]
                                    m = h // 2
                                    hp = slice(64 * (h % 2), 64 * (h % 2) + 64)
                                    K.mm(pr, pr.v(128, o=128 * e4), AR[d][hp, m, 0, :], v3(BH[d])[hp, m, :],
                                         [AR[d], BH[d]])
                                K.tt("dve", LAB[gq][:, :, :], pr.v(512).rearrange("p (a c) -> p a c", c=128),
                                     mk2[:, :].rearrange("p (a c) -> p a c", c=128), ALU.mult, [pr, mk2], [LAB[gq]])
                            if 't' in wkvb:
                                K.tt("pool", TTt[gq][0][:, :, :], identw4[:, :].rearrange("p (a c) -> p a c", c=128),
                                     LMB[gq][:, :, 0:128], ALU.add, [identw4, LMB[gq]], [TTt[gq][0]])
                        drip(gen, 3)
                        if os.environ.get('WKV_STOP') == 'B':
                            return
                        for k in range(6):
                            for gq in range(4):
                                pkb = LAB[gq] if k == 0 else PP[gq][(k - 1) % 2]
                                ptb = LMB[gq] if k == 0 else PT[gq][(k - 1) % 2]
                                pn = PP[gq][k % 2]
                                pr = pbig.new()
                                for e4 in range(4):
                                    ptk = ptb[:, e4, 0:128]
                                    K.mm(pr, pr.v(128, o=128 * e4), ptk, pkb[:, e4, :], [ptb, pkb])
                                K.cp("act", pn[:, :, :], pr.v(512).rearrange("p (a c) -> p a c", c=128), [pr], [pn])
                                if k < 5:
                                    ptn = PT[gq][k % 2]
                                    pr2 = pbig.new()
                                    for e4 in range(4):
                                        ptk = ptb[:, e4, 0:128]
                                        K.mm(pr2, pr2.v(128, o=128 * e4), pkb[:, e4, :], ptk, [ptb, pkb])
                                    K.cp("act", ptn[:, :, :],
                                         pr2.v(512).rearrange("p (a c) -> p a c", c=128), [pr2], [ptn])
                            drip(gen, 3)
                            for gq in range(4):
                                pn = PP[gq][k % 2]
                                tcur = TTt[gq][k % 2]
                                tnx = TTt[gq][(k + 1) % 2]
                                pr3 = pbig.new()
                                for e4 in range(4):
                                    K.mm(pr3, pr3.v(128, o=128 * e4), pn[:, e4, :], tcur[:, e4, :], [pn, tcur])
                                K.tt("dve", tnx[:, :, :], pr3.v(512).rearrange("p (a c) -> p a c", c=128),
                                     tcur[:, :, :], ALU.add, [pr3, tcur], [tnx])
                            drip(gen, 3)
                        if os.environ.get('WKV_STOP') == 'C':
                            return
                        for h8 in range(2):
                            px = pbig.new()
                            for e8 in range(8):
                                h = 8 * h8 + e8
                                m, q = h // 2, h % 2
                                hp = slice(64 * q, 64 * q + 64)
                                vh = VT[d][:, m, 64 * q:64 * q + 64]
                                K.mm(px, px.v(64, o=64 * e8), AR[d][hp, m, 0, :], S16[d][hp, m, :], [AR[d], S16[d]],
                                     start=True, stop=False)
                                K.mm(px, px.v(64, o=64 * e8), LMK[gof(h)[0]][:, gof(h)[1], 0:128], vh, [LMK[gof(h)[0]], VT[d]],
                                     start=False, stop=True)
                            K.cp("act" if h8 else "dve", XX[h8][:, :, :], px.v(512).rearrange("p (a c) -> p a c", c=64),
                                 [px], [XX[h8]])
                        drip(gen, 4)
                        for h8 in range(2):
                            pu = pbig.new()
                            for e8 in range(8):
                                h = 8 * h8 + e8
                                tT = TTt[gof(h)[0]][0]
                                K.mm(pu, pu.v(64, o=64 * e8), tT[:, gof(h)[1], :], XX[h8][:, e8, :], [tT, XX[h8]])
                            K.cp("dve" if h8 else "act", UU[h8][:, :, :], pu.v(512).rearrange("p (a c) -> p a c", c=64),
                                 [pu], [UU[h8]])
                        drip(gen, 4)
                        if os.environ.get('WKV_STOP') == 'D2':
                            return
                        psn = pbig.new()
                        for m4 in range(2):
                            py = pbig.new()
                            for e4 in range(4):
                                m = 4 * m4 + e4
                                for q in range(2):
                                    h = 2 * m + q
                                    hp = slice(64 * q, 64 * q + 64)
                                    vh = VT[d][:, m, 64 * q:64 * q + 64]
                                    uh = UU[h // 8][:, h % 8, :]
                                    lmk = LMK[gof(h)[0]]
                                    lmb = LMB[gof(h)[0]]
                                    yo = py.v(128, hp.start, hp.stop, o=128 * e4)
                                    K.mm(py, yo, S16[d][hp, m, :], AR[d][hp, m, 1, :], [S16[d], AR[d]],
                                         start=True, stop=False)
                                    K.mm(py, yo, uh, lmb[:, gof(h)[1], 128:256], [UU[h // 8], lmb], start=False, stop=False)
                                    K.mm(py, yo, vh, lmk[:, gof(h)[1], 128:256], [VT[d], lmk], start=False, stop=True)
                                    so_ = psn.v(64, hp.start, hp.stop, o=64 * m)
                                    K.mm(psn, so_, BT[d][:, m, 64 * q:64 * q + 64], uh, [BT[d], UU[h // 8]],
                                         start=True, stop=False)
                                    K.mm(psn, so_, KT[d][:, m, 64 * q:64 * q + 64], vh, [KT[d], VT[d]],
                                         start=False, stop=True)
                            K.cp("act", YT[d][:, 4 * m4:4 * m4 + 4, :], py.v(512).rearrange("p (a c) -> p a c", c=128),
                                 [py], [YT[d]])
                        for m in range(8):
                            K.stt(S32[d][:, m, :], S32[d][:, m, :], GC[d][:, m:m + 1], psn.v(64, o=64 * m), ALU.mult,
                                  ALU.add, [S32[d], GC[d], psn], [S32[d]])
                        K.cp("act", S16[d][:, :, :], S32[d][:, :, :], [S32[d]], [S16[d]])
                        K.dma("pool", YTD[d, :, tk0:tk0 + C].rearrange("(m p) t -> p m t", p=128), YT[d][:, :, :],
                              [YT[d]], [])

                    g0 = stageA(0, 0)
                    drip(g0, 10 ** 6)
                    for step in range(nch):
                        for d in range(2):
                            if d == 0:
                                gen = stageA(step, 1)
                            elif step + 1 < nch:
                                gen = stageA(step + 1, 0)
                            else:
                                gen = None
                            stageBCD(step, d, gen)
                            drip(gen, 10 ** 6)
                    if not g["grid"]:
                        for d in range(2):
                            for m in range(8):
                                pr = pbig.new()
                                K.I("pe", [S32[d], ident], [pr],
                                    lambda e, pr=pr, d=d, m=m: e.transpose(out=pr.v(128, 0, 64), in_=S32[d][:, m, :],
                                                                           identity=ident[:, :]))
                                so = sout.new()
                                K.cp("dve", so[:, :], pr.v(128, 0, 64), [pr], [so])
                                K.dma("pool", nsout[si, l, d, 2 * m:2 * m + 2].rearrange("h i j -> i h j"),
                                      so[:, :].rearrange("i (h j) -> i h j", j=64), [so], [])

        def phase3(l, g):
            ci = g["cond"]
            with ExitStack() as pes:
                yf = [K.sb("p3yf%d" % m, [128, TT], F32, pes) for m in range(8)]
                yb = [K.sb("p3yb%d" % m, [128, TT], F32, pes) for m in range(8)]
                h16 = [K.sb("p3h%d" % m, [128, TT], BF16, pes) for m in range(8)]
                bon = K.sb("p3bon", [128, 8, TT], BF16, pes)
                gg = K.sb("p3g", [128, 8, TT], BF16, pes)
                ob = K.sb("p3ob", [128, 8, TT], BF16, pes)
                gab = K.sb("p3gab", [128, 16, TT], BF16, pes)
                xT = K.sb("p3x", [128, 8, TT], F32, pes)
                oa = K.sb("p3oa", [128, 8, TT], BF16, pes)
                mg = K.sb("p3mg", [128, 8, TT], BF16, pes)
                tmp = K.ring("p3t", [128, TT], F32, 6, pes)
                wr = K.ring("p3w", [128, 8, 128], BF16, 6, pes)
                ntile3 = g["nt"] // TT

                def ld3(b, src):
                    K.dma("sp", b[:, :, :], src.rearrange("(m p) t -> p m t", p=128), [], [b])

                def loads3(tix, which):
                    if tix >= ntile3:
                        return
                    cs_ = slice(g["t0"] + tix * TT, g["t0"] + (tix + 1) * TT)
                    if which == 0:
                        for m in range(8):
                            K.dma("sp", yf[m][:, :], YTD[0, m * 128:(m + 1) * 128, cs_], [], [yf[m]])
                            K.dma("sp", yb[m][:, :], YTD[1, m * 128:(m + 1) * 128, cs_], [], [yb[m]])
                        ld3(bon, ZBON[:, cs_])
                        ld3(gg, ZG[:, cs_])
                    elif which == 1:
                        ld3(ob, ZOB[:, cs_])
                    elif which == 2:
                        ld3(gab, ZGAB[:, cs_])
                    else:
                        ld3(xT, XTA[:, cs_])

                for w_ in range(4):
                    loads3(0, w_)
                for tix in range(ntile3):
                    tk0 = g["t0"] + tix * TT
                    cs = slice(tk0, tk0 + TT)
                    for m in range(8):
                        K.tt("pool", yf[m][:, :], yf[m][:, :], yb[m][:, :], ALU.add, [yf[m], yb[m]], [yf[m]])
                    for m in range(8):
                        K.cp("act", h16[m][:, :], yf[m][:, :], [yf[m]], [h16[m]])
                    pms = []
                    for m in range(8):
                        pm = pbig.new()
                        K.mm(pm, pm.v(TT), oblk64[:, :], h16[m][:, :], [oblk64, h16[m]])
                        pms.append(pm)
                    for m in range(8):
                        K.tt("dve", yf[m][:, :], yf[m][:, :], pms[m].v(TT), ALU.subtract, [yf[m], pms[m]], [yf[m]])
                    for m in range(8):
                        K.act(h16[m][:, :], yf[m][:, :], AF.Square, [yf[m]], [h16[m]])
                    pvs = []
                    for m in range(8):
                        pv_ = pbig.new()
                        K.mm(pv_, pv_.v(TT), oblk64[:, :], h16[m][:, :], [oblk64, h16[m]])
                        pvs.append(pv_)
                    for m in range(8):
                        K.act(yb[m][:, :], pvs[m].v(TT), AF.Sqrt, [pvs[m], epsg], [yb[m]], bias=epsg[:, 0:1])
                    for m in range(8):
                        K.I("dve", [yb[m]], [yb[m]], lambda e, b=yb[m]: e.reciprocal(out=b[:, :], in_=b[:, :]))
                    for m in range(8):
                        K.tt("dve", yf[m][:, :], yf[m][:, :], yb[m][:, :], ALU.mult, [yf[m], yb[m]], [yf[m]])
                    for m in range(8):
                        K.ts("pool", yf[m][:, :], yf[m][:, :], pvc(l, "gnw", m), ALU.mult, [yf[m], pvt], [yf[m]],
                             s2=pvc(l, "gnb", m), op1=ALU.add)
                    for m in range(8):
                        K.tt("dve" if m % 2 else "pool", yf[m][:, :], yf[m][:, :], bon[:, m, :], ALU.add,
                             [yf[m], bon], [yf[m]])
                    for m in range(8):
                        K.tt("dve", oa[:, m, :], yf[m][:, :], gg[:, m, :], ALU.mult, [yf[m], gg], [oa])
                    loads3(tix + 1, 0)
                    for mo in range(8):
                        wa = wr.new()
                        K.dma("sp", wa[:, :, :], WB["wpa"][l, mo], [], [wa])
                        wb_ = wr.new()
                        K.dma("sp", wb_[:, :, :], WB["wpb"][l, mo], [], [wb_])
                        pA = pbig.new()
                        pB = pbig.new()
                        for m in range(8):
                            K.mm(pA, pA.v(TT), wa[:, m, :], oa[:, m, :], [wa, oa], start=(m == 0), stop=(m == 7))
                        for m in range(8):
                            K.mm(pB, pB.v(TT), wb_[:, m, :], ob[:, m, :], [wb_, ob], start=(m == 0), stop=(m == 7))
                        t1 = tmp.new()
                        K.tt("dve", t1[:, :], pA.v(TT), gab[:, mo, :], ALU.mult, [pA, gab], [t1])
                        t2 = tmp.new()
                        K.tt("dve", t2[:, :], pB.v(TT), gab[:, 8 + mo, :], ALU.mult, [pB, gab], [t2])
                        K.tt("pool", mg[:, mo, :], t1[:, :], t2[:, :], ALU.add, [t1, t2], [mg])
                    loads3(tix + 1, 1)
                    loads3(tix + 1, 2)
                    for mo in range(8):
                        ww = wr.new()
                        K.dma("sp", ww[:, :, :], WB["wo"][l, mo], [], [ww])
                        pO = pbig.new()
                        for m in range(8):
                            K.mm(pO, pO.v(TT), ww[:, m, :], mg[:, m, :], [ww, mg], start=(m == 0), stop=(m == 7))
                        xm = tmp.new()
                        K.stt(xm[:, :], pO.v(TT), modc(l, 2, mo, ci), xT[:, mo, :], ALU.mult, ALU.add,
                              [pO, modt, xT], [xm])
                        K.dma("pool", XTB[mo * 128:(mo + 1) * 128, cs], xm[:, :], [xm], [])
                    loads3(tix + 1, 3)

        def phase4(l, g):
            ci = g["cond"]
            H = g["H"]
            sh = g["sh"]
            W = TT + 2 * H
            with ExitStack() as pes:
                xm_r = K.ring("p4x", [128, 8, W], F32, 2, pes)
                h2r = [K.sb("p4h%d" % i, [128, 8, W], BF16, pes) for i in range(2)]
                rb = rms_bufs("p4r_", pes)
                sq = K.sb("p4sq", [128, 8, 512], BF16, pes)
                qq = K.sb("p4q", [128, 22, TT], BF16, pes)
                tmp = K.ring("p4t", [128, TT], F32, 12, pes)
                wr = K.ring("p4w", [128, 8, 128], BF16, 6, pes)
                wdr = K.ring("p4wd", [128, 22, 128], BF16, 2, pes)
                pairs = [K.reg(PReg(PS[2 * i], 0, 1024)) for i in range(3)]
                p4big = Ring([preg_big[6], preg_big[7]])
                for (s0, slen) in ([(g["t0"], g["nt"])] if not g["grid"] else g["seqs"]):
                    xms = {}

                    def xmload(tix):
                        if tix >= slen // TT or tix in xms:
                            return
                        tk = s0 + tix * TT
                        lo_ = max(s0, tk - H)
                        hi_ = min(s0 + slen, tk + TT + H)
                        b = xm_r.new()
                        K.dma("sp", b[:, :, 0:hi_ - lo_], XTB[:, lo_:hi_].rearrange("(kc p) t -> p kc t", p=128), [], [b])
                        xms[tix] = b

                    def nof(tix):
                        tk = s0 + tix * TT
                        return min(s0 + slen, tk + TT + H) - max(s0, tk - H)

                    xmload(0)
                    gdrip(rms_gen(xms[0], h2r[0], nof(0), a2t, l, 3, ci, rb, sq, p4big), 10 ** 6)
                    for tix in range(slen // TT):
                        tk0 = s0 + tix * TT
                        lo = max(s0, tk0 - H)
                        hi = min(s0 + slen, tk0 + TT + H)
                        n = hi - lo
                        co = tk0 - lo
                        xm = xms.pop(tix)
                        h2 = h2r[tix % 2]
                        xmload(tix + 1)
                        rgen = None
                        if tix + 1 < slen // TT:
                            rgen = rms_gen(xms[tix + 1], h2r[(tix + 1) % 2], nof(tix + 1), a2t, l, 3, ci, rb, sq, p4big)

                        def umm(j, pi):
                            w = wr.new()
                            K.dma("sp", w[:, :, :], WB["wup"][l, j], [], [w])
                            pr = pairs[pi]
                            for c0 in range(0, n, 512):
                                cw = min(512, n - c0)
                                for kc in range(8):
                                    K.mm(pr, PS[2 * pi + c0 // 512][:, 0:cw], w[:, kc, :],
                                         h2[:, kc, c0:c0 + cw], [w, h2], start=(kc == 0), stop=(kc == 7))
                            return pr

                        def uap(pi, a, b):
                            bk = a // 512
                            assert (b - 1) // 512 == bk
                            return PS[2 * pi + bk][:, a - 512 * bk:b - 512 * bk]

                        def conv(pr, pi, j):
                            acc = tmp.new()
                            w0 = pvc(l, "cfw", j)
                            w1 = pvc(l, "cfw", 44 + j)
                            w2 = pvc(l, "cfw", 88 + j)
                            bb = pvc(l, "cfb", j)
                            for (a, b) in splits(co, co + TT):
                                K.ts("dve", acc[:, a - co:b - co], uap(pi, a, b), w1, ALU.mult, [pr, pvt], [acc],
                                     s2=bb, op1=ALU.add)
                            if g["grid"]:
                                a0 = max(tk0, s0 + sh)
                                b0 = min(tk0 + TT, s0 + slen - sh)
                                for (a, b) in splits(a0 - sh - lo, tk0 + TT - sh - lo):
                                    K.stt(acc[:, a + sh + lo - tk0:b + sh + lo - tk0], uap(pi, a, b), w0,
                                          acc[:, a + sh + lo - tk0:b + sh + lo - tk0], ALU.mult, ALU.add,
                                          [pr, acc, pvt], [acc])
                                for (a, b) in splits(tk0 + sh - lo, b0 + sh - lo):
                                    K.stt(acc[:, a - sh + lo - tk0:b - sh + lo - tk0], uap(pi, a, b), w2,
                                          acc[:, a - sh + lo - tk0:b - sh + lo - tk0], ALU.mult, ALU.add,
                                          [pr, acc, pvt], [acc])
                            else:
                                rl = g["rl"]
                                u3 = PS[2 * pi][:, 0:TT].rearrange("p (r c) -> p r c", c=rl)
                                a3 = acc[:, :].rearrange("p (r c) -> p r c", c=rl)
                                K.stt(a3[:, :, 1:rl], u3[:, :, 0:rl - 1], w0, a3[:, :, 1:rl], ALU.mult, ALU.add,
                                      [pr, acc, pvt], [acc])
                                K.stt(a3[:, :, 0:rl - 1], u3[:, :, 1:rl], w2, a3[:, :, 0:rl - 1], ALU.mult, ALU.add,
                                      [pr, acc, pvt], [acc])
                            return acc

                        def splits(a, b):
                            out = []
                            while a < b:
                                e = min(b, (a // 512 + 1) * 512)
                                out.append((a, e))
                                a = e
                            return out

                        for f in range(22):
                            pa_i = (2 * f) % 3
                            pl_i = (2 * f + 1) % 3
                            pra = umm(f, pa_i)
                            prl = umm(22 + f, pl_i)
                            ua = conv(pra, pa_i, f)
                            ul = conv(prl, pl_i, 22 + f)
                            sa = tmp.new()
                            K.act(sa[:, :], ua[:, :], AF.Silu, [ua], [sa])
                            K.tt("pool", qq[:, f, :], sa[:, :], ul[:, :], ALU.mult, [sa, ul], [qq])
                        for mo in range(8):
                            wd = wdr.new()
                            K.dma("sp", wd[:, :, :], WB["wdn"][l, mo], [], [wd])
                            pO = p4big.new()
                            for f in range(22):
                                K.mm(pO, pO.v(TT), wd[:, f, :], qq[:, f, :], [wd, qq], start=(f == 0), stop=(f == 21))
                            xo = tmp.new()
                            K.stt(xo[:, :], pO.v(TT), modc(l, 5, mo, ci), xm[:, mo, co:co + TT], ALU.mult, ALU.add,
                                  [pO, modt, xm], [xo])
                            K.dma("pool", XTA[mo * 128:(mo + 1) * 128, tk0:tk0 + TT], xo[:, :], [xo], [])
                            gdrip(rgen, 9)
                        gdrip(rgen, 10 ** 6)

        def phase5():
            with ExitStack() as pes:
                xT_r = K.ring("p5x", [128, 8, TT], F32, 2, pes)
                yT = K.sb("p5y", [128, 8, TT], F32, pes)
                sq = K.sb("p5sq", [128, 8, TT], BF16, pes)
                rb5 = rms_bufs("p5r_", pes)
                yo_r = K.ring("p5o", [128, D], F32, 2, pes)
                for tix in range(NTA // TT):
                    tk0 = tix * TT
                    xT = xT_r.new()
                    K.dma("sp", xT[:, :, :], XTA[:, tk0:tk0 + TT].rearrange("(kc p) t -> p kc t", p=128), [], [xT])
                    gdrip(rms_gen(xT, yT, TT, None, 0, 0, 0, rb5, sq), 10 ** 6)
                    for blk in range(TT // 128):
                        yo = yo_r.new()
                        for half in range(2):
                            pr = pbig.new()
                            for q in range(4):
                                kc = half * 4 + q
                                K.I("pe", [yT, ident], [pr],
                                    lambda e, kc=kc, q=q, pr=pr, blk=blk: e.transpose(
                                        out=pr.v(128, o=q * 128), in_=yT[:, kc, blk * 128:(blk + 1) * 128],
                                        identity=ident[:, :]))
                            K.cp("act" if half else "dve", yo[:, half * 512:(half + 1) * 512], pr.v(512), [pr], [yo])
                        K.dma("pool", yout[tk0 + blk * 128:tk0 + (blk + 1) * 128, :], yo[:, :], [yo], [])

        ph = phases or ("w", "0", "1", "2", "3", "4", "5")
        if "w" in ph:
            phase_w()
            K.barrier()
        else:
            with ExitStack() as pes:
                ff = K.sb("dbgfill", [128, 1024], F32, pes)
                fb = K.sb("dbgfillb", [128, 2816], BF16, pes)
                memset(ff, ff[:, :], -0.05)
                memset(fb, fb[:, :], 0.05)
                for t5 in range(NTA // 1024):
                    cs = slice(t5 * 1024, (t5 + 1) * 1024)
                    for m in range(8):
                        rs_ = slice(m * 128, (m + 1) * 128)
                        for dst in (ZR[rs_, cs], ZKK[rs_, cs], ZKD[0, rs_, cs], ZKD[1, rs_, cs], ZBA[0, rs_, cs],
                                    ZBA[1, rs_, cs], ZLW[0, rs_, cs], ZLW[1, rs_, cs]):
                            K.dma("sp", dst, ff[:, :], [ff], [])
                        K.dma("sp", ZV[rs_, cs], fb[:, 0:1024], [fb], [])
                        K.dma("sp", XTA[rs_, cs], ff[:, :], [ff], [])
                        K.dma("sp", XTB[rs_, cs], ff[:, :], [ff], [])
                        K.dma("sp", YTD[0, rs_, cs], ff[:, :], [ff], [])
                        K.dma("sp", YTD[1, rs_, cs], ff[:, :], [ff], [])
                        for dst in (ZBON[rs_, cs], ZG[rs_, cs], ZOB[rs_, cs], ZGAB[rs_, cs], ZGAB[1024 + m * 128:1152 + m * 128, cs]):
                            K.dma("sp", dst, fb[:, 0:1024], [fb], [])
                for nm_, _, Kd_, Nd_ in WSPEC:
                    for j in range(Nd_ // 128):
                        K.dma("sp", WB[nm_][0, j], fb[:, 0:Kd_].rearrange("p (k c) -> p k c", c=128), [fb], [])
                K.barrier()
            for b in (modt, a1t, a2t):
                memset(b, b[:, :, :, :], 0.5)
            for b in smallw.values():
                memset(b, b[:, :, :], 0.01)
            K.barrier()
        if "0" in ph:
            phase0()
            K.barrier()
        for l in range(nlayers):
            for gi in groups:
                g = GROUPS[gi]
                for nm, fn in (("1", phase1), ("2", phase2), ("3", phase3), ("4", phase4)):
                    if nm in ph:
                        fn(l, g)
                        K.barrier()
        if "5" in ph:
            phase5()
        K.barrier()
        build_nc.nins = K.nins
    return nc


DEBUG_OUT = set()


def _pvec(v):
    v = np.asarray(v, np.float32).reshape(-1)
    return v.reshape(-1, 128).T


def make_inputs(core, inp):
    b = core % 2
    xin = np.concatenate([np.asarray(inp["x_prompt"][4 * core:4 * core + 4]).reshape(NPT, D),
                          np.asarray(inp["x_sample"][b]).reshape(NST, D)], 0)
    cond = np.stack([_pvec(inp["c_ctx"]), _pvec(inp["c"][b])], -1)
    pv = np.zeros((L, 128, NPV), np.float32)
    for l in range(L):
        def put(name, v):
            a = _pvec(v)
            pv[l, :, PV[name]:PV[name] + a.shape[1]] = a
        put("n1g", inp["norm1_g"][l])
        put("n2g", inp["norm2_g"][l])
        put("w0", inp["decay_w0"][l])
        put("a0", inp["iclr_a0"][l])
        put("kk", inp["k_k"][l])
        put("ka", inp["k_a"][l])
        put("rk", inp["r_k"][l])
        put("gnw", inp["gn_w"][l])
        put("gnb", inp["gn_b"][l])
        put("cmw", inp["conv_mix_w"][l])
        put("cmb", inp["conv_mix_b"][l])
        put("cfw", inp["conv_ffn_w"][l])
        put("cfb", inp["conv_ffn_b"][l])
        put("bmod", inp["b_mod"][l])
    m = {
        "xin": np.ascontiguousarray(xin, np.float32),
        "st_in": np.ascontiguousarray(inp["state_wkv"][b], np.float32),
        "cond": np.ascontiguousarray(cond, np.float32),
        "pv": pv,
        "nfg": np.ascontiguousarray(_pvec(inp["norm_f_g"]), np.float32),
        "w_mod": inp["w_mod"], "w_in": inp["w_in"],
        "decay_w2": np.asarray(inp["decay_w2"]).reshape(L, 128, D),
        "iclr_a2": np.asarray(inp["iclr_a2"]).reshape(L, 128, D),
        "gate_g2": inp["gate_g2"], "w_pa": inp["w_pa"], "w_pb": inp["w_pb"], "w_o": inp["w_o"],
        "w_up": inp["w_up"], "w_down": inp["w_down"],
    }
    return {k: np.ascontiguousarray(np.asarray(v, np.float32)) for k, v in m.items()}


def kernel(**inputs):
    inp = {k: np.asarray(v) for k, v in inputs.items()}
    nc = build_nc()
    in_maps = [make_inputs(c, inp) for c in range(8)]
    res = run_bass_kernel_spmd(nc, in_maps, core_ids=list(range(8)))
    rs = res.results
    y_prompt = np.concatenate([np.asarray(rs[c]["yout"])[0:NPT].reshape(4, 256, D) for c in range(8)], 0)
    y_sample = np.stack([np.asarray(rs[b]["yout"])[NPT:NTA] for b in range(2)], 0)
    new_state = np.concatenate([np.asarray(rs[c]["nsout"]) for c in range(8)], 0)
    return (y_prompt.astype(np.float32), y_sample.astype(np.float32), new_state.astype(np.float32))
```

```python
import os
import numpy as np
from contextlib import ExitStack
import concourse.bass as bass
import concourse.mybir as mybir
from concourse.bass_utils import run_bass_kernel_spmd

F32 = mybir.dt.float32
BF16 = mybir.dt.bfloat16
AF = mybir.ActivationFunctionType
ALU = mybir.AluOpType

D = 1024
L = 2
NH = 16
HD = 64
DFF = 2816
NIN = 8576
NPT = 1024
NST = 4096
NTA = NPT + NST
TT = 512
C = 128
EPOCH = 8000
NSLOT = 20
SAME_SYNC = True
WD = BF16

PV = {}
_o = 0
for _n, _c in [("n1g", 8), ("n2g", 8), ("w0", 16), ("a0", 16), ("kk", 8), ("ka", 8), ("rk", 8), ("gnw", 8),
               ("gnb", 8), ("cmw", 24), ("cmb", 8), ("cfw", 132), ("cfb", 44), ("bmod", 48)]:
    PV[_n] = _o
    _o += _c
NPV = _o


class Buf:
    def __init__(self, t):
        self.t = t
        self.w = None
        self.r = {}

    def __getitem__(self, i):
        return self.t[i]


class PReg(Buf):
    def __init__(self, t, c0, n, bank=None):
        self.t = t
        self.bank = bank if bank is not None else Buf(t)
        self.c0 = c0
        self.n = n

    w = property(lambda self: self.bank.w, lambda self, v: setattr(self.bank, "w", v))
    r = property(lambda self: self.bank.r, lambda self, v: setattr(self.bank, "r", v))

    def v(self, n=None, p0=0, p1=128, o=0):
        n = self.n - o if n is None else n
        return self.t[p0:p1, self.c0 + o:self.c0 + o + n]


class Ring:
    def __init__(self, bufs):
        self.bufs = bufs
        self.i = 0

    def new(self):
        b = self.bufs[self.i % len(self.bufs)]
        self.i += 1
        return b


class KB:
    def __init__(self, nc, es):
        self.nc = nc
        self.es = es
        self.E = {"pe": nc.tensor, "act": nc.scalar, "dve": nc.vector, "pool": nc.gpsimd, "sp": nc.sync}
        self.comp = ["pe", "act", "dve", "pool"]
        self.cnt = {e: 0 for e in self.comp}
        self.esem = {e: [] for e in self.comp}
        self.waited = {e: {} for e in self.E}
        self.dsem = {}
        self.duse = {}
        self.dnext = {"sp": 0, "pool": 0}
        for q in ("sp", "pool"):
            for i in range(NSLOT):
                self.dsem[(q, i)] = es.enter_context(nc.semaphore("d%s%d" % (q, i)))
                self.duse[(q, i)] = 0
        self.barsem = es.enter_context(nc.semaphore("bar"))
        self.bar = 0
        self.bufs = []
        self.nins = 0

    def sb(self, name, shape, dt, es=None):
        self.uid = getattr(self, "uid", 0) + 1
        t = (es or self.es).enter_context(self.nc.sbuf_tensor("s%d_%s" % (self.uid, name), list(shape), dt))
        b = Buf(t)
        self.bufs.append(b)
        return b

    def ring(self, name, shape, dt, n, es=None):
        return Ring([self.sb("%s%d" % (name, i), shape, dt, es) for i in range(n)])

    def reg(self, b):
        self.bufs.append(b)
        return b

    def _sem(self, e, ep):
        while len(self.esem[e]) <= ep:
            self.esem[e].append(self.es.enter_context(self.nc.semaphore("s%s%d" % (e, len(self.esem[e])))))
        return self.esem[e][ep]

    def _wait(self, e, tok):
        kind, key, val = tok
        if kind == "eng":
            if key == e and (e == "pe" or not SAME_SYNC):
                return
            if self.waited[e].get(key, 0) >= val:
                return
            self.waited[e][key] = val
            ep = (val - 1) // EPOCH
            self.E[e].wait_ge(self._sem(key, ep), (val - 1) % EPOCH + 1)
        else:
            if self.waited[e].get(key, 0) >= val:
                return
            self.waited[e][key] = val
            self.E[e].wait_ge(self.dsem[key], val)

    def _deps(self, e, reads, writes):
        for b in reads:
            if b.w is not None:
                self._wait(e, b.w)
        for b in writes:
            if b.w is not None:
                self._wait(e, b.w)
            for tk in b.r.values():
                self._wait(e, tk)

    def _mark(self, tok, reads, writes):
        k = tok[1]
        for b in reads:
            b.r[k] = tok
        for b in writes:
            b.w = tok
            b.r = {}

    def I(self, e, reads, writes, fn):
        self._deps(e, reads, writes)
        n = self.cnt[e]
        sem = self._sem(e, n // EPOCH)
        fn(self.E[e]).then_inc(sem, 1)
        self.cnt[e] = n + 1
        self._mark(("eng", e, n + 1), reads, writes)
        self.nins += 1

    def dma(self, q, out, in_, reads, writes):
        self._deps(q, reads, writes)
        key = (q, self.dnext[q] % NSLOT)
        self.dnext[q] += 1
        u = self.duse[key]
        if u > 0:
            self._wait(q, ("dma", key, 16 * u))
        self.E[q].dma_start(out=out, in_=in_).then_inc(self.dsem[key], 16)
        self.duse[key] = u + 1
        self._mark(("dma", key, 16 * (u + 1)), reads, writes)
        self.nins += 1

    def barrier(self):
        for e in self.comp:
            if self.cnt[e] > 0:
                self._wait("sp", ("eng", e, self.cnt[e]))
        for key, u in self.duse.items():
            if u > 0:
                self._wait("sp", ("dma", key, 16 * u))
        self.bar += 1
        self.nc.sync.sem_inc(self.barsem, 1)
        for e in self.comp:
            self.E[e].wait_ge(self.barsem, self.bar)
        for e in self.E:
            for f in self.comp:
                self.waited[e][f] = self.cnt[f]
            for key, u in self.duse.items():
                self.waited[e][key] = 16 * u
        for b in self.bufs:
            b.w = None
            b.r = {}

    def mm(self, pr, out, lhsT, rhs, reads, start=True, stop=True):
        self.I("pe", reads, [pr], lambda e: e.matmul(out, lhsT=lhsT, rhs=rhs, start=start, stop=stop))

    def act(self, out, in_, func, reads, writes, bias=None, scale=None):
        kw = {}
        if bias is not None:
            kw["bias"] = bias
        if scale is not None:
            kw["scale"] = scale
        self.I("act", reads, writes, lambda e: e.activation(out=out, in_=in_, func=func, **kw))

    def tt(self, eng, out, in0, in1, op, reads, writes):
        self.I(eng, reads, writes, lambda e: e.tensor_tensor(out=out, in0=in0, in1=in1, op=op))

    def ts(self, eng, out, in0, s1, op0, reads, writes, s2=None, op1=None):
        if op1 is None:
            self.I(eng, reads, writes, lambda e: e.tensor_scalar(out=out, in0=in0, scalar1=s1, scalar2=None, op0=op0))
        else:
            self.I(eng, reads, writes,
                   lambda e: e.tensor_scalar(out=out, in0=in0, scalar1=s1, scalar2=s2, op0=op0, op1=op1))

    def stt(self, out, in0, scalar, in1, op0, op1, reads, writes):
        self.I("dve", reads, writes,
               lambda e: e.scalar_tensor_tensor(out=out, in0=in0, scalar=scalar, in1=in1, op0=op0, op1=op1))

    def cp(self, eng, out, in_, reads, writes):
        if eng == "act":
            self.act(out, in_, AF.Copy, reads, writes)
        else:
            self.I(eng, reads, writes, lambda e: e.tensor_copy(out=out, in_=in_))


GROUPS = [
    dict(t0=0, nt=NPT, seqs=[(256 * s, 256) for s in range(4)], grid=False, cond=0, rl=256, H=0, sh=1),
    dict(t0=NPT, nt=NST, seqs=[(NPT, NST)], grid=True, cond=1, rl=64, H=64, sh=64),
]


def build_nc(debug=False, nlayers=L, groups=(0, 1), phases=None):
    nc = bass.Bass("TRN2", target_bir_lowering=False)
    dbg = {}

    def din(name, shape, dt=F32):
        return nc.dram_tensor(name, list(shape), dt, kind="ExternalInput").ap()

    def dscr(name, shape, dt=F32):
        kind = "ExternalOutput" if (debug and name in DEBUG_OUT) else "Internal"
        return nc.dram_tensor(name, list(shape), dt, kind=kind).ap()

    xin = din("xin", [NTA, D])
    st_in = din("st_in", [L, 2, NH, HD, HD])
    cond_in = din("cond", [128, 8, 2])
    pv_in = din("pv", [L, 128, NPV])
    nfg_in = din("nfg", [128, 8])
    tinyw = phases is not None and "w" not in phases
    if tinyw:
        w_mod = w_in = w_pa = w_pb = w_o = w_up = w_dn = None
    else:
        w_mod = din("w_mod", [L, D, 6 * D])
        w_in = din("w_in", [L, D, NIN])
    dw2 = din("decay_w2", [L, 128, D])
    ia2 = din("iclr_a2", [L, 128, D])
    gg2 = din("gate_g2", [L, 128, D])
    if not tinyw:
        w_pa = din("w_pa", [L, D, D])
        w_pb = din("w_pb", [L, D, D])
        w_o = din("w_o", [L, D, D])
        w_up = din("w_up", [L, D, 2 * DFF])
        w_dn = din("w_down", [L, DFF, D])

    yout = nc.dram_tensor("yout", [NTA, D], F32, kind="ExternalOutput").ap()
    nsout = nc.dram_tensor("nsout", [4, L, 2, NH, HD, HD], F32, kind="ExternalOutput").ap()

    WSPEC = [("win", w_in, D, NIN), ("wpa", w_pa, D, D), ("wpb", w_pb, D, D), ("wo", w_o, D, D),
             ("wup", w_up, D, 2 * DFF), ("wdn", w_dn, DFF, D)]
    WB = {}
    for nm, _, Kd, Nd in WSPEC:
        WB[nm] = dscr("wb_" + nm, [L, Nd // 128, 128, Kd // 128, 128], BF16)
    XTA = dscr("XTA", [D, NTA])
    XTB = dscr("XTB", [D, NTA])
    ZR = dscr("ZR", [D, NTA])
    ZKK = dscr("ZKK", [D, NTA])
    ZKD = dscr("ZKD", [2, D, NTA])
    ZBA = dscr("ZBA", [2, D, NTA])
    ZLW = dscr("ZLW", [2, D, NTA])
    ZV = dscr("ZV", [D, NTA], BF16)
    ZBON = dscr("ZBON", [D, NTA], BF16)
    ZG = dscr("ZG", [D, NTA], BF16)
    ZOB = dscr("ZOB", [D, NTA], BF16)
    ZGAB = dscr("ZGAB", [2 * D, NTA], BF16)
    YTD = dscr("YTD", [2, D, NTA])

    with ExitStack() as es:
        K = KB(nc, es)
        ident = K.sb("ident", [128, 128], F32)
        identb = K.sb("identb", [128, 128], BF16)
        onesb = K.sb("onesb", [128, 128], BF16)
        oblk = K.sb("oblk", [128, 128], BF16)
        oblk64 = K.sb("oblk64", [128, 128], BF16)
        maskf = K.sb("maskf", [128, 256], BF16)
        maskb = K.sb("maskb", [128, 256], BF16)
        maskf2 = K.sb("maskf2", [128, 512], BF16)
        maskb2 = K.sb("maskb2", [128, 512], BF16)
        mlow4 = K.sb("mlow4", [128, 512], BF16)
        mupp4 = K.sb("mupp4", [128, 512], BF16)
        identw4 = K.sb("identw4", [128, 512], WD)
        rstf = K.sb("rstf", [128, 1024], F32)
        rstb = K.sb("rstb", [128, 1024], F32)
        epsn = K.sb("epsn", [128, 1], F32)
        epsg = K.sb("epsg", [128, 1], F32)
        pvt = K.sb("pvt", [128, L, NPV], F32)
        omka = K.sb("omka", [128, L, 8], F32)
        nfg = K.sb("nfg", [128, 8], F32)
        condt = K.sb("condt", [128, 8, 2], F32)
        sct = K.sb("sct", [128, 8, 2], F32)
        modt = K.sb("modt", [128, L, 48, 2], F32)
        a1t = K.sb("a1t", [128, L, 8, 2], F32)
        a2t = K.sb("a2t", [128, L, 8, 2], F32)
        smallw = {}
        for nm in ("dw2", "ia2", "gg2"):
            smallw[nm] = K.sb("sw_" + nm, [128, L, D], BF16)
        PS = [K.es.enter_context(nc.psum_tensor("ps%d" % i, [128, 512], F32)) for i in range(8)]

        def memset(b, ap, v):
            K.I("pool", [], [b], lambda e: e.memset(ap, v))

        def asel(b, ap, step, cm, op, fill, base=0, n=128):
            K.I("pool", [b], [b], lambda e: e.affine_select(out=ap, in_=ap, pattern=[[step, n]], compare_op=op,
                                                            fill=fill, base=base, channel_multiplier=cm))

        memset(ident, ident[:, :], 0.0)
        asel(ident, ident[:, :], -1, 1, ALU.not_equal, 1.0)
        K.cp("pool", identb[:, :], ident[:, :], [ident], [identb])
        memset(onesb, onesb[:, :], 1.0)
        memset(oblk, oblk[:, :], 0.0)
        memset(oblk, oblk[0:64, 0:64], 1.0)
        memset(oblk, oblk[64:128, 64:128], 1.0)
        memset(oblk64, oblk64[:, :], 0.0)
        memset(oblk64, oblk64[0:64, 0:64], 1.0 / 64)
        memset(oblk64, oblk64[64:128, 64:128], 1.0 / 64)
        memset(maskf, maskf[:, :], 1.0)
        memset(maskb, maskb[:, :], 1.0)
        asel(maskf, maskf[:, 0:128], 1, -1, ALU.is_gt, 0.0)
        asel(maskf, maskf[:, 128:256], 1, -1, ALU.is_ge, 0.0)
        asel(maskb, maskb[:, 0:128], -1, 1, ALU.is_gt, 0.0)
        asel(maskb, maskb[:, 128:256], -1, 1, ALU.is_ge, 0.0)
        for r2 in range(2):
            K.cp("pool", maskf2[:, 256 * r2:256 * r2 + 256], maskf[:, :], [maskf], [maskf2])
            K.cp("pool", maskb2[:, 256 * r2:256 * r2 + 256], maskb[:, :], [maskb], [maskb2])
        for r4 in range(4):
            K.cp("pool", mlow4[:, 128 * r4:128 * r4 + 128], maskb[:, 0:128], [maskb], [mlow4])
            K.cp("pool", mupp4[:, 128 * r4:128 * r4 + 128], maskf[:, 0:128], [maskf], [mupp4])
            K.cp("pool", identw4[:, 128 * r4:128 * r4 + 128], ident[:, :], [ident], [identw4])
        memset(rstf, rstf[:, :], 1.0)
        memset(rstb, rstb[:, :], 1.0)
        memset(rstf, rstf[:, :].rearrange("p (m t) -> p m t", t=128)[:, :, 0:1], 0.0)
        memset(rstb, rstb[:, :].rearrange("p (m t) -> p m t", t=128)[:, :, 127:128], 0.0)
        memset(epsn, epsn[:, :], 1e-6)
        memset(epsg, epsg[:, :], 64e-5)
        K.dma("sp", pvt[:, :, :], pv_in.rearrange("l p n -> p l n"), [], [pvt])
        K.dma("sp", nfg[:, :], nfg_in, [], [nfg])
        K.dma("sp", condt[:, :, :], cond_in, [], [condt])
        for l in range(L):
            K.ts("dve", omka[:, l, :], pvt[:, l, PV["ka"]:PV["ka"] + 8], -1.0, ALU.mult, [pvt], [omka], s2=1.0,
                 op1=ALU.add)
        K.act(sct[:, :, :], condt[:, :, :], AF.Silu, [condt], [sct])

        def pvc(l, name, j):
            o = PV[name] + j
            return pvt[:, l, o:o + 1]

        def phase_w():
            with ExitStack() as pes:
                stg = K.ring("wstg", [128, 2048], F32, 3, pes)
                stb = K.ring("wstb", [128, 2048], BF16, 3, pes)
                ci = 0
                for nm, wap, Kd, Nd in WSPEC:
                    for l in range(nlayers):
                        for kc in range(Kd // 128):
                            for c0 in range(0, Nd, 2048):
                                cw = min(2048, Nd - c0)
                                s = stg.new()
                                b = stb.new()
                                K.dma("sp", s[:, 0:cw], wap[l, kc * 128:(kc + 1) * 128, c0:c0 + cw], [], [s])
                                K.cp(("dve", "pool", "act")[ci % 3], b[:, 0:cw], s[:, 0:cw], [s], [b])
                                ci += 1
                                K.dma("pool", WB[nm][l, c0 // 128:(c0 + cw) // 128, :, kc, :].rearrange("j p c -> p j c"),
                                      b[:, 0:cw].rearrange("p (j c) -> p j c", c=128), [b], [])
                for nm, wap in (("dw2", dw2), ("ia2", ia2), ("gg2", gg2)):
                    for l in range(L):
                        s = stg.new()
                        K.dma("sp", s[:, 0:D], wap[l], [], [s])
                        K.cp("dve", smallw[nm][:, l, :], s[:, 0:D], [s], [smallw[nm]])
                wm = K.ring("wmod", [128, 8, 128], F32, 3, pes)
                pi = 0
                for l in range(nlayers):
                    for j in range(48):
                        w = wm.new()
                        K.dma("sp", w[:, :, :], w_mod[l, :, j * 128:(j + 1) * 128].rearrange("(kc p) c -> p kc c", p=128),
                              [], [w])
                        pr = preg_big[pi % 8]
                        pi += 1
                        for kc in range(8):
                            K.mm(pr, pr.v(2), w[:, kc, :], sct[:, kc, :], [w, sct], start=(kc == 0), stop=(kc == 7))
                        K.ts("dve", modt[:, l, j, :], pr.v(2), pvc(l, "bmod", j), ALU.add, [pr, pvt], [modt])
                    for kc in range(8):
                        K.ts("dve", a1t[:, l, kc, :], modt[:, l, 8 + kc, :], 1.0, ALU.add, [modt, pvt], [a1t],
                             s2=pvc(l, "n1g", kc), op1=ALU.mult)
                        K.ts("dve", a2t[:, l, kc, :], modt[:, l, 32 + kc, :], 1.0, ALU.add, [modt, pvt], [a2t],
                             s2=pvc(l, "n2g", kc), op1=ALU.mult)

        preg_big = [K.reg(PReg(PS[i], 0, 512)) for i in range(8)]
        preg_half = [K.reg(PReg(PS[i % 8], 256 * (i // 8), 256, bank=preg_big[i % 8].bank)) for i in range(16)]
        pbig = Ring(preg_big)
        phalf = Ring(preg_half)

        def modc(l, which, kc, ci):
            return modt[:, l, which * 8 + kc, ci:ci + 1]

        def phase0():
            with ExitStack() as pes:
                xin_r = K.ring("xin", [128, D], F32, 2, pes)
                xo_r = K.ring("xo", [128, 8, 128], F32, 2, pes)
                for blk in range(NTA // 128):
                    t0 = blk * 128
                    xi = xin_r.new()
                    K.dma("sp", xi[:, :], xin[t0:t0 + 128, :], [], [xi])
                    xo = xo_r.new()
                    for half in range(2):
                        pr = pbig.new()
                        for q in range(4):
                            kc = half * 4 + q
                            K.I("pe", [xi, ident], [pr],
                                lambda e, kc=kc, q=q, pr=pr, xi=xi: e.transpose(out=pr.v(128, o=q * 128),
                                                                               in_=xi[:, kc * 128:(kc + 1) * 128],
                                                                               identity=ident[:, :]))
                        K.cp("act" if half else "dve", xo[:, half * 4:half * 4 + 4, :],
                             pr.v(512).rearrange("p (k t) -> p k t", t=128), [pr], [xo])
                    K.dma("pool", XTA[:, t0:t0 + 128].rearrange("(kc p) t -> p kc t", p=128), xo[:, :, :], [xo], [])

        def rms_gen(xT, hT, nt, a_t, l, shwhich, ci, rb, sq, pring=None):
            pring = pring or pbig
            for c0 in range(0, nt, 512):
                cw = min(512, nt - c0)
                for kc in range(8):
                    K.act(sq[:, kc, 0:cw], xT[:, kc, c0:c0 + cw], AF.Square, [xT], [sq])
                    yield
                pr = pring.new()
                for kc in range(8):
                    K.mm(pr, pr.v(cw), onesb[:, :], sq[:, kc, 0:cw], [onesb, sq], start=(kc == 0), stop=(kc == 7))
                yield
                sd = rb["sd"]
                K.act(sd[:, 0:cw], pr.v(cw), AF.Sqrt, [pr, epsn], [sd], bias=epsn[:, 0:1], scale=1.0 / D)
                yield
                rs = rb["rs"]
                K.I("dve", [sd], [rs], lambda e, rs=rs, sd=sd, cw=cw: e.reciprocal(out=rs[:, 0:cw], in_=sd[:, 0:cw]))
                yield
                for kc in range(8):
                    t = rb["t"].new()
                    K.tt("dve" if kc % 2 else "pool", t[:, 0:cw], xT[:, kc, c0:c0 + cw], rs[:, 0:cw], ALU.mult,
                         [xT, rs], [t])
                    yield
                    if a_t is None:
                        K.ts("dve", hT[:, kc, c0:c0 + cw], t[:, 0:cw], nfg[:, kc:kc + 1], ALU.mult, [t, nfg], [hT])
                    else:
                        K.ts("dve", hT[:, kc, c0:c0 + cw], t[:, 0:cw], a_t[:, l, kc, ci:ci + 1], ALU.mult,
                             [t, a_t, modt], [hT], s2=modc(l, shwhich, kc, ci), op1=ALU.add)
                    yield

        def rms_bufs(name, pes):
            return dict(sd=K.sb(name + "sd", [128, 512], F32, pes), rs=K.sb(name + "rs", [128, 512], F32, pes),
                        t=K.ring(name + "t", [128, 512], F32, 2, pes))

        def gdrip(gen, n):
            if gen is None:
                return
            for _ in range(n):
                try:
                    next(gen)
                except StopIteration:
                    return

        def phase1(l, g):
            ci = g["cond"]
            rl = g["rl"]
            with ExitStack() as pes:
                xT_r = K.ring("p1x", [128, 8, TT], F32, 2, pes)
                hT2 = [K.sb("p1h%d" % i, [128, 8, TT], BF16, pes) for i in range(2)]
                rb = rms_bufs("p1r_", pes)
                sq = K.sb("p1sq", [128, 8, TT], BF16, pes)
                wr = K.ring("p1w", [128, 8, 128], BF16, 6, pes)
                tmp = K.ring("p1t", [128, TT], F32, 14, pes)
                tmb = K.ring("p1tb", [128, TT], BF16, 6, pes)
                r32r = K.ring("p1r", [128, TT], F32, 2, pes)
                k32r = K.ring("p1k", [128, TT], F32, 2, pes)
                kqr = K.ring("p1kq", [128, TT], F32, 2, pes)
                v16r = K.ring("p1v", [128, TT], BF16, 2, pes)
                sqkr = K.ring("p1sqk", [128, TT], BF16, 2, pes)
                avr = K.ring("p1av", [128, TT], F32, 4, pes)
                rkdr = K.ring("p1rkd", [128, TT], BF16, 4, pes)
                tw = K.sb("p1tw", [128, TT], BF16, pes)
                ta = K.sb("p1ta", [128, TT], BF16, pes)
                tg = K.sb("p1tg", [128, TT], BF16, pes)
                order = [24, 25, 26]
                for m in range(8):
                    order += [m, 8 + m, 16 + m]
                for m in range(8):
                    order += [27 + m, 35 + m, 43 + m]
                order += list(range(51, 67))
                ntile = g["nt"] // TT
                xTs = {}

                def xload(tix):
                    if tix < ntile and tix not in xTs:
                        tk = g["t0"] + tix * TT
                        xT = xT_r.new()
                        K.dma("sp", xT[:, :, :], XTA[:, tk:tk + TT].rearrange("(kc p) t -> p kc t", p=128), [], [xT])
                        xTs[tix] = xT

                xload(0)
                gdrip(rms_gen(xTs.pop(0), hT2[0], TT, a1t, l, 0, ci, rb, sq), 10 ** 6)
                for tix in range(ntile):
                    tk0 = g["t0"] + tix * TT
                    hT = hT2[tix % 2]
                    xload(tix + 1)
                    rgen = None
                    if tix + 1 < ntile:
                        rgen = rms_gen(xTs.pop(tix + 1), hT2[(tix + 1) % 2], TT, a1t, l, 0, ci, rb, sq)
                    wl = {}
                    st = {"n": 0}

                    def wload(upto):
                        while st["n"] <= min(upto, len(order) - 1):
                            w = wr.new()
                            K.dma("sp", w[:, :, :], WB["win"][l, order[st["n"]]], [], [w])
                            wl[st["n"]] = w
                            st["n"] += 1

                    pos = {"i": 0}

                    def zmm():
                        i = pos["i"]
                        pos["i"] += 1
                        wload(i + 4)
                        w = wl.pop(i)
                        pr = pbig.new()
                        for kc in range(8):
                            K.mm(pr, pr.v(TT), w[:, kc, :], hT[:, kc, :], [w, hT], start=(kc == 0), stop=(kc == 7))
                        return pr

                    def store(dst, src_b, src_ap):
                        K.dma("pool", dst, src_ap, [src_b], [])

                    z = zmm()
                    K.act(tw[:, :], z.v(TT), AF.Tanh, [z], [tw])
                    z = zmm()
                    K.cp("dve", ta[:, :], z.v(TT), [z], [ta])
                    z = zmm()
                    K.act(tg[:, :], z.v(TT), AF.Sigmoid, [z], [tg])
                    cols = slice(tk0, tk0 + TT)

                    def part1(m):
                        rows = slice(m * 128, (m + 1) * 128)
                        zr = zmm()
                        zk = zmm()
                        zv = zmm()
                        r32 = r32r.new()
                        K.cp("act", r32[:, :], zr.v(TT), [zr], [r32])
                        store(ZR[rows, cols], r32, r32[:, :])
                        k32 = k32r.new()
                        K.cp("dve", k32[:, :], zk.v(TT), [zk], [k32])
                        kq = kqr.new()
                        K.ts("dve", kq[:, :], zk.v(TT), pvc(l, "kk", m), ALU.mult, [zk, pvt], [kq])
                        v16 = v16r.new()
                        K.cp("act", v16[:, :], zv.v(TT), [zv], [v16])
                        store(ZV[rows, cols], v16, v16[:, :])
                        sqk = sqkr.new()
                        K.act(sqk[:, :], kq[:, :], AF.Square, [kq], [sqk])
                        avs, rkds = [], []
                        pls = []
                        for d in range(2):
                            if os.environ.get('P1_Y') == '1':
                                pls.append((None, None))
                                continue
                            hp = slice(64 * d, 64 * d + 64)
                            pl = pbig.new()
                            K.mm(pl, pl.v(TT), smallw["dw2"][hp, l, m * 128:(m + 1) * 128], tw[hp, :],
                                 [smallw["dw2"], tw])
                            pa = pbig.new()
                            K.mm(pa, pa.v(TT), smallw["ia2"][hp, l, m * 128:(m + 1) * 128], ta[hp, :],
                                 [smallw["ia2"], ta])
                            pls.append((pl, pa))
                        pg = pbig.new()
                        K.mm(pg, pg.v(TT), smallw["gg2"][:, l, m * 128:(m + 1) * 128], tg[:, :], [smallw["gg2"], tg])
                        for d in range(2):
                            pl, pa = pls[d]
                            if os.environ.get('P1_Y') == '1':
                                hp = slice(64 * d, 64 * d + 64)
                                pl = pbig.new()
                                K.mm(pl, pl.v(TT), smallw["dw2"][hp, l, m * 128:(m + 1) * 128], tw[hp, :],
                                     [smallw["dw2"], tw])
                            sg = tmp.new()
                            K.act(sg[:, :], pl.v(TT), AF.Sigmoid, [pl, pvt], [sg], bias=pvc(l, "w0", d * 8 + m))
                            lw = tmp.new()
                            K.act(lw[:, :], sg[:, :], AF.Identity, [sg], [lw], scale=-0.6065306597126334)
                            store(ZLW[d, rows, cols], lw, lw[:, :])
                            if os.environ.get('P1_Y') == '1':
                                pa = pbig.new()
                                K.mm(pa, pa.v(TT), smallw["ia2"][hp, l, m * 128:(m + 1) * 128], ta[hp, :],
                                     [smallw["ia2"], ta])
                            av = avr.new()
                            K.act(av[:, :], pa.v(TT), AF.Sigmoid, [pa, pvt], [av], bias=pvc(l, "a0", d * 8 + m))
                            f = tmp.new()
                            K.ts("dve", f[:, :], av[:, :], pvc(l, "ka", m), ALU.mult, [av, pvt, omka], [f],
                                 s2=omka[:, l, m:m + 1], op1=ALU.add)
                            kd = tmp.new()
                            K.tt("dve", kd[:, :], k32[:, :], f[:, :], ALU.mult, [k32, f], [kd])
                            store(ZKD[d, rows, cols], kd, kd[:, :])
                            rkd = rkdr.new()
                            K.stt(rkd[:, :], r32[:, :], pvc(l, "rk", m), kd[:, :], ALU.mult, ALU.mult,
                                  [r32, kd, pvt], [rkd])
                            avs.append(av)
                            rkds.append(rkd)
                        gt = tmb.new()
                        K.cp("act", gt[:, :], pg.v(TT), [pg], [gt])
                        store(ZG[rows, cols], gt, gt[:, :])
                        return dict(m=m, kq=kq, sqk=sqk, v16=v16, avs=avs, rkds=rkds)

                    def part2(c):
                        m = c["m"]
                        rows = slice(m * 128, (m + 1) * 128)
                        pss = pbig.new()
                        K.mm(pss, pss.v(TT), oblk[:, :], c["sqk"][:, :], [oblk, c["sqk"]])
                        pbs = pbig.new()
                        for d in range(2):
                            K.mm(pbs, pbs.v(TT), oblk[:, :], c["rkds"][d][:, :], [oblk, c["rkds"][d]], start=(d == 0),
                                 stop=(d == 1))
                        nrm = tmp.new()
                        K.act(nrm[:, :], pss.v(TT), AF.Sqrt, [pss], [nrm])
                        K.ts("dve", nrm[:, :], nrm[:, :], 1e-12, ALU.max, [nrm], [nrm])
                        rinv = tmp.new()
                        K.I("dve", [nrm], [rinv], lambda e, a=rinv, b=nrm: e.reciprocal(out=a[:, :], in_=b[:, :]))
                        kk = tmp.new()
                        K.tt("dve", kk[:, :], c["kq"][:, :], rinv[:, :], ALU.mult, [c["kq"], rinv], [kk])
                        store(ZKK[rows, cols], kk, kk[:, :])
                        for d in range(2):
                            ba = tmp.new()
                            K.tt("dve" if d else "pool", ba[:, :], kk[:, :], c["avs"][d][:, :], ALU.mult,
                                 [kk, c["avs"][d]], [ba])
                            store(ZBA[d, rows, cols], ba, ba[:, :])
                        bon = tmb.new()
                        K.tt("dve", bon[:, :], pbs.v(TT), c["v16"][:, :], ALU.mult, [pbs, c["v16"]], [bon])
                        store(ZBON[rows, cols], bon, bon[:, :])

                    prev = None
                    for m in range(8):
                        cur = part1(m)
                        if os.environ.get('P1_X') == '1':
                            part2(cur)
                            continue
                        if prev is not None:
                            part2(prev)
                        prev = cur
                    first_conv = True
                    for m in range(8):
                        rows = slice(m * 128, (m + 1) * 128)
                        zcb = zmm()
                        zcc = zmm()
                        zcx = zmm()
                        if first_conv and prev is not None:
                            part2(prev)
                            first_conv = False
                        cxs = tmp.new()
                        K.cp("act", cxs[:, :], zcx.v(TT), [zcx], [cxs])
                        pp = tmp.new()
                        K.tt("dve", pp[:, :], zcc.v(TT), cxs[:, :], ALU.mult, [zcc, cxs], [pp])
                        acc = tmp.new()
                        K.ts("pool", acc[:, :], pp[:, :], pvc(l, "cmw", 8 + m), ALU.mult, [pp, pvt], [acc],
                             s2=pvc(l, "cmb", m), op1=ALU.add)
                        p3 = pp[:, :].rearrange("p (r c) -> p r c", c=rl)
                        a3 = acc[:, :].rearrange("p (r c) -> p r c", c=rl)
                        K.stt(a3[:, :, 1:rl], p3[:, :, 0:rl - 1], pvc(l, "cmw", m), a3[:, :, 1:rl], ALU.mult, ALU.add,
                              [pp, acc, pvt], [acc])
                        K.stt(a3[:, :, 0:rl - 1], p3[:, :, 1:rl], pvc(l, "cmw", 16 + m), a3[:, :, 0:rl - 1], ALU.mult,
                              ALU.add, [pp, acc, pvt], [acc])
                        ob = tmb.new()
                        K.tt("dve", ob[:, :], zcb.v(TT), acc[:, :], ALU.mult, [zcb, acc], [ob])
                        store(ZOB[rows, cols], ob, ob[:, :])
                    for j in range(16):
                        zg = zmm()
                        sgt = tmb.new()
                        K.act(sgt[:, :], zg.v(TT), AF.Sigmoid, [zg], [sgt])
                        store(ZGAB[j * 128:(j + 1) * 128, tk0:tk0 + TT], sgt, sgt[:, :])
                        gdrip(rgen, 3)
                    gdrip(rgen, 10 ** 6)

        def phase2(l, g):
            with ExitStack() as pes:
                ld = {}
                for nm, dt in (("R", F32), ("KK", F32), ("KD", F32), ("BA", F32), ("LW", F32), ("V", BF16)):
                    ld[nm] = [K.sb("p2%s%d" % (nm, d), [128, 1024], dt, pes) for d in range(2)]
                cum = [K.sb("p2cum%d" % d, [128, 1024], F32, pes) for d in range(2)]
                tmp = K.ring("p2t", [128, 1024], F32, 3, pes)
                AR = [K.sb("p2ar%d" % d, [128, 8, 2, 128], BF16, pes) for d in range(2)]
                BH = [K.sb("p2bh%d" % d, [128, 1024], BF16, pes) for d in range(2)]
                KH = [K.sb("p2kh%d" % d, [128, 1024], BF16, pes) for d in range(2)]
                GC = [K.sb("p2gc%d" % d, [128, 8], F32, pes) for d in range(2)]
                DG = [K.sb("p2dg%d" % d, [128, 8, 128], BF16, pes) for d in range(2)]
                VT = [K.sb("p2vt%d" % d, [128, 8, 128], BF16, pes) for d in range(2)]
                BT = [K.sb("p2bt%d" % d, [128, 8, 128], BF16, pes) for d in range(2)]
                KT = [K.sb("p2kt%d" % d, [128, 8, 128], BF16, pes) for d in range(2)]
                YT = [K.sb("p2yt%d" % d, [128, 8, 128], F32, pes) for d in range(2)]
                S32 = [K.sb("p2s32_%d" % d, [128, 8, 64], F32, pes) for d in range(2)]
                S16 = [K.sb("p2s16_%d" % d, [128, 8, 64], BF16, pes) for d in range(2)]
                LMK = [K.sb("p2lmk%d" % gq, [128, 4, 256], WD, pes) for gq in range(4)]
                LMB = [K.sb("p2lmb%d" % gq, [128, 4, 256], WD, pes) for gq in range(4)]
                LAB = [K.sb("p2lab%d" % gq, [128, 4, 128], WD, pes) for gq in range(4)]
                PP = [[K.sb("p2p%d_%d" % (gq, i), [128, 4, 128], WD, pes) for i in range(2)] for gq in range(4)]
                PT = [[K.sb("p2pt%d_%d" % (gq, i), [128, 4, 128], WD, pes) for i in range(2)] for gq in range(4)]
                TTt = [[K.sb("p2tt%d_%d" % (gq, i), [128, 4, 128], WD, pes) for i in range(2)] for gq in range(4)]
                XX = [K.sb("p2x%d" % u, [128, 8, 64], BF16, pes) for u in range(2)]
                UU = [K.sb("p2u%d" % u, [128, 8, 64], BF16, pes) for u in range(2)]
                sld = K.sb("p2sld", [64, NH, HD], F32, pes)
                sout = K.ring("p2so", [64, 128], F32, 2, pes)
                identw = identb if WD == BF16 else ident

                def v3(b):
                    return b[:, :].rearrange("p (m t) -> p m t", t=128)

                for si, (s0, slen) in enumerate(g["seqs"]):
                    nch = slen // C
                    for d in range(2):
                        if g["grid"]:
                            K.dma("sp", sld[:, :, :], st_in[l, d].rearrange("h i j -> i h j"), [], [sld])
                            for m in range(8):
                                pr = pbig.new()
                                K.I("pe", [sld, ident], [pr],
                                    lambda e, pr=pr, m=m: e.transpose(
                                        out=pr.v(64), in_=sld[:, 2 * m:2 * m + 2, :].rearrange("i h j -> i (h j)"),
                                        identity=ident[0:64, 0:64]))
                                K.cp("dve", S32[d][:, m, :], pr.v(64), [pr], [S32[d]])
                                K.cp("act", S16[d][:, m, :], pr.v(64), [pr], [S16[d]])
                        else:
                            memset(S32[d], S32[d][:, :, :], 0.0)
                            memset(S16[d], S16[d][:, :, :], 0.0)

                    def stageA(step, d):
                        cidx = step if d == 0 else nch - 1 - step
                        tk0 = s0 + cidx * C
                        for nm, src in (("R", ZR), ("KK", ZKK), ("KD", ZKD[d]), ("BA", ZBA[d]), ("LW", ZLW[d]),
                                        ("V", ZV)):
                            b = ld[nm][d]
                            K.dma("sp", v3(b), src[:, tk0:tk0 + C].rearrange("(m p) t -> p m t", p=128), [], [b])
                            yield
                        lw = ld["LW"][d]
                        cm = cum[d]
                        if d == 0:
                            K.I("dve", [rstf, lw], [cm],
                                lambda e, cm=cm, lw=lw: e.tensor_tensor_scan(out=cm[:, :], data0=rstf[:, :],
                                                                            data1=lw[:, :], initial=0.0,
                                                                            op0=ALU.mult, op1=ALU.add))
                            cend = v3(cm)[:, :, 127:128]
                        else:
                            K.I("dve", [rstb, lw], [cm],
                                lambda e, cm=cm, lw=lw: e.tensor_tensor_scan(out=cm[:, ::-1], data0=rstb[:, ::-1],
                                                                            data1=lw[:, ::-1], initial=0.0,
                                                                            op0=ALU.mult, op1=ALU.add))
                            cend = v3(cm)[:, :, 0:1]
                        yield
                        K.act(GC[d][:, :].rearrange("p (m o) -> p m o", o=1), cend, AF.Exp, [cm], [GC[d]])
                        yield
                        for m in range(8):
                            K.ts("dve", DG[d][:, m, :], identb[:, :], GC[d][:, m:m + 1], ALU.mult,
                                 [identb, GC[d]], [DG[d]])
                        yield
                        er = tmp.new()
                        K.act(er[:, :], cm[:, :], AF.Exp, [cm], [er])
                        yield
                        K.tt("dve", AR[d][:, :, 1, :], v3(ld["R"][d]), v3(er), ALU.mult, [ld["R"][d], er], [AR[d]])
                        yield
                        cml = tmp.new()
                        K.tt("pool", cml[:, :], cm[:, :], lw[:, :], ALU.subtract, [cm, lw], [cml])
                        yield
                        ea = tmp.new()
                        K.act(ea[:, :], cml[:, :], AF.Exp, [cml], [ea])
                        yield
                        K.stt(AR[d][:, :, 0, :], v3(ld["KK"][d]), -1.0, v3(ea), ALU.mult, ALU.mult,
                              [ld["KK"][d], ea], [AR[d]])
                        yield
                        en = tmp.new()
                        K.act(en[:, :], cm[:, :], AF.Exp, [cm], [en], scale=-1.0)
                        yield
                        K.tt("dve", BH[d][:, :], ld["BA"][d][:, :], en[:, :], ALU.mult, [ld["BA"][d], en], [BH[d]])
                        yield
                        K.tt("dve", KH[d][:, :], ld["KD"][d][:, :], en[:, :], ALU.mult, [ld["KD"][d], en], [KH[d]])
                        yield
                        for (srcb, dst, useid) in ((ld["V"][d], VT[d], True), (BH[d], BT[d], False),
                                                   (KH[d], KT[d], False)):
                            for m4 in range(2):
                                pr = pbig.new()
                                for q in range(4):
                                    m = m4 * 4 + q
                                    rhs = identb[:, :] if useid else DG[d][:, m, :]
                                    K.mm(pr, pr.v(128, o=q * 128), v3(srcb)[:, m, :], rhs,
                                         [srcb, identb if useid else DG[d]])
                                K.cp("act" if m4 % 2 else "dve", dst[:, 4 * m4:4 * m4 + 4, :],
                                     pr.v(512).rearrange("p (m c) -> p m c", c=128), [pr], [dst])
                                yield

                    def drip(gen, n):
                        if gen is None:
                            return
                        for _ in range(n):
                            try:
                                next(gen)
                            except StopIteration:
                                return

                    GH = [[8 * (gq // 2) + (gq % 2) + 2 * e4 for e4 in range(4)] for gq in range(4)]

                    def gof(h):
                        return 2 * (h // 8) + (h % 2), (h % 8) // 2

                    def stageBCD(step, d, gen):
                        cidx = step if d == 0 else nch - 1 - step
                        tk0 = s0 + cidx * C
                        if os.environ.get('WKV_STOP') == 'A':
                            return
                        mk = maskf2 if d == 0 else maskb2
                        mk2 = mlow4 if d == 0 else mupp4
                        wkvb = os.environ.get('WKV_B', 'klt')
                        for gq in range(4):
                            for hh in range(2 if 'k' in wkvb else 0):
                                for (srcK, dstT) in ((KH[d], LMK[gq]), (BH[d], LMB[gq])):
                                    pr = pbig.new()
                                    for e2 in range(2):
                                        h = GH[gq][2 * hh + e2]
                                        m = h // 2
                                        hp = slice(64 * (h % 2), 64 * (h % 2) + 64)
                                        arh = AR[d][hp, m, :, :].rearrange("p a t -> p (a t)")
                                        K.mm(pr, pr.v(256, o=256 * e2), v3(srcK)[hp, m, :], arh, [srcK, AR[d]])
                                    K.tt("dve", dstT[:, 2 * hh:2 * hh + 2, :],
                                         pr.v(512).rearrange("p (a c) -> p a c", c=256),
                                         mk[:, :].rearrange("p (a c) -> p a c", c=256), ALU.mult, [pr, mk], [dstT])
                            if 'l' in wkvb:
                                pr = pbig.new()
                                for e4 in range(4):
                                    h = GH[gq][e4]
                                    m = h // 2
                                    hp = slice(64 * (h % 2), 64 * (h % 2) + 64)
                                    K.mm(pr, pr.v(128, o=128 * e4), AR[d][hp, m, 0, :], v3(BH[d])[hp, m, :],
                                         [AR[d], BH[d]])
                                K.tt("dve", LAB[gq][:, :, :], pr.v(512).rearrange("p (a c) -> p a c", c=128),
                                     mk2[:, :].rearrange("p (a c) -> p a c", c=128), ALU.mult, [pr, mk2], [LAB[gq]])
                            if 't' in wkvb:
                                K.tt("pool", TTt[gq][0][:, :, :], identw4[:, :].rearrange("p (a c) -> p a c", c=128),
                                     LMB[gq][:, :, 0:128], ALU.add, [identw4, LMB[gq]], [TTt[gq][0]])
                        drip(gen, 3)
                        if os.environ.get('WKV_STOP') == 'B':
                            return
                        def pass1(k, gq):
                            pkb = LAB[gq] if k == 0 else PP[gq][(k - 1) % 2]
                            ptb = LMB[gq] if k == 0 else PT[gq][(k - 1) % 2]
                            pn = PP[gq][k % 2]
                            pr = pbig.new()
                            for e4 in range(4):
                                K.mm(pr, pr.v(128, o=128 * e4), ptb[:, e4, 0:128], pkb[:, e4, :], [ptb, pkb])
                            K.cp("act", pn[:, :, :], pr.v(512).rearrange("p (a c) -> p a c", c=128), [pr], [pn])
                            if k < 5:
                                ptn = PT[gq][k % 2]
                                pr2 = pbig.new()
                                for e4 in range(4):
                                    K.mm(pr2, pr2.v(128, o=128 * e4), pkb[:, e4, :], ptb[:, e4, 0:128], [ptb, pkb])
                                K.cp("act" if gq % 2 else "dve", ptn[:, :, :],
                                     pr2.v(512).rearrange("p (a c) -> p a c", c=128), [pr2], [ptn])

                        def pass2(k, gq):
                            pn = PP[gq][k % 2]
                            tcur = TTt[gq][k % 2]
                            tnx = TTt[gq][(k + 1) % 2]
                            pr3 = pbig.new()
                            for e4 in range(4):
                                K.mm(pr3, pr3.v(128, o=128 * e4), pn[:, e4, :], tcur[:, e4, :], [pn, tcur])
                            K.tt("dve", tnx[:, :, :], pr3.v(512).rearrange("p (a c) -> p a c", c=128),
                                 tcur[:, :, :], ALU.add, [pr3, tcur], [tnx])

                        for k in range(6):
                            for gq in range(4):
                                pass1(k, gq)
                                if k > 0:
                                    pass2(k - 1, gq)
                            drip(gen, 6)
                        for gq in range(4):
                            pass2(5, gq)
                        drip(gen, 3)
                        if os.environ.get('WKV_STOP') == 'C':
                            return
                        for h8 in range(2):
                            px = pbig.new()
                            for e8 in range(8):
                                h = 8 * h8 + e8
                                m, q = h // 2, h % 2
                                hp = slice(64 * q, 64 * q + 64)
                                vh = VT[d][:, m, 64 * q:64 * q + 64]
                                K.mm(px, px.v(64, o=64 * e8), AR[d][hp, m, 0, :], S16[d][hp, m, :], [AR[d], S16[d]],
                                     start=True, stop=False)
                                K.mm(px, px.v(64, o=64 * e8), LMK[gof(h)[0]][:, gof(h)[1], 0:128], vh, [LMK[gof(h)[0]], VT[d]],
                                     start=False, stop=True)
                            K.cp("act" if h8 else "dve", XX[h8][:, :, :], px.v(512).rearrange("p (a c) -> p a c", c=64),
                                 [px], [XX[h8]])
                        drip(gen, 4)
                        for h8 in range(2):
                            pu = pbig.new()
                            for e8 in range(8):
                                h = 8 * h8 + e8
                                tT = TTt[gof(h)[0]][0]
                                K.mm(pu, pu.v(64, o=64 * e8), tT[:, gof(h)[1], :], XX[h8][:, e8, :], [tT, XX[h8]])
                            K.cp("dve" if h8 else "act", UU[h8][:, :, :], pu.v(512).rearrange("p (a c) -> p a c", c=64),
                                 [pu], [UU[h8]])
                        drip(gen, 4)
                        if os.environ.get('WKV_STOP') == 'D2':
                            return
                        psn = pbig.new()
                        for m4 in range(2):
                            py = pbig.new()
                            for e4 in range(4):
                                m = 4 * m4 + e4
                                for q in range(2):
                                    h = 2 * m + q
                                    hp = slice(64 * q, 64 * q + 64)
                                    vh = VT[d][:, m, 64 * q:64 * q + 64]
                                    uh = UU[h // 8][:, h % 8, :]
                                    lmk = LMK[gof(h)[0]]
                                    lmb = LMB[gof(h)[0]]
                                    yo = py.v(128, hp.start, hp.stop, o=128 * e4)
                                    K.mm(py, yo, S16[d][hp, m, :], AR[d][hp, m, 1, :], [S16[d], AR[d]],
                                         start=True, stop=False)
                                    K.mm(py, yo, uh, lmb[:, gof(h)[1], 128:256], [UU[h // 8], lmb], start=False, stop=False)
                                    K.mm(py, yo, vh, lmk[:, gof(h)[1], 128:256], [VT[d], lmk], start=False, stop=True)
                                    so_ = psn.v(64, hp.start, hp.stop, o=64 * m)
                                    K.mm(psn, so_, BT[d][:, m, 64 * q:64 * q + 64], uh, [BT[d], UU[h // 8]],
                                         start=True, stop=False)
                                    K.mm(psn, so_, KT[d][:, m, 64 * q:64 * q + 64], vh, [KT[d], VT[d]],
                                         start=False, stop=True)
                            K.cp("act", YT[d][:, 4 * m4:4 * m4 + 4, :], py.v(512).rearrange("p (a c) -> p a c", c=128),
                                 [py], [YT[d]])
                        for m in range(8):
                            K.stt(S32[d][:, m, :], S32[d][:, m, :], GC[d][:, m:m + 1], psn.v(64, o=64 * m), ALU.mult,
                                  ALU.add, [S32[d], GC[d], psn], [S32[d]])
                        K.cp("act", S16[d][:, :, :], S32[d][:, :, :], [S32[d]], [S16[d]])
                        K.dma("pool", YTD[d, :, tk0:tk0 + C].rearrange("(m p) t -> p m t", p=128), YT[d][:, :, :],
                              [YT[d]], [])

                    g0 = stageA(0, 0)
                    drip(g0, 10 ** 6)
                    for step in range(nch):
                        for d in range(2):
                            if d == 0:
                                gen = stageA(step, 1)
                            elif step + 1 < nch:
                                gen = stageA(step + 1, 0)
                            else:
                                gen = None
                            stageBCD(step, d, gen)
                            drip(gen, 10 ** 6)
                    if not g["grid"]:
                        for d in range(2):
                            for m in range(8):
                                pr = pbig.new()
                                K.I("pe", [S32[d], ident], [pr],
                                    lambda e, pr=pr, d=d, m=m: e.transpose(out=pr.v(128, 0, 64), in_=S32[d][:, m, :],
                                                                           identity=ident[:, :]))
                                so = sout.new()
                                K.cp("dve", so[:, :], pr.v(128, 0, 64), [pr], [so])
                                K.dma("pool", nsout[si, l, d, 2 * m:2 * m + 2].rearrange("h i j -> i h j"),
                                      so[:, :].rearrange("i (h j) -> i h j", j=64), [so], [])

        def phase3(l, g):
            ci = g["cond"]
            with ExitStack() as pes:
                yf = [K.sb("p3yf%d" % m, [128, TT], F32, pes) for m in range(8)]
                yb = [K.sb("p3yb%d" % m, [128, TT], F32, pes) for m in range(8)]
                h16 = [K.sb("p3h%d" % m, [128, TT], BF16, pes) for m in range(8)]
                bon = K.sb("p3bon", [128, 8, TT], BF16, pes)
                gg = K.sb("p3g", [128, 8, TT], BF16, pes)
                ob = K.sb("p3ob", [128, 8, TT], BF16, pes)
                gab = K.sb("p3gab", [128, 16, TT], BF16, pes)
                xT = K.sb("p3x", [128, 8, TT], F32, pes)
                oa = K.sb("p3oa", [128, 8, TT], BF16, pes)
                mg = K.sb("p3mg", [128, 8, TT], BF16, pes)
                tmp = K.ring("p3t", [128, TT], F32, 6, pes)
                wr = K.ring("p3w", [128, 8, 128], BF16, 6, pes)
                ntile3 = g["nt"] // TT

                def ld3(b, src):
                    K.dma("sp", b[:, :, :], src.rearrange("(m p) t -> p m t", p=128), [], [b])

                def loads3(tix, which):
                    if tix >= ntile3:
                        return
                    cs_ = slice(g["t0"] + tix * TT, g["t0"] + (tix + 1) * TT)
                    if which == 0:
                        for m in range(8):
                            K.dma("sp", yf[m][:, :], YTD[0, m * 128:(m + 1) * 128, cs_], [], [yf[m]])
                            K.dma("sp", yb[m][:, :], YTD[1, m * 128:(m + 1) * 128, cs_], [], [yb[m]])
                        ld3(bon, ZBON[:, cs_])
                        ld3(gg, ZG[:, cs_])
                    elif which == 1:
                        ld3(ob, ZOB[:, cs_])
                    elif which == 2:
                        ld3(gab, ZGAB[:, cs_])
                    else:
                        ld3(xT, XTA[:, cs_])

                for w_ in range(4):
                    loads3(0, w_)
                for tix in range(ntile3):
                    tk0 = g["t0"] + tix * TT
                    cs = slice(tk0, tk0 + TT)
                    for m in range(8):
                        K.tt("pool", yf[m][:, :], yf[m][:, :], yb[m][:, :], ALU.add, [yf[m], yb[m]], [yf[m]])
                    for m in range(8):
                        K.cp("act", h16[m][:, :], yf[m][:, :], [yf[m]], [h16[m]])
                    pms = []
                    for m in range(8):
                        pm = pbig.new()
                        K.mm(pm, pm.v(TT), oblk64[:, :], h16[m][:, :], [oblk64, h16[m]])
                        pms.append(pm)
                    for m in range(8):
                        K.tt("dve", yf[m][:, :], yf[m][:, :], pms[m].v(TT), ALU.subtract, [yf[m], pms[m]], [yf[m]])
                    for m in range(8):
                        K.act(h16[m][:, :], yf[m][:, :], AF.Square, [yf[m]], [h16[m]])
                    pvs = []
                    for m in range(8):
                        pv_ = pbig.new()
                        K.mm(pv_, pv_.v(TT), oblk64[:, :], h16[m][:, :], [oblk64, h16[m]])
                        pvs.append(pv_)
                    for m in range(8):
                        K.act(yb[m][:, :], pvs[m].v(TT), AF.Sqrt, [pvs[m], epsg], [yb[m]], bias=epsg[:, 0:1])
                    for m in range(8):
                        K.I("dve", [yb[m]], [yb[m]], lambda e, b=yb[m]: e.reciprocal(out=b[:, :], in_=b[:, :]))
                    for m in range(8):
                        K.tt("dve", yf[m][:, :], yf[m][:, :], yb[m][:, :], ALU.mult, [yf[m], yb[m]], [yf[m]])
                    for m in range(8):
                        K.ts("pool", yf[m][:, :], yf[m][:, :], pvc(l, "gnw", m), ALU.mult, [yf[m], pvt], [yf[m]],
                             s2=pvc(l, "gnb", m), op1=ALU.add)
                    for m in range(8):
                        K.tt("dve" if m % 2 else "pool", yf[m][:, :], yf[m][:, :], bon[:, m, :], ALU.add,
                             [yf[m], bon], [yf[m]])
                    for m in range(8):
                        K.tt("dve", oa[:, m, :], yf[m][:, :], gg[:, m, :], ALU.mult, [yf[m], gg], [oa])
                    loads3(tix + 1, 0)
                    for mo in range(8):
                        wa = wr.new()
                        K.dma("sp", wa[:, :, :], WB["wpa"][l, mo], [], [wa])
                        wb_ = wr.new()
                        K.dma("sp", wb_[:, :, :], WB["wpb"][l, mo], [], [wb_])
                        pA = pbig.new()
                        pB = pbig.new()
                        for m in range(8):
                            K.mm(pA, pA.v(TT), wa[:, m, :], oa[:, m, :], [wa, oa], start=(m == 0), stop=(m == 7))
                        for m in range(8):
                            K.mm(pB, pB.v(TT), wb_[:, m, :], ob[:, m, :], [wb_, ob], start=(m == 0), stop=(m == 7))
                        t1 = tmp.new()
                        K.tt("dve", t1[:, :], pA.v(TT), gab[:, mo, :], ALU.mult, [pA, gab], [t1])
                        t2 = tmp.new()
                        K.tt("dve", t2[:, :], pB.v(TT), gab[:, 8 + mo, :], ALU.mult, [pB, gab], [t2])
                        K.tt("pool", mg[:, mo, :], t1[:, :], t2[:, :], ALU.add, [t1, t2], [mg])
                    loads3(tix + 1, 1)
                    loads3(tix + 1, 2)
                    for mo in range(8):
                        ww = wr.new()
                        K.dma("sp", ww[:, :, :], WB["wo"][l, mo], [], [ww])
                        pO = pbig.new()
                        for m in range(8):
                            K.mm(pO, pO.v(TT), ww[:, m, :], mg[:, m, :], [ww, mg], start=(m == 0), stop=(m == 7))
                        xm = tmp.new()
                        K.stt(xm[:, :], pO.v(TT), modc(l, 2, mo, ci), xT[:, mo, :], ALU.mult, ALU.add,
                              [pO, modt, xT], [xm])
                        K.dma("pool", XTB[mo * 128:(mo + 1) * 128, cs], xm[:, :], [xm], [])
                    loads3(tix + 1, 3)

        def phase4(l, g):
            ci = g["cond"]
            H = g["H"]
            sh = g["sh"]
            W = TT + 2 * H
            with ExitStack() as pes:
                xm_r = K.ring("p4x", [128, 8, W], F32, 2, pes)
                h2r = [K.sb("p4h%d" % i, [128, 8, W], BF16, pes) for i in range(2)]
                rb = rms_bufs("p4r_", pes)
                sq = K.sb("p4sq", [128, 8, 512], BF16, pes)
                qq = K.sb("p4q", [128, 22, TT], BF16, pes)
                tmp = K.ring("p4t", [128, TT], F32, 12, pes)
                wr = K.ring("p4w", [128, 8, 128], BF16, 6, pes)
                wdr = K.ring("p4wd", [128, 22, 128], BF16, 2, pes)
                pairs = [K.reg(PReg(PS[2 * i], 0, 1024)) for i in range(3)]
                p4big = Ring([preg_big[6], preg_big[7]])
                for (s0, slen) in ([(g["t0"], g["nt"])] if not g["grid"] else g["seqs"]):
                    xms = {}

                    def xmload(tix):
                        if tix >= slen // TT or tix in xms:
                            return
                        tk = s0 + tix * TT
                        lo_ = max(s0, tk - H)
                        hi_ = min(s0 + slen, tk + TT + H)
                        b = xm_r.new()
                        K.dma("sp", b[:, :, 0:hi_ - lo_], XTB[:, lo_:hi_].rearrange("(kc p) t -> p kc t", p=128), [], [b])
                        xms[tix] = b

                    def nof(tix):
                        tk = s0 + tix * TT
                        return min(s0 + slen, tk + TT + H) - max(s0, tk - H)

                    xmload(0)
                    gdrip(rms_gen(xms[0], h2r[0], nof(0), a2t, l, 3, ci, rb, sq, p4big), 10 ** 6)
                    for tix in range(slen // TT):
                        tk0 = s0 + tix * TT
                        lo = max(s0, tk0 - H)
                        hi = min(s0 + slen, tk0 + TT + H)
                        n = hi - lo
                        co = tk0 - lo
                        xm = xms.pop(tix)
                        h2 = h2r[tix % 2]
                        xmload(tix + 1)
                        rgen = None
                        if tix + 1 < slen // TT:
                            rgen = rms_gen(xms[tix + 1], h2r[(tix + 1) % 2], nof(tix + 1), a2t, l, 3, ci, rb, sq, p4big)

                        def umm(j, pi):
                            w = wr.new()
                            K.dma("sp", w[:, :, :], WB["wup"][l, j], [], [w])
                            pr = pairs[pi]
                            for c0 in range(0, n, 512):
                                cw = min(512, n - c0)
                                for kc in range(8):
                                    K.mm(pr, PS[2 * pi + c0 // 512][:, 0:cw], w[:, kc, :],
                                         h2[:, kc, c0:c0 + cw], [w, h2], start=(kc == 0), stop=(kc == 7))
                            return pr

                        def uap(pi, a, b):
                            bk = a // 512
                            assert (b - 1) // 512 == bk
                            return PS[2 * pi + bk][:, a - 512 * bk:b - 512 * bk]

                        def conv(pr, pi, j):
                            acc = tmp.new()
                            w0 = pvc(l, "cfw", j)
                            w1 = pvc(l, "cfw", 44 + j)
                            w2 = pvc(l, "cfw", 88 + j)
                            bb = pvc(l, "cfb", j)
                            for (a, b) in splits(co, co + TT):
                                K.ts("dve", acc[:, a - co:b - co], uap(pi, a, b), w1, ALU.mult, [pr, pvt], [acc],
                                     s2=bb, op1=ALU.add)
                            if g["grid"]:
                                a0 = max(tk0, s0 + sh)
                                b0 = min(tk0 + TT, s0 + slen - sh)
                                for (a, b) in splits(a0 - sh - lo, tk0 + TT - sh - lo):
                                    K.stt(acc[:, a + sh + lo - tk0:b + sh + lo - tk0], uap(pi, a, b), w0,
                                          acc[:, a + sh + lo - tk0:b + sh + lo - tk0], ALU.mult, ALU.add,
                                          [pr, acc, pvt], [acc])
                                for (a, b) in splits(tk0 + sh - lo, b0 + sh - lo):
                                    K.stt(acc[:, a - sh + lo - tk0:b - sh + lo - tk0], uap(pi, a, b), w2,
                                          acc[:, a - sh + lo - tk0:b - sh + lo - tk0], ALU.mult, ALU.add,
                                          [pr, acc, pvt], [acc])
                            else:
                                rl = g["rl"]
                                u3 = PS[2 * pi][:, 0:TT].rearrange("p (r c) -> p r c", c=rl)
                                a3 = acc[:, :].rearrange("p (r c) -> p r c", c=rl)
                                K.stt(a3[:, :, 1:rl], u3[:, :, 0:rl - 1], w0, a3[:, :, 1:rl], ALU.mult, ALU.add,
                                      [pr, acc, pvt], [acc])
                                K.stt(a3[:, :, 0:rl - 1], u3[:, :, 1:rl], w2, a3[:, :, 0:rl - 1], ALU.mult, ALU.add,
                                      [pr, acc, pvt], [acc])
                            return acc

                        def splits(a, b):
                            out = []
                            while a < b:
                                e = min(b, (a // 512 + 1) * 512)
                                out.append((a, e))
                                a = e
                            return out

                        for f in range(22):
                            pa_i = (2 * f) % 3
                            pl_i = (2 * f + 1) % 3
                            pra = umm(f, pa_i)
                            prl = umm(22 + f, pl_i)
                            ua = conv(pra, pa_i, f)
                            ul = conv(prl, pl_i, 22 + f)
                            sa = tmp.new()
                            K.act(sa[:, :], ua[:, :], AF.Silu, [ua], [sa])
                            K.tt("pool", qq[:, f, :], sa[:, :], ul[:, :], ALU.mult, [sa, ul], [qq])
                        for mo in range(8):
                            wd = wdr.new()
                            K.dma("sp", wd[:, :, :], WB["wdn"][l, mo], [], [wd])
                            pO = p4big.new()
                            for f in range(22):
                                K.mm(pO, pO.v(TT), wd[:, f, :], qq[:, f, :], [wd, qq], start=(f == 0), stop=(f == 21))
                            xo = tmp.new()
                            K.stt(xo[:, :], pO.v(TT), modc(l, 5, mo, ci), xm[:, mo, co:co + TT], ALU.mult, ALU.add,
                                  [pO, modt, xm], [xo])
                            K.dma("pool", XTA[mo * 128:(mo + 1) * 128, tk0:tk0 + TT], xo[:, :], [xo], [])
                            gdrip(rgen, 9)
                        gdrip(rgen, 10 ** 6)

        def phase5():
            with ExitStack() as pes:
                xT_r = K.ring("p5x", [128, 8, TT], F32, 2, pes)
                yT = K.sb("p5y", [128, 8, TT], F32, pes)
                sq = K.sb("p5sq", [128, 8, TT], BF16, pes)
                rb5 = rms_bufs("p5r_", pes)
                yo_r = K.ring("p5o", [128, D], F32, 2, pes)
                for tix in range(NTA // TT):
                    tk0 = tix * TT
                    xT = xT_r.new()
                    K.dma("sp", xT[:, :, :], XTA[:, tk0:tk0 + TT].rearrange("(kc p) t -> p kc t", p=128), [], [xT])
                    gdrip(rms_gen(xT, yT, TT, None, 0, 0, 0, rb5, sq), 10 ** 6)
                    for blk in range(TT // 128):
                        yo = yo_r.new()
                        for half in range(2):
                            pr = pbig.new()
                            for q in range(4):
                                kc = half * 4 + q
                                K.I("pe", [yT, ident], [pr],
                                    lambda e, kc=kc, q=q, pr=pr, blk=blk: e.transpose(
                                        out=pr.v(128, o=q * 128), in_=yT[:, kc, blk * 128:(blk + 1) * 128],
                                        identity=ident[:, :]))
                            K.cp("act" if half else "dve", yo[:, half * 512:(half + 1) * 512], pr.v(512), [pr], [yo])
                        K.dma("pool", yout[tk0 + blk * 128:tk0 + (blk + 1) * 128, :], yo[:, :], [yo], [])

        ph = phases or ("w", "0", "1", "2", "3", "4", "5")
        if "w" in ph:
            phase_w()
            K.barrier()
        else:
            with ExitStack() as pes:
                ff = K.sb("dbgfill", [128, 1024], F32, pes)
                fb = K.sb("dbgfillb", [128, 2816], BF16, pes)
                memset(ff, ff[:, :], -0.05)
                memset(fb, fb[:, :], 0.05)
                for t5 in range(NTA // 1024):
                    cs = slice(t5 * 1024, (t5 + 1) * 1024)
                    for m in range(8):
                        rs_ = slice(m * 128, (m + 1) * 128)
                        for dst in (ZR[rs_, cs], ZKK[rs_, cs], ZKD[0, rs_, cs], ZKD[1, rs_, cs], ZBA[0, rs_, cs],
                                    ZBA[1, rs_, cs], ZLW[0, rs_, cs], ZLW[1, rs_, cs]):
                            K.dma("sp", dst, ff[:, :], [ff], [])
                        K.dma("sp", ZV[rs_, cs], fb[:, 0:1024], [fb], [])
                        K.dma("sp", XTA[rs_, cs], ff[:, :], [ff], [])
                        K.dma("sp", XTB[rs_, cs], ff[:, :], [ff], [])
                        K.dma("sp", YTD[0, rs_, cs], ff[:, :], [ff], [])
                        K.dma("sp", YTD[1, rs_, cs], ff[:, :], [ff], [])
                        for dst in (ZBON[rs_, cs], ZG[rs_, cs], ZOB[rs_, cs], ZGAB[rs_, cs], ZGAB[1024 + m * 128:1152 + m * 128, cs]):
                            K.dma("sp", dst, fb[:, 0:1024], [fb], [])
                for nm_, _, Kd_, Nd_ in WSPEC:
                    for j in range(Nd_ // 128):
                        K.dma("sp", WB[nm_][0, j], fb[:, 0:Kd_].rearrange("p (k c) -> p k c", c=128), [fb], [])
                K.barrier()
            for b in (modt, a1t, a2t):
                memset(b, b[:, :, :, :], 0.5)
            for b in smallw.values():
                memset(b, b[:, :, :], 0.01)
            K.barrier()
        if "0" in ph:
            phase0()
            K.barrier()
        for l in range(nlayers):
            for gi in groups:
                g = GROUPS[gi]
                for nm, fn in (("1", phase1), ("2", phase2), ("3", phase3), ("4", phase4)):
                    if nm in ph:
                        fn(l, g)
                        K.barrier()
        if "5" in ph:
            phase5()
        K.barrier()
        build_nc.nins = K.nins
    return nc


DEBUG_OUT = set()


def _pvec(v):
    v = np.asarray(v, np.float32).reshape(-1)
    return v.reshape(-1, 128).T


def make_inputs(core, inp):
    b = core % 2
    xin = np.concatenate([np.asarray(inp["x_prompt"][4 * core:4 * core + 4]).reshape(NPT, D),
                          np.asarray(inp["x_sample"][b]).reshape(NST, D)], 0)
    cond = np.stack([_pvec(inp["c_ctx"]), _pvec(inp["c"][b])], -1)
    pv = np.zeros((L, 128, NPV), np.float32)
    for l in range(L):
        def put(name, v):
            a = _pvec(v)
            pv[l, :, PV[name]:PV[name] + a.shape[1]] = a
        put("n1g", inp["norm1_g"][l])
        put("n2g", inp["norm2_g"][l])
        put("w0", inp["decay_w0"][l])
        put("a0", inp["iclr_a0"][l])
        put("kk", inp["k_k"][l])
        put("ka", inp["k_a"][l])
        put("rk", inp["r_k"][l])
        put("gnw", inp["gn_w"][l])
        put("gnb", inp["gn_b"][l])
        put("cmw", inp["conv_mix_w"][l])
        put("cmb", inp["conv_mix_b"][l])
        put("cfw", inp["conv_ffn_w"][l])
        put("cfb", inp["conv_ffn_b"][l])
        put("bmod", inp["b_mod"][l])
    m = {
        "xin": np.ascontiguousarray(xin, np.float32),
        "st_in": np.ascontiguousarray(inp["state_wkv"][b], np.float32),
        "cond": np.ascontiguousarray(cond, np.float32),
        "pv": pv,
        "nfg": np.ascontiguousarray(_pvec(inp["norm_f_g"]), np.float32),
        "w_mod": inp["w_mod"], "w_in": inp["w_in"],
        "decay_w2": np.asarray(inp["decay_w2"]).reshape(L, 128, D),
        "iclr_a2": np.asarray(inp["iclr_a2"]).reshape(L, 128, D),
        "gate_g2": inp["gate_g2"], "w_pa": inp["w_pa"], "w_pb": inp["w_pb"], "w_o": inp["w_o"],
        "w_up": inp["w_up"], "w_down": inp["w_down"],
    }
    return {k: np.ascontiguousarray(np.asarray(v, np.float32)) for k, v in m.items()}


def kernel(**inputs):
    inp = {k: np.asarray(v) for k, v in inputs.items()}
    nc = build_nc()
    in_maps = [make_inputs(c, inp) for c in range(8)]
    res = run_bass_kernel_spmd(nc, in_maps, core_ids=list(range(8)))
    rs = res.results
    y_prompt = np.concatenate([np.asarray(rs[c]["yout"])[0:NPT].reshape(4, 256, D) for c in range(8)], 0)
    y_sample = np.stack([np.asarray(rs[b]["yout"])[NPT:NTA] for b in range(2)], 0)
    new_state = np.concatenate([np.asarray(rs[c]["nsout"]) for c in range(8)], 0)
    return (y_prompt.astype(np.float32), y_sample.astype(np.float32), new_state.astype(np.float32))
```
